# Optimizing a Trainium2 kernel written in Bass

```python
import math
import jax, jax.numpy as jnp
from jax import lax
import numpy as np

D_MODEL = 2048
BATCH = 2
SEQ = 16384
DEPTH = 2

N_MIXERS = 4
GROUP_W = D_MODEL // N_MIXERS
MIX_W = N_MIXERS * GROUP_W
CHUNK = 64
Q_BLOCK = 128
HG_DK = 128
HG_HEADS = GROUP_W // HG_DK
HG_DV = GROUP_W // HG_HEADS
NSA_HEAD_DIM = 64
NSA_HEADS = GROUP_W // NSA_HEAD_DIM
NSA_KV = 2
NSA_GROUP = NSA_HEADS // NSA_KV
NSA_CMP_BLOCK = 32
NSA_CMP_STRIDE = 16
NSA_SLC_BLOCK = 64
NSA_TOP_N = 16
NSA_WINDOW = 512
NSA_KVW = NSA_KV * NSA_HEAD_DIM
NSA_CMP_HIDDEN = 256
SSM_HEAD_DIM = 64
SSM_HEADS = GROUP_W // SSM_HEAD_DIM
SSM_GROUPS = 2
SSM_STATE = 128
SSM_CONV = 4
SSM_CONV_DIM = GROUP_W + 2 * SSM_GROUPS * SSM_STATE
RET_HEADS = 4
RET_DK = GROUP_W // RET_HEADS
REL_BUCKETS = 32
REL_EXACT = REL_BUCKETS // 2
REL_MAX_DIST = 2048
D_FF = 5632
DEEPNORM_ALPHA = (2 * DEPTH) ** 0.25
DEEPNORM_BETA = (8 * DEPTH) ** -0.25

IN_SIZES = ((GROUP_W,) * 4
            + (GROUP_W,) + (NSA_KVW,) * 6 + (3 * NSA_HEADS,)
            + (GROUP_W, SSM_CONV_DIM, SSM_HEADS)
            + (GROUP_W,) * 4)
D_IN = sum(IN_SIZES)
IN_SPLITS = tuple(int(v) for v in np.cumsum(IN_SIZES)[:-1])

kernel_name = 'hybrid_parallel_groups_hgrn2_nsa_ssd_retention'


def layer_norm(x, g, b, eps=1e-5):
    xf = x.astype(jnp.float32)
    mu = jnp.mean(xf, axis=-1, keepdims=True)
    var = jnp.mean(jnp.square(xf - mu), axis=-1, keepdims=True)
    return ((xf - mu) * lax.rsqrt(var + eps) * g + b).astype(x.dtype)


def rms_normalize(x, eps=1e-6):
    xf = x.astype(jnp.float32)
    return xf * lax.rsqrt(jnp.mean(jnp.square(xf), axis=-1, keepdims=True) + eps)


def swiglu(h, w1, w3, w2):
    return (jax.nn.silu(h @ w1) * (h @ w3)) @ w2


def t5_bucket(dist):
    n = jnp.maximum(dist, 0)
    nf = jnp.maximum(n, 1).astype(jnp.float32)
    large = REL_EXACT + (jnp.log(nf / REL_EXACT) / math.log(REL_MAX_DIST / REL_EXACT)
                         * (REL_BUCKETS - REL_EXACT)).astype(jnp.int32)
    return jnp.where(n < REL_EXACT, n, jnp.minimum(large, REL_BUCKETS - 1))


def masked_softmax(s, mask):
    p = jax.nn.softmax(jnp.where(mask, s.astype(jnp.float32), -1e30), axis=-1)
    return jnp.where(mask, p, 0.0)


def causal_depthwise_conv(x, w, b):
    out = lax.conv_general_dilated(x, w[:, None, :], window_strides=(1,),
                                   padding=[(SSM_CONV - 1, 0)],
                                   dimension_numbers=('NWC', 'WIO', 'NWC'),
                                   feature_group_count=x.shape[-1])
    return out + b


def rotary(t, pos):
    half = t.shape[-1] // 2
    theta = 1.0 / (10000.0 ** jnp.linspace(0.0, 1.0, half, dtype=jnp.float32))
    ang = pos[:, None] * theta[None, :]
    cos, sin = jnp.cos(ang)[None, :, None, :], jnp.sin(ang)[None, :, None, :]
    t1, t2 = t[..., 0::2], t[..., 1::2]
    return jnp.stack([t1 * cos - t2 * sin, t1 * sin + t2 * cos], axis=-1).reshape(t.shape)


def chunked_scalar_decay(q, k, v, log_a):
    B_, S_, H, dk = q.shape
    dv = v.shape[-1]
    nc = S_ // CHUNK
    qc = q.reshape(B_, nc, CHUNK, H, dk)
    kc = k.reshape(B_, nc, CHUNK, H, dk)
    vc = v.reshape(B_, nc, CHUNK, H, dv)
    b = jnp.cumsum(log_a.reshape(B_, nc, CHUNK, H), axis=2)
    bh = b.transpose(0, 1, 3, 2)
    causal = jnp.tril(jnp.ones((CHUNK, CHUNK), bool))
    decay = jnp.exp(jnp.where(causal, bh[..., :, None] - bh[..., None, :], -jnp.inf))
    scores = jnp.einsum('bcihk,bcjhk->bchij', qc, kc) * decay
    y_intra = jnp.einsum('bchij,bcjhv->bcihv', scores, vc)
    b_last = b[:, :, -1]
    chunk_state = jnp.einsum('bcjhk,bcjh,bcjhv->bchkv', kc, jnp.exp(b_last[:, :, None, :] - b), vc)

    def step(state, inp):
        a, r = inp
        return a[..., None, None] * state + r, state

    _, s_prev = lax.scan(step, jnp.zeros((B_, H, dk, dv), jnp.float32),
                         (jnp.exp(b_last).transpose(1, 0, 2), chunk_state.transpose(1, 0, 2, 3, 4)))
    s_prev = s_prev.transpose(1, 0, 2, 3, 4)
    y_inter = jnp.einsum('bcihk,bchkv->bcihv', qc * jnp.exp(b)[..., None], s_prev)
    return (y_intra + y_inter).reshape(B_, S_, H, dv)


def hgrn2_chunked(q, k, v, log_f):
    B_, S_, H, dk = q.shape
    dv = v.shape[-1]
    nc = S_ // CHUNK

    def chunks(t):
        return t.reshape(B_, nc, CHUNK, H, t.shape[-1]).transpose(1, 0, 3, 2, 4)

    qc, kc, vc = chunks(q), chunks(k), chunks(v)
    bc = jnp.cumsum(chunks(log_f), axis=3)
    causal = jnp.tril(jnp.ones((CHUNK, CHUNK), bool))[:, :, None]

    def step(state, inp):
        qi, ki, vi, bi = inp
        dec = jnp.exp(jnp.where(causal, bi[:, :, :, None, :] - bi[:, :, None, :, :], -jnp.inf))
        a = jnp.einsum('bhid,bhjd,bhijd->bhij', qi, ki, dec)
        o = (jnp.einsum('bhij,bhjv->bhiv', a, vi)
             + jnp.einsum('bhid,bhdv->bhiv', qi * jnp.exp(bi), state))
        b_last = bi[:, :, -1:, :]
        state = (jnp.exp(b_last[:, :, 0, :])[..., None] * state
                 + jnp.einsum('bhjd,bhjv->bhdv', ki * jnp.exp(b_last - bi), vi))
        return state, o

    _, o = lax.scan(step, jnp.zeros((B_, H, dk, dv), jnp.float32), (qc, kc, vc, bc))
    return o.transpose(1, 0, 3, 2, 4).reshape(B_, S_, H, dv)


def hgrn2_mixer(q, f_logit, i_in, g, lb, norm_g):
    B_, S_, _ = q.shape
    f32 = jnp.float32
    q = jax.nn.silu(q.astype(f32)).reshape(B_, S_, HG_HEADS, HG_DK)
    z = f_logit.astype(f32).reshape(B_, S_, HG_HEADS, HG_DK)
    lb = lb.astype(f32).reshape(HG_HEADS, HG_DK)
    log_f = jnp.logaddexp(jnp.log(lb), jnp.log1p(-lb) + jax.nn.log_sigmoid(z))
    k = (1.0 - lb) * jax.nn.sigmoid(-z)
    v = i_in.astype(f32).reshape(B_, S_, HG_HEADS, HG_DV)
    o = rms_normalize(hgrn2_chunked(q, k, v, log_f)).reshape(B_, S_, GROUP_W) * norm_g
    return o * jax.nn.silu(g.astype(f32))


def compress_kv(t, pe, w1, w2):
    B_, S_ = t.shape[:2]
    blk = t.reshape(B_, S_ // NSA_CMP_STRIDE, NSA_CMP_STRIDE, NSA_KV, NSA_HEAD_DIM)
    blk = jnp.concatenate([blk[:, :-1], blk[:, 1:]], axis=2) + pe[None, None, :, None, :]
    flat = blk.transpose(0, 1, 3, 2, 4).reshape(B_, blk.shape[1], NSA_KV, NSA_CMP_BLOCK * NSA_HEAD_DIM)
    return jax.nn.silu(flat @ w1) @ w2


def cmp_to_slc(p, n_slc):
    r = NSA_SLC_BLOCK // NSA_CMP_STRIDE
    pad_r = r * n_slc + r - (p.shape[-1] + 1)
    pp = jnp.pad(p, [(0, 0)] * (p.ndim - 1) + [(1, pad_r)])
    lead = p.shape[:-1]
    a = pp[..., : r * n_slc].reshape(*lead, n_slc, r)
    e = pp[..., r: r * n_slc + r].reshape(*lead, n_slc, r)[..., 0]
    return 0.5 * a[..., 0] + a[..., 1:].sum(-1) + 0.5 * e


def nsa_mixer(q, k_c, v_c, k_s, v_s, k_w, v_w, gate, pe_k, w1_k, w2_k, pe_v, w1_v, w2_v,
              rel_bias, norm_g):
    B_, S_, _ = q.shape
    f32 = jnp.float32

    def kvh(t):
        return t.astype(f32).reshape(B_, S_, NSA_KV, NSA_HEAD_DIM)

    k_c, v_c, k_s, v_s, k_w, v_w = kvh(k_c), kvh(v_c), kvh(k_s), kvh(v_s), kvh(k_w), kvh(v_w)
    k_cmp = compress_kv(k_c, pe_k, w1_k, w2_k)
    v_cmp = compress_kv(v_c, pe_v, w1_v, w2_v)
    n_cmp = k_cmp.shape[1]
    n_slc = S_ // NSA_SLC_BLOCK
    n_top = min(NSA_TOP_N, n_slc)
    k_blk = k_s.reshape(B_, n_slc, NSA_SLC_BLOCK, NSA_KV, NSA_HEAD_DIM).transpose(0, 3, 1, 2, 4)
    v_blk = v_s.reshape(B_, n_slc, NSA_SLC_BLOCK, NSA_KV, NSA_HEAD_DIM).transpose(0, 3, 1, 2, 4)
    pad = ((0, 0), (NSA_WINDOW, 0), (0, 0), (0, 0))
    k_wp, v_wp = jnp.pad(k_w, pad), jnp.pad(v_w, pad)
    table = rel_bias.astype(f32)
    table_kv = table.reshape(REL_BUCKETS, NSA_KV, NSA_GROUP).transpose(1, 0, 2)
    cmp_end = jnp.arange(n_cmp) * NSA_CMP_STRIDE + NSA_CMP_BLOCK - 1
    n_qb = S_ // Q_BLOCK
    qs = (q.astype(f32) * NSA_HEAD_DIM ** -0.5).reshape(
        B_, n_qb, Q_BLOCK, NSA_KV, NSA_GROUP, NSA_HEAD_DIM).transpose(1, 0, 2, 3, 4, 5)
    take = jax.vmap(jax.vmap(lambda blk, ix: blk[ix]))

    def head_bias(dist):
        return table[t5_bucket(dist)].transpose(2, 0, 1).reshape(NSA_KV, NSA_GROUP, *dist.shape)

    def block(args):
        bi, qb = args
        q0 = bi * Q_BLOCK
        qpos = q0 + jnp.arange(Q_BLOCK)
        dist_c = qpos[:, None] - cmp_end[None, :]
        s_c = jnp.einsum('bqkgd,bmkd->bkgqm', qb, k_cmp) + head_bias(dist_c)
        p_c = masked_softmax(s_c, dist_c >= 0)
        o_c = jnp.einsum('bkgqm,bmkd->bqkgd', p_c, v_cmp)
        imp = cmp_to_slc(p_c.sum(axis=2), n_slc)
        j = jnp.arange(n_slc)[None, :]
        cur = (qpos // NSA_SLC_BLOCK)[:, None]
        forced = (j == 0) | (j == cur) | (j == cur - 1)
        score = jnp.where(j > cur, -1.0, jnp.where(forced, NSA_GROUP + 1.0, imp))
        _, idx = lax.top_k(score, n_top)
        kg = take(k_blk, idx).reshape(B_, NSA_KV, Q_BLOCK, n_top * NSA_SLC_BLOCK, NSA_HEAD_DIM)
        vg = take(v_blk, idx).reshape(B_, NSA_KV, Q_BLOCK, n_top * NSA_SLC_BLOCK, NSA_HEAD_DIM)
        pos = (idx[..., None] * NSA_SLC_BLOCK + jnp.arange(NSA_SLC_BLOCK)).reshape(
            B_, NSA_KV, Q_BLOCK, n_top * NSA_SLC_BLOCK)
        dist_s = qpos[None, None, :, None] - pos
        bias_s = jax.vmap(lambda t, bk: t[bk], in_axes=(0, 1), out_axes=1)(table_kv, t5_bucket(dist_s))
        s_s = jnp.einsum('bqkgd,bkqnd->bkgqn', qb, kg) + bias_s.transpose(0, 1, 4, 2, 3)
        p_s = masked_softmax(s_s, (dist_s >= 0)[:, :, None])
        o_s = jnp.einsum('bkgqn,bkqnd->bqkgd', p_s, vg)
        kw = lax.dynamic_slice_in_dim(k_wp, q0, Q_BLOCK + NSA_WINDOW, axis=1)
        vw = lax.dynamic_slice_in_dim(v_wp, q0, Q_BLOCK + NSA_WINDOW, axis=1)
        kpos = q0 - NSA_WINDOW + jnp.arange(Q_BLOCK + NSA_WINDOW)
        dist_w = qpos[:, None] - kpos[None, :]
        mask_w = (dist_w >= 0) & (dist_w < NSA_WINDOW) & (kpos[None, :] >= 0)
        s_w = jnp.einsum('bqkgd,bwkd->bkgqw', qb, kw) + head_bias(dist_w)
        p_w = masked_softmax(s_w, mask_w)
        o_w = jnp.einsum('bkgqw,bwkd->bqkgd', p_w, vw)
        return o_c, o_s, o_w

    o_c, o_s, o_w = lax.map(block, (jnp.arange(n_qb), qs))

    def unblock(t):
        return t.transpose(1, 0, 2, 3, 4, 5).reshape(B_, S_, NSA_HEADS, NSA_HEAD_DIM)

    gates = jax.nn.sigmoid(gate.astype(f32)).reshape(B_, S_, NSA_HEADS, 3)
    o = (gates[..., 0:1] * unblock(o_c) + gates[..., 1:2] * unblock(o_s)
         + gates[..., 2:3] * unblock(o_w))
    return rms_normalize(o.reshape(B_, S_, GROUP_W)) * norm_g


def ssd_mixer(z, xbc, dt, conv_w, conv_b, dt_bias, a_log, d_skip, norm_g):
    B_, S_, _ = z.shape
    f32 = jnp.float32
    xbc = jax.nn.silu(causal_depthwise_conv(xbc.astype(f32), conv_w.astype(f32), conv_b.astype(f32)))
    xs, bm, cm = jnp.split(xbc, [GROUP_W, GROUP_W + SSM_GROUPS * SSM_STATE], axis=-1)
    xs = xs.reshape(B_, S_, SSM_HEADS, SSM_HEAD_DIM)
    rep = SSM_HEADS // SSM_GROUPS
    bm = jnp.repeat(bm.reshape(B_, S_, SSM_GROUPS, SSM_STATE), rep, axis=2)
    cm = jnp.repeat(cm.reshape(B_, S_, SSM_GROUPS, SSM_STATE), rep, axis=2)
    dt = jax.nn.softplus(dt.astype(f32) + dt_bias)
    log_a = dt * (-jnp.exp(a_log.astype(f32)))
    y = chunked_scalar_decay(cm, bm, xs * dt[..., None], log_a) + d_skip[:, None] * xs
    y = y.reshape(B_, S_, GROUP_W) * jax.nn.silu(z.astype(f32))
    y = rms_normalize(y.reshape(B_, S_, SSM_GROUPS, GROUP_W // SSM_GROUPS)).reshape(B_, S_, GROUP_W)
    return y * norm_g


def retention_mixer(q, k, v, g):
    B_, S_, _ = q.shape
    f32 = jnp.float32
    pos = jnp.arange(S_, dtype=f32)
    q = rotary(q.astype(f32).reshape(B_, S_, RET_HEADS, RET_DK), pos)
    k = rotary(k.astype(f32).reshape(B_, S_, RET_HEADS, RET_DK), pos) * RET_DK ** -0.5
    v = v.astype(f32).reshape(B_, S_, RET_HEADS, RET_DK)
    log_gamma = jnp.log(1.0 - 2.0 ** (-5.0 - jnp.arange(RET_HEADS, dtype=f32)))
    y = chunked_scalar_decay(q, k, v, jnp.broadcast_to(log_gamma, (B_, S_, RET_HEADS)))
    mu = jnp.mean(y, axis=-1, keepdims=True)
    y = (y - mu) * lax.rsqrt(jnp.mean(jnp.square(y - mu), axis=-1, keepdims=True) + 1e-5)
    return jax.nn.silu(g.astype(f32)) * y.reshape(B_, S_, GROUP_W)


def hybrid_mixer(h, w_in, w_out, lb, hg_norm_g, pe_k, w1_k, w2_k, pe_v, w1_v, w2_v, nsa_norm_g,
                 rel_bias, conv_w, conv_b, dt_bias, a_log, d_skip, ssm_norm_g):
    (hq, hf, hi, hg, nq, nkc, nvc, nks, nvs, nkw, nvw, ngate,
     sz, sxbc, sdt, rq, rk, rv, rg) = jnp.split(h @ w_in, IN_SPLITS, axis=-1)
    o_a = hgrn2_mixer(hq, hf, hi, hg, lb, hg_norm_g)
    o_b = nsa_mixer(nq, nkc, nvc, nks, nvs, nkw, nvw, ngate, pe_k, w1_k, w2_k, pe_v, w1_v, w2_v,
                    rel_bias, nsa_norm_g)
    o_c = ssd_mixer(sz, sxbc, sdt, conv_w, conv_b, dt_bias, a_log, d_skip, ssm_norm_g)
    o_d = retention_mixer(rq, rk, rv, rg)
    mixed = jnp.concatenate([o_a, o_b, o_c, o_d], axis=-1)
    return (mixed.astype(w_out.dtype) @ w_out).astype(h.dtype)


def setup_inputs(seed: int = 0) -> dict:
    key = jax.random.key(seed)
    keys = iter(jax.random.split(key, 40))
    L = DEPTH

    def normal(shape, scale):
        return jax.random.normal(next(keys), shape, jnp.float32) * scale

    def gain(shape):
        return 1.0 + normal(shape, 0.02)

    w_in_s = D_MODEL ** -0.5
    ffn_out_s = D_FF ** -0.5 * DEEPNORM_BETA
    return {
        'x': normal((BATCH, SEQ, D_MODEL), 1.0),
        'ln1_g': gain((L, D_MODEL)),
        'ln1_b': normal((L, D_MODEL), 0.02),
        'ffn1_w1': normal((L, D_MODEL, D_FF), w_in_s),
        'ffn1_w3': normal((L, D_MODEL, D_FF), w_in_s),
        'ffn1_w2': normal((L, D_FF, D_MODEL), ffn_out_s),
        'ln2_g': gain((L, D_MODEL)),
        'ln2_b': normal((L, D_MODEL), 0.02),
        'w_in': normal((L, D_MODEL, D_IN), w_in_s),
        'w_out': normal((L, MIX_W, D_MODEL), MIX_W ** -0.5 * DEEPNORM_BETA),
        'hgrn_lb_logits': normal((L, HG_HEADS * HG_DK), 0.5),
        'hgrn_norm_g': gain((L, GROUP_W)),
        'nsa_pe_k': normal((L, NSA_CMP_BLOCK, NSA_HEAD_DIM), 0.1),
        'nsa_w1_k': normal((L, NSA_CMP_BLOCK * NSA_HEAD_DIM, NSA_CMP_HIDDEN), (NSA_CMP_BLOCK * NSA_HEAD_DIM) ** -0.5),
        'nsa_w2_k': normal((L, NSA_CMP_HIDDEN, NSA_HEAD_DIM), NSA_CMP_HIDDEN ** -0.5),
        'nsa_pe_v': normal((L, NSA_CMP_BLOCK, NSA_HEAD_DIM), 0.1),
        'nsa_w1_v': normal((L, NSA_CMP_BLOCK * NSA_HEAD_DIM, NSA_CMP_HIDDEN), (NSA_CMP_BLOCK * NSA_HEAD_DIM) ** -0.5),
        'nsa_w2_v': normal((L, NSA_CMP_HIDDEN, NSA_HEAD_DIM), NSA_CMP_HIDDEN ** -0.5),
        'nsa_norm_g': gain((L, GROUP_W)),
        'rel_bias': normal((REL_BUCKETS, NSA_HEADS), 0.2),
        'ssm_conv_w': normal((L, SSM_CONV, SSM_CONV_DIM), 0.5),
        'ssm_conv_b': normal((L, SSM_CONV_DIM), 0.02),
        'ssm_dt_bias': (lambda dt: dt + jnp.log(-jnp.expm1(-dt)))(jnp.exp(jax.random.uniform(
            next(keys), (L, SSM_HEADS), jnp.float32, math.log(1e-3), math.log(1e-1)))),
        'ssm_a_log': jnp.log(jax.random.uniform(next(keys), (L, SSM_HEADS), jnp.float32, 1.0, 16.0)),
        'ssm_d': gain((L, SSM_HEADS)),
        'ssm_norm_g': gain((L, GROUP_W)),
        'ln3_g': gain((L, D_MODEL)),
        'ln3_b': normal((L, D_MODEL), 0.02),
        'ffn2_w1': normal((L, D_MODEL, D_FF), w_in_s),
        'ffn2_w3': normal((L, D_MODEL, D_FF), w_in_s),
        'ffn2_w2': normal((L, D_FF, D_MODEL), ffn_out_s),
    }


def reference(x, ln1_g, ln1_b, ffn1_w1, ffn1_w3, ffn1_w2, ln2_g, ln2_b, w_in, w_out,
              hgrn_lb_logits, hgrn_norm_g, nsa_pe_k, nsa_w1_k, nsa_w2_k, nsa_pe_v, nsa_w1_v,
              nsa_w2_v, nsa_norm_g, rel_bias, ssm_conv_w, ssm_conv_b, ssm_dt_bias, ssm_a_log,
              ssm_d, ssm_norm_g, ln3_g, ln3_b, ffn2_w1, ffn2_w3, ffn2_w2):
    cum = jnp.cumsum(jax.nn.softmax(hgrn_lb_logits.astype(jnp.float32), axis=0), axis=0)
    lower_bounds = cum - cum[:1]
    for l in range(DEPTH):
        x = layer_norm(DEEPNORM_ALPHA * x + 0.5 * swiglu(x, ffn1_w1[l], ffn1_w3[l], ffn1_w2[l]),
                       ln1_g[l], ln1_b[l])
        mix = hybrid_mixer(x, w_in[l], w_out[l], lower_bounds[l], hgrn_norm_g[l],
                           nsa_pe_k[l], nsa_w1_k[l], nsa_w2_k[l], nsa_pe_v[l], nsa_w1_v[l],
                           nsa_w2_v[l], nsa_norm_g[l], rel_bias, ssm_conv_w[l], ssm_conv_b[l],
                           ssm_dt_bias[l], ssm_a_log[l], ssm_d[l], ssm_norm_g[l])
        x = layer_norm(DEEPNORM_ALPHA * x + mix, ln2_g[l], ln2_b[l])
        x = layer_norm(DEEPNORM_ALPHA * x + 0.5 * swiglu(x, ffn2_w1[l], ffn2_w3[l], ffn2_w2[l]),
                       ln3_g[l], ln3_b[l])
    return x
```

```python
import numpy as np
import ml_dtypes
from contextlib import ExitStack
import concourse.bass as bass
import concourse.mybir as mybir
from concourse.bass_utils import run_bass_kernel_spmd

F32 = mybir.dt.float32
BF16 = mybir.dt.bfloat16
AF = mybir.ActivationFunctionType
ALU = mybir.AluOpType
AX = mybir.AxisListType

D = 2048
DFF = 5632
DEPTH = 2
ALPHA = float((2 * DEPTH) ** 0.25)
KQ = 6


class TB:
    def __init__(self, h, psum=False):
        self.h = h
        self.w = None
        self.r = {}
        self.psum = psum

    def __getitem__(self, k):
        return self.h[k]


class Prog:
    def __init__(self, nc):
        self.nc = nc
        self.eng = {'pe': nc.tensor, 'dve': nc.vector, 'act': nc.scalar, 'pool': nc.gpsimd, 'sp': nc.sync}
        self.sem = {k: nc.alloc_semaphore('s_' + k) for k in self.eng}
        self.cnt = {k: 0 for k in self.eng}
        self.waited = {k: {} for k in self.eng}
        self.dq = {q: dict(sems=[nc.alloc_semaphore(f'd_{q}{i}') for i in range(KQ)], n=0)
                   for q in ('sp', 'act', 'pool')}
        self.arrive = nc.alloc_semaphore('arrive')
        self.go = nc.alloc_semaphore('go')
        self.epoch = 0
        self.uid = 0
        self.rr = 0

    def sb(self, es, shape, dt, name='t'):
        self.uid += 1
        return TB(es.enter_context(self.nc.sbuf_tensor(f'{name}_{self.uid}', list(shape), dt)))

    def ps(self, es, shape, dt, name='p'):
        self.uid += 1
        return TB(es.enter_context(self.nc.psum_tensor(f'{name}_{self.uid}', list(shape), dt)), psum=True)

    def _wait(self, e, tok):
        if tok is None or tok[-1] != self.epoch:
            return
        if tok[0] == 'c':
            key, sem, v = tok[1], self.sem[tok[1]], tok[2]
            if key == e and e == 'pe':
                return
        else:
            key, sem, v = tok[2], tok[1], tok[3]
        if self.waited[e].get(key, 0) >= v:
            return
        self.eng[e].wait_ge(sem, v)
        self.waited[e][key] = v

    def _deps(self, e, reads, writes, is_dma=False):
        for b in reads:
            self._wait(e, b.w)
            if b.psum:
                for k, tok in b.r.items():
                    if k != e:
                        self._wait(e, tok)
        for b in writes:
            self._wait(e, b.w)
            for k, tok in b.r.items():
                self._wait(e, tok)

    def op(self, e, fn, reads=(), writes=(), inc=True):
        self._deps(e, reads, writes)
        ins = fn(self.eng[e])
        if inc:
            self.cnt[e] += 1
            assert self.cnt[e] < 60000, "semaphore overflow: add a barrier"
            ins.then_inc(self.sem[e], 1)
            tok = ('c', e, self.cnt[e], self.epoch)
        else:
            tok = ('c', e, self.cnt[e] + 1, self.epoch)
        for b in reads:
            b.r[e] = tok
        for b in writes:
            b.w = tok
            b.r = {}
        return ins

    def dma(self, q, out, in_, reads=(), writes=(), **kw):
        d = self.dq[q]
        i = d['n']
        slot = i % KQ
        sem = d['sems'][slot]
        v = 16 * (i // KQ + 1)
        assert v < 60000, "dma semaphore overflow: add a barrier"
        if i >= KQ:
            self._wait(q, ('d', sem, (q, slot), v - 16, self.epoch))
        self._deps(q, reads, writes, is_dma=True)
        self.eng[q].dma_start(out=out, in_=in_, **kw).then_inc(sem, 16)
        d['n'] += 1
        tok = ('d', sem, (q, slot), v, self.epoch)
        for b in reads:
            b.r[('d', q, slot)] = tok
        for b in writes:
            b.w = tok
            b.r = {}

    def dmaq(self):
        self.rr += 1
        return ('sp', 'sp')[self.rr % 2]

    def barrier(self):
        names = list(self.eng)
        for e in names:
            for w in names:
                if w != e and self.cnt[w] > 0:
                    self._wait(e, ('c', w, self.cnt[w], self.epoch))
            for q, d in self.dq.items():
                for slot in range(KQ):
                    n_used = (d['n'] - slot + KQ - 1) // KQ
                    if n_used > 0:
                        self._wait(e, ('d', d['sems'][slot], (q, slot), 16 * n_used, self.epoch))
        self.epoch += 1
        assert self.epoch < 4000
        for e in names:
            if e != 'pool':
                self.eng[e].sem_inc(self.arrive, 1)
        g = self.eng['pool']
        g.wait_ge(self.arrive, 4 * self.epoch)
        for s in self.sem.values():
            g.sem_clear(s)
        for d in self.dq.values():
            for s in d['sems']:
                g.sem_clear(s)
            d['n'] = 0
        g.sem_inc(self.go, 1)
        for e in names:
            self.eng[e].wait_ge(self.go, self.epoch)
        self.cnt = {k: 0 for k in self.eng}
        self.waited = {k: {} for k in self.eng}


def cast_rr(P, i, out_ap, in_ap, reads, writes):
    e = ('dve', 'pool', 'act')[i % 3]
    if e == 'act':
        P.op('act', lambda g: g.copy(out=out_ap, in_=in_ap), reads, writes)
    else:
        P.op(e, lambda g: g.tensor_copy(out=out_ap, in_=in_ap), reads, writes)


def phase_convert_ffn(P, w1, w3, w2, w1t, w3t, w2t, dff):
    nc = P.nc
    HC = dff // 128
    KC = D // 128
    with ExitStack() as es:
        nb = 2
        fin = [P.sb(es, [128, max(dff, D)], F32, 'cvf') for _ in range(nb)]
        fo = [P.sb(es, [128, max(dff, D)], BF16, 'cvb') for _ in range(nb)]
        i = 0
        for (w, wt) in ((w1, w1t), (w3, w3t)):
            for kc in range(KC):
                a, b = fin[i % nb], fo[i % nb]
                for c0 in range(0, dff, 1024):
                    c1 = min(dff, c0 + 1024)
                    P.dma(P.dmaq(), a[:, c0:c1], w[kc * 128:(kc + 1) * 128, c0:c1], writes=[a])
                for c0 in range(0, dff, 2048):
                    c1 = min(dff, c0 + 2048)
                    cast_rr(P, i + c0 // 2048, b[:, c0:c1], a[:, c0:c1], [a], [b])
                for h0 in range(0, HC, 4):
                    h1 = min(HC, h0 + 4)
                    P.dma(P.dmaq(), wt[h0:h1, :, kc, :].rearrange("h p n -> p h n"),
                          b[:, h0 * 128:h1 * 128].rearrange("p (h n) -> p h n", n=128), reads=[b])
                i += 1
        for hc in range(HC):
            a, b = fin[i % nb], fo[i % nb]
            P.dma(P.dmaq(), a[:, :D], w2[hc * 128:(hc + 1) * 128, :], writes=[a])
            cast_rr(P, i, b[:, :D], a[:, :D], [a], [b])
            for c0 in range(0, KC, 4):
                P.dma(P.dmaq(), w2t[c0:c0 + 4, :, hc, :].rearrange("c p n -> p c n"),
                      b[:, c0 * 128:(c0 + 4) * 128].rearrange("p (c n) -> p c n", n=128), reads=[b])
            i += 1
    P.barrier()


def load_bcast_rows(P, es, vec_ap, n, name):
    t = P.sb(es, [128, n], F32, name)
    P.dma('sp', t[:, :], vec_ap.partition_broadcast(128), writes=[t])
    return t


def phase_ffn(P, S, x_in, x_out, w1t, w3t, w2t, ln_g, ln_b, consts, dff):
    nc = P.nc
    HC = dff // 128
    KC = D // 128
    NT = 512 if S % 512 == 0 else S
    ntile = S // NT
    nsub = NT // 128
    with ExitStack() as es:
        ident = consts['ident']
        cb = consts.t
        gt = load_bcast_rows(P, es, ln_g, D, 'lng')
        bt = load_bcast_rows(P, es, ln_b, D, 'lnb')
        xs = [[P.sb(es, [128, D], F32, 'xs') for _ in range(nsub)] for _ in range(1)]
        xT = [P.sb(es, [128, KC, NT], BF16, 'xT') for _ in range(2)]
        gT = P.sb(es, [128, HC, NT], BF16, 'gT')
        w1b = [P.sb(es, [128, KC, 128], BF16, 'w1b') for _ in range(3)]
        w3b = [P.sb(es, [128, KC, 128], BF16, 'w3b') for _ in range(3)]
        w2b = [P.sb(es, [128, HC, 128], BF16, 'w2b') for _ in range(2)]
        sg = [P.sb(es, [128, NT], F32, 'sg') for _ in range(2)]
        oT = [P.sb(es, [128, NT], F32, 'oT') for _ in range(2)]
        y = [P.sb(es, [128, D], F32, 'y') for _ in range(2)]
        st = [P.sb(es, [128, 4, 6], F32, 'st') for _ in range(2)]
        mv = [P.sb(es, [128, 2], F32, 'mv') for _ in range(2)]
        rs = [P.sb(es, [128, 1], F32, 'rs') for _ in range(2)]
        pa = [P.ps(es, [128, 512], F32, 'pa') for _ in range(2)]
        pb = [P.ps(es, [128, 512], F32, 'pb') for _ in range(2)]
        po = [P.ps(es, [128, 512], F32, 'po') for _ in range(2)]
        pt = [P.ps(es, [128, 512], F32, 'pt') for _ in range(2)]
        ic = 0
        for it in range(ntile):
            X, XT = xs[0], xT[it % 2]
            t0 = it * NT
            for s in range(nsub):
                P.dma(P.dmaq(), X[s][:, :], x_in[t0 + s * 128: t0 + (s + 1) * 128, :], writes=[X[s]])
            for s in range(nsub):
                for k4 in range(KC // 4):
                    p_ = pt[ic % 2]
                    ic += 1
                    for j in range(4):
                        kc = k4 * 4 + j
                        P.op('pe', lambda g, p_=p_, j=j, kc=kc, s=s: g.transpose(
                            out=p_[:, j * 128:(j + 1) * 128], in_=X[s][:, kc * 128:(kc + 1) * 128],
                            identity=ident), [X[s], cb], [p_], inc=(j == 3))
                    e = 'dve' if (ic % 2) else 'act'
                    outap = XT[:, k4 * 4:(k4 + 1) * 4, s * 128:(s + 1) * 128]
                    inap = p_[:, :].rearrange("p (j n) -> p j n", n=128)
                    if e == 'dve':
                        P.op('dve', lambda g, o=outap, i_=inap: g.tensor_copy(out=o, in_=i_), [p_], [XT])
                    else:
                        P.op('act', lambda g, o=outap, i_=inap: g.copy(out=o, in_=i_), [p_], [XT])
            for hc in range(HC):
                wa, wb = w1b[hc % 3], w3b[hc % 3]
                P.dma('sp', wa[:, :, :], w1t[hc, :, :, :], writes=[wa])
                P.dma('act', wb[:, :, :], w3t[hc, :, :, :], writes=[wb])
                A, B = pa[hc % 2], pb[hc % 2]
                for kc in range(KC):
                    P.op('pe', lambda g, A=A, wa=wa, kc=kc: g.matmul(
                        A[:, :NT], lhsT=wa[:, kc, :], rhs=XT[:, kc, :], start=(kc == 0), stop=(kc == KC - 1)),
                        [wa, XT], [A], inc=(kc == KC - 1))
                for kc in range(KC):
                    P.op('pe', lambda g, B=B, wb=wb, kc=kc: g.matmul(
                        B[:, :NT], lhsT=wb[:, kc, :], rhs=XT[:, kc, :], start=(kc == 0), stop=(kc == KC - 1)),
                        [wb, XT], [B], inc=(kc == KC - 1))
                S_ = sg[hc % 2]
                P.op('act', lambda g, S_=S_, A=A: g.activation(out=S_[:, :NT], in_=A[:, :NT], func=AF.Silu),
                     [A], [S_])
                P.op('dve', lambda g, S_=S_, B=B, hc=hc: g.tensor_tensor(
                    out=gT[:, hc, :], in0=S_[:, :NT], in1=B[:, :NT], op=ALU.mult), [S_, B], [gT])
            for dc in range(KC):
                w2 = w2b[dc % 2]
                P.dma(P.dmaq(), w2[:, :, :], w2t[dc, :, :, :], writes=[w2])
                O = po[dc % 2]
                for hc in range(HC):
                    P.op('pe', lambda g, O=O, w2=w2, hc=hc: g.matmul(
                        O[:, :NT], lhsT=w2[:, hc, :], rhs=gT[:, hc, :], start=(hc == 0), stop=(hc == HC - 1)),
                        [w2, gT], [O], inc=(hc == HC - 1))
                ot = oT[dc % 2]
                P.op('act', lambda g, ot=ot, O=O: g.copy(out=ot[:, :NT], in_=O[:, :NT]), [O], [ot])
                for s in range(nsub):
                    p_ = pt[ic % 2]
                    ic += 1
                    P.op('pe', lambda g, p_=p_, ot=ot, s=s: g.transpose(
                        out=p_[:, 0:128], in_=ot[:, s * 128:(s + 1) * 128], identity=ident),
                        [ot, cb], [p_])
                    P.op('dve', lambda g, p_=p_, s=s, dc=dc: g.scalar_tensor_tensor(
                        out=X[s][:, dc * 128:(dc + 1) * 128], in0=X[s][:, dc * 128:(dc + 1) * 128], scalar=2.0 * ALPHA,
                        in1=p_[:, 0:128], op0=ALU.mult, op1=ALU.add), [p_, X[s]], [X[s]])
            for s in range(nsub):
                layer_norm_rows(P, X[s], y[s % 2], gt, bt, st[s % 2], mv[s % 2], rs[s % 2], 0.5, 1e-5)
                P.dma(P.dmaq(), x_out[t0 + s * 128: t0 + (s + 1) * 128, :], y[s % 2][:, :], reads=[y[s % 2]])
    P.barrier()


def layer_norm_rows(P, xin, yout, gt, bt, st, mv, rs, prescale, eps):
    nchunk = D // 512
    for c in range(nchunk):
        P.op('dve', lambda g, c=c: g.bn_stats(out=st[:, c, :], in_=xin[:, c * 512:(c + 1) * 512]), [xin], [st])
    P.op('dve', lambda g: g.bn_aggr(out=mv[:, :], in_=st[:, :, :]), [st], [mv])
    P.op('act', lambda g: g.activation(out=rs[:, :], in_=mv[:, 1:2], func=AF.Sqrt,
                                       bias=eps / (prescale * prescale), scale=1.0), [mv], [rs])
    P.op('dve', lambda g: g.reciprocal(out=rs[:, :], in_=rs[:, :]), [rs], [rs])
    P.op('dve', lambda g: g.tensor_scalar(out=yout[:, :], in0=xin[:, :], scalar1=mv[:, 0:1], scalar2=rs[:, 0:1],
                                          op0=ALU.subtract, op1=ALU.mult), [xin, mv, rs], [yout])
    P.op('pool', lambda g: g.tensor_tensor(out=yout[:, :], in0=yout[:, :], in1=gt[:, :], op=ALU.mult),
         [yout, gt], [yout])
    P.op('pool', lambda g: g.tensor_tensor(out=yout[:, :], in0=yout[:, :], in1=bt[:, :], op=ALU.add),
         [yout, bt], [yout])


RET_H = 4
GROUPS = [
    ('hq', 0, 512, 'FM'), ('hfF', 512, 512, 'FM'), ('hi', 1024, 512, 'TM'), ('hg', 1536, 512, 'TM'),
    ('nq', 2048, 512, 'FM'), ('nkc', 2560, 128, 'FM'), ('nvc', 2688, 128, 'FM'), ('nks', 2816, 128, 'FM'),
    ('nvs', 2944, 128, 'TM'), ('nkw', 3072, 128, 'FM'), ('nvw', 3200, 128, 'TM'), ('ngate', 3328, 24, 'TM'),
    ('sz', 3352, 512, 'TM'), ('sxbc', 3864, 1024, 'FM'), ('sdt', 4888, 8, 'TM'),
    ('rq', 4896, 512, 'TM'), ('rk', 5408, 512, 'TM'), ('rv', 5920, 512, 'TM'), ('rg', 6432, 512, 'TM'),
]
DIN = 6944


def _const_table():
    c = {}
    i = np.arange(128)
    c['ident'] = np.eye(128)
    c['U'] = (i[:, None] <= i[None, :]).astype(np.float64)
    c['J'] = np.eye(128)[::-1].copy()
    c['ones'] = np.ones((128, 128))
    c['negmask'] = np.where(i[:, None] <= i[None, :], 0.0, -1e5)
    blk = (i[:, None] // 32) == (i[None, :] // 32)
    c['M32'] = (blk & (i[:, None] <= i[None, :])).astype(np.float64)
    c['U32'] = c['M32'].copy()
    c['B32'] = blk.astype(np.float64)
    c['V32'] = (blk & (i[:, None] > i[None, :])).astype(np.float64)
    ns = np.ones((128, 512)); ns[:, ::32] = 0.0
    c['notstart'] = ns
    c['bm'] = np.stack([(i // 32 == (cc % 4)).astype(np.float64) for cc in range(32)], axis=1)
    lg = np.log(1.0 - 2.0 ** (-5.0 - np.arange(RET_H)))
    L = np.zeros((RET_H, 128, 128))
    for h in range(RET_H):
        L[h] = np.where(i[:, None] <= i[None, :], np.exp(lg[h] * (i[None, :] - i[:, None])), 0.0)
    c['retL'] = np.concatenate(list(L), axis=1)
    c['retG1'] = np.concatenate([np.broadcast_to(np.exp(lg[h] * (i[None, :] + 1)), (128, 128)) for h in range(RET_H)], 1)
    c['retW'] = np.concatenate([np.broadcast_to(np.exp(lg[h] * (127 - i[:, None])) * 128 ** -0.5, (128, 128))
                                for h in range(RET_H)], 1)
    return c, [float(np.exp(lg[h] * 128)) for h in range(RET_H)]


_CT, RET_G128 = _const_table()
COFF = {}
_o = 0
for _k, _v in _CT.items():
    COFF[_k] = (_o, _v.shape[1])
    _o += _v.shape[1]
NCONST = _o


def host_consts(S):
    pack = np.concatenate([v for v in _CT.values()], axis=1).astype(np.float32)
    half = 64
    theta = (1.0 / (np.float32(10000.0) ** np.linspace(0.0, 1.0, half, dtype=np.float32))).astype(np.float32)
    ang = (np.arange(S, dtype=np.float32)[:, None] * theta[None, :]).astype(np.float32)
    d = {'c_pack': pack, 'c_cos': np.cos(ang).astype(np.float32), 'c_sin': np.sin(ang).astype(np.float32)}
    d.update(nsa_host_consts(S))
    return d


class Consts:
    def __init__(self, P, es, cin):
        self.t = P.sb(es, [128, NCONST], F32, 'cpack')
        P.dma('sp', self.t[:, :], cin['pack'][:, :], writes=[self.t])
        self.cin = cin

    def __getitem__(self, k):
        o, n = COFF[k]
        return self.t[:, o:o + n]

    def sl(self, k, a, b):
        o, n = COFF[k]
        return self.t[:, o + a:o + b]


def build_xT(P, X, XT, s, ident, cb, pt, ic):
    KC = D // 128
    for k4 in range(KC // 4):
        p_ = pt[ic[0] % 2]
        ic[0] += 1
        for j in range(4):
            kc = k4 * 4 + j
            P.op('pe', lambda g, p_=p_, j=j, kc=kc: g.transpose(
                out=p_[:, j * 128:(j + 1) * 128], in_=X[:, kc * 128:(kc + 1) * 128], identity=ident),
                [X, cb], [p_], inc=(j == 3))
        outap = XT[:, k4 * 4:(k4 + 1) * 4, s * 128:(s + 1) * 128]
        inap = p_[:, :].rearrange("p (j n) -> p j n", n=128)
        if ic[0] % 2:
            P.op('dve', lambda g, o=outap, i_=inap: g.tensor_copy(out=o, in_=i_), [p_], [XT])
        else:
            P.op('act', lambda g, o=outap, i_=inap: g.copy(out=o, in_=i_), [p_], [XT])


def phase_convert_rows(P, w, wt, ncols):
    KC = D // 128
    with ExitStack() as es:
        fin = [P.sb(es, [128, ncols], F32, 'cvf') for _ in range(2)]
        fo = [P.sb(es, [128, ncols], BF16, 'cvb') for _ in range(2)]
        for kc in range(KC):
            a, b = fin[kc % 2], fo[kc % 2]
            for c0 in range(0, ncols, 1024):
                c1 = min(ncols, c0 + 1024)
                P.dma('sp', a[:, c0:c1], w[kc * 128:(kc + 1) * 128, c0:c1], writes=[a])
            for c0 in range(0, ncols, 2048):
                c1 = min(ncols, c0 + 2048)
                cast_rr(P, kc + c0 // 2048, b[:, c0:c1], a[:, c0:c1], [a], [b])
            P.dma('act', wt[:, kc, :], b[:, :], reads=[b])
    P.barrier()


def phase_proj(P, S, h_in, wint, dst, C):
    KC = D // 128
    NT = 512 if S % 512 == 0 else S
    nsub = NT // 128
    with ExitStack() as es:
        X = [P.sb(es, [128, D], F32, 'px') for _ in range(2)]
        XT = [P.sb(es, [128, KC, NT], BF16, 'pxT') for _ in range(2)]
        wb = [P.sb(es, [128, KC, 512], BF16, 'pw') for _ in range(2)]
        ob = [P.sb(es, [128, 512], F32, 'po') for _ in range(3)]
        pm = [P.ps(es, [128, 512], F32, 'pm') for _ in range(4)]
        pt = [P.ps(es, [128, 512], F32, 'pt') for _ in range(2)]
        ic = [0]
        io = 0
        ipm = 0
        pieces = []
        for (nm, c0, n, mode) in GROUPS:
            step = 128 if mode == 'FM' else 512
            for a in range(0, n, step):
                pieces.append((nm, c0, a, min(step, n - a), mode))
        for it in range(S // NT):
            t0 = it * NT
            xt = XT[it % 2]
            for s in range(nsub):
                x = X[s % 2]
                P.dma('sp', x[:, :], h_in[t0 + s * 128:t0 + (s + 1) * 128, :], writes=[x])
                build_xT(P, x, xt, s, C['ident'], C.t, pt, ic)
            for ip, (nm, c0, a, n, mode) in enumerate(pieces):
                w = wb[ip % 2]
                P.dma('sp', w[:, :, :n], wint[:, :, c0 + a:c0 + a + n], writes=[w])
                if mode == 'FM':
                    pp = pm[ipm % 4]
                    ipm += 1
                    for kc in range(KC):
                        P.op('pe', lambda g, pp=pp, w=w, kc=kc, n=n: g.matmul(
                            pp[:n, :NT], lhsT=w[:, kc, :n], rhs=xt[:, kc, :], start=(kc == 0), stop=(kc == KC - 1)),
                            [w, xt], [pp], inc=(kc == KC - 1))
                    o = ob[io % 3]
                    io += 1
                    ev = 'act' if io % 2 else 'dve'
                    if ev == 'act':
                        P.op('act', lambda g, o=o, pp=pp, n=n: g.copy(out=o[:n, :NT], in_=pp[:n, :NT]), [pp], [o])
                    else:
                        P.op('dve', lambda g, o=o, pp=pp, n=n: g.tensor_copy(out=o[:n, :NT], in_=pp[:n, :NT]), [pp], [o])
                    P.dma('act', dst[nm][a:a + n, t0:t0 + NT], o[:n, :NT], reads=[o])
                else:
                    for s in range(nsub):
                        pp = pm[ipm % 4]
                        ipm += 1
                        for kc in range(KC):
                            P.op('pe', lambda g, pp=pp, w=w, kc=kc, n=n, s=s: g.matmul(
                                pp[:, :n], lhsT=xt[:, kc, s * 128:(s + 1) * 128], rhs=w[:, kc, :n],
                                start=(kc == 0), stop=(kc == KC - 1)), [w, xt], [pp], inc=(kc == KC - 1))
                        o = ob[io % 3]
                        io += 1
                        if io % 2:
                            P.op('act', lambda g, o=o, pp=pp, n=n: g.copy(out=o[:, :n], in_=pp[:, :n]), [pp], [o])
                        else:
                            P.op('dve', lambda g, o=o, pp=pp, n=n: g.tensor_copy(out=o[:, :n], in_=pp[:, :n]), [pp], [o])
                        P.dma('act', dst[nm][t0 + s * 128:t0 + (s + 1) * 128, a:a + n], o[:, :n], reads=[o])
    P.barrier()


import os
RET_STOP = int(os.environ.get('RET_STOP', '99'))
RET_SUB = int(os.environ.get('RET_SUB', '99'))
LBL = int(os.environ.get('LBL', '0'))


def phase_retention(P, S, src, mixed, C, cin):
    H = RET_H
    with ExitStack() as es:
        q = [P.sb(es, [128, 512], F32, 'rq') for _ in range(2)]
        k = [P.sb(es, [128, 512], F32, 'rk') for _ in range(2)]
        v = [P.sb(es, [128, 512], F32, 'rv') for _ in range(2)]
        gt = [P.sb(es, [128, 512], F32, 'rg') for _ in range(2)]
        cs = [P.sb(es, [128, 64], F32, 'rc') for _ in range(2)]
        sn = [P.sb(es, [128, 64], F32, 'rs') for _ in range(2)]
        qr = [P.sb(es, [128, 512], F32, 'rqr') for _ in range(2)]
        kr = [P.sb(es, [128, 512], F32, 'rkr') for _ in range(2)]
        kh = [P.sb(es, [128, 512], F32, 'rkh') for _ in range(2)]
        tA = P.sb(es, [128, 256], F32, 'rtA')
        tB = P.sb(es, [128, 256], F32, 'rtB')
        qT = [P.sb(es, [128, 128], F32, 'rqT') for _ in range(2)]
        qTd = [P.sb(es, [128, 128], F32, 'rqTd') for _ in range(2)]
        kT = [P.sb(es, [128, 128], F32, 'rkT') for _ in range(2)]
        AT = [P.sb(es, [128, 128], F32, 'rAT') for _ in range(2)]
        St = [P.sb(es, [128, 128], F32, 'rS') for _ in range(H)]
        oall = [P.sb(es, [128, 512], F32, 'ro') for _ in range(2)]
        st = P.sb(es, [128, H, 6], F32, 'rst')
        mv = P.sb(es, [128, H, 2], F32, 'rmv')
        rs = P.sb(es, [128, H], F32, 'rrs')
        pT = [P.ps(es, [128, 512], F32, 'rpT') for _ in range(2)]
        pA = [P.ps(es, [128, 512], F32, 'rpA') for _ in range(2)]
        pO = [P.ps(es, [128, 512], F32, 'rpO') for _ in range(2)]
        pS = [P.ps(es, [128, 512], F32, 'rpS') for _ in range(2)]
        for h in range(H):
            P.op('pool', lambda g, h=h: g.memset(St[h][:, :], 0.0), [], [St[h]])
        for a_ in range(2):
            P.op('pool', lambda g, a_=a_: g.memset(AT[a_][:, :], 0.0), [], [AT[a_]])
        ia = 0
        for it in range(S // 128):
            b = it % 2
            t0 = it * 128
            sl = slice(t0, t0 + 128)
            P.dma('sp', q[b][:, :], src['rq'][sl, :], writes=[q[b]])
            P.dma('sp', k[b][:, :], src['rk'][sl, :], writes=[k[b]])
            P.dma('sp', v[b][:, :], src['rv'][sl, :], writes=[v[b]])
            P.dma('sp', gt[b][:, :], src['rg'][sl, :], writes=[gt[b]])
            P.dma('sp', cs[b][:, :], cin['cos'][sl, :], writes=[cs[b]])
            P.dma('sp', sn[b][:, :], cin['sin'][sl, :], writes=[sn[b]])
            for (src_t, dst_t, eng) in ((q[b], qr[b], 'dve'), (k[b], kr[b], 'pool')):
                s4 = src_t[:, :].rearrange("p (h k two) -> p h k two", h=H, two=2)
                d4 = dst_t[:, :].rearrange("p (h k two) -> p h k two", h=H, two=2)
                t1, t2 = s4[:, :, :, 0], s4[:, :, :, 1]
                cosb = cs[b][:, :].unsqueeze(1).broadcast_to([128, H, 64])
                sinb = sn[b][:, :].unsqueeze(1).broadcast_to([128, H, 64])
                a3 = tA[:, :].rearrange("p (h k) -> p h k", h=H)
                b3 = tB[:, :].rearrange("p (h k) -> p h k", h=H)
                P.op(eng, lambda g, a3=a3, t1=t1, cosb=cosb: g.tensor_tensor(out=a3, in0=t1, in1=cosb, op=ALU.mult),
                     [src_t, cs[b]], [tA])
                P.op(eng, lambda g, b3=b3, t2=t2, sinb=sinb: g.tensor_tensor(out=b3, in0=t2, in1=sinb, op=ALU.mult),
                     [src_t, sn[b]], [tB])
                P.op(eng, lambda g, d4=d4, a3=a3, b3=b3: g.tensor_tensor(out=d4[:, :, :, 0], in0=a3, in1=b3,
                                                                        op=ALU.subtract), [tA, tB], [dst_t])
                P.op(eng, lambda g, a3=a3, t1=t1, sinb=sinb: g.tensor_tensor(out=a3, in0=t1, in1=sinb, op=ALU.mult),
                     [src_t, sn[b]], [tA])
                P.op(eng, lambda g, b3=b3, t2=t2, cosb=cosb: g.tensor_tensor(out=b3, in0=t2, in1=cosb, op=ALU.mult),
                     [src_t, cs[b]], [tB])
                P.op(eng, lambda g, d4=d4, a3=a3, b3=b3: g.tensor_tensor(out=d4[:, :, :, 1], in0=a3, in1=b3,
                                                                        op=ALU.add), [tA, tB], [dst_t])
            if RET_STOP == 1:
                P.dma('act', mixed[sl, 1536:2048], qr[b][:, :], reads=[qr[b]])
                P.dma('act', mixed[sl, 1024:1536], kr[b][:, :], reads=[kr[b]])
                continue
            P.op('pool', lambda g, b=b: g.tensor_tensor(out=kh[b][:, :], in0=kr[b][:, :], in1=C['retW'], op=ALU.mult),
                 [kr[b], C.t], [kh[b]])
            P.op('act', lambda g, b=b: g.mul(out=kr[b][:, :], in_=kr[b][:, :], mul=128 ** -0.5), [kr[b]], [kr[b]])
            if RET_STOP == 2:
                P.dma('act', mixed[sl, 1536:2048], kh[b][:, :], reads=[kh[b]])
                P.dma('act', mixed[sl, 1024:1536], kr[b][:, :], reads=[kr[b]])
                continue
            for h in range(H):
                hs = slice(h * 128, (h + 1) * 128)
                a = ia % 2
                ia += 1
                P.op('pe', lambda g, a=a, hs=hs: g.transpose(out=pT[a][:, 0:128], in_=qr[b][:, hs], identity=C['ident']),
                     [qr[b], C.t], [pT[a]], inc=False)
                P.op('pe', lambda g, a=a, hs=hs: g.transpose(out=pT[a][:, 128:256], in_=kr[b][:, hs], identity=C['ident']),
                     [kr[b], C.t], [pT[a]])
                if RET_SUB == 1:
                    continue
                P.op('act', lambda g, a=a: g.copy(out=qT[a][:, :], in_=pT[a][:, 0:128]), [pT[a]], [qT[a]])
                if RET_SUB == 2:
                    continue
                P.op('dve', lambda g, a=a, h=h: g.tensor_tensor(out=qTd[a][:, :], in0=C.sl('retG1', h * 128, (h + 1) * 128),
                                                               in1=qT[a][:, :], op=ALU.mult),
                     [qT[a], C.t], [qTd[a]])
                if RET_SUB == 3:
                    continue
                P.op('act', lambda g, a=a: g.copy(out=kT[a][:, :], in_=pT[a][:, 128:256]), [pT[a]], [kT[a]])
                if RET_STOP == 5:
                    P.op('dve', lambda g, a=a: g.tensor_copy(out=AT[a][:, :], in_=kT[a][:, :]), [kT[a], qT[a], qTd[a]], [AT[a]])
                    continue
                P.op('pe', lambda g, a=a: g.matmul(pA[a][:, 0:128], lhsT=kT[a][:, :], rhs=qT[a][:, :], start=True, stop=True),
                     [kT[a], qT[a]], [pA[a]])
                P.op('dve', lambda g, a=a, h=h: g.tensor_tensor(out=AT[a][:, :], in0=C.sl('retL', h * 128, (h + 1) * 128),
                                                               in1=pA[a][:, 0:128], op=ALU.mult),
                     [pA[a], C.t], [AT[a]])
                if RET_STOP == 4:
                    continue
                po = pO[b]
                P.op('pe', lambda g, a=a, hs=hs, po=po: g.matmul(po[:, hs], lhsT=AT[a][:, :], rhs=v[b][:, hs],
                                                                 start=True, stop=False), [AT[a], v[b]], [po], inc=False)
                P.op('pe', lambda g, a=a, hs=hs, po=po, h=h: g.matmul(po[:, hs], lhsT=qTd[a][:, :], rhs=St[h][:, :],
                                                                      start=False, stop=True), [qTd[a], St[h]], [po])
                P.op('pe', lambda g, a=a, hs=hs: g.matmul(pS[a][:, 0:128], lhsT=kh[b][:, hs], rhs=v[b][:, hs],
                                                          start=True, stop=True), [kh[b], v[b]], [pS[a]])
                P.op('dve', lambda g, a=a, h=h: g.scalar_tensor_tensor(
                    out=St[h][:, :], in0=St[h][:, :], scalar=RET_G128[h], in1=pS[a][:, 0:128], op0=ALU.mult, op1=ALU.add),
                    [St[h], pS[a]], [St[h]])
            if RET_STOP in (4, 5):
                P.dma('act', mixed[sl, 1536:1664], AT[0][:, :], reads=[AT[0]])
                P.dma('act', mixed[sl, 1664:1792], AT[1][:, :], reads=[AT[1]])
                continue
            o = oall[b]
            P.op('act', lambda g, o=o, b=b: g.copy(out=o[:, :], in_=pO[b][:, :]), [pO[b]], [o])
            if RET_STOP == 3:
                P.dma('act', mixed[sl, 1536:2048], o[:, :], reads=[o])
                continue
            for h in range(H):
                P.op('dve', lambda g, h=h, o=o: g.bn_stats(out=st[:, h, :], in_=o[:, h * 128:(h + 1) * 128]), [o], [st])
                P.op('dve', lambda g, h=h: g.bn_aggr(out=mv[:, h, :], in_=st[:, h, :]), [st], [mv])
            P.op('act', lambda g: g.activation(out=rs[:, :], in_=mv[:, :, 1], func=AF.Sqrt, bias=1e-5, scale=1.0),
                 [mv], [rs])
            P.op('dve', lambda g: g.reciprocal(out=rs[:, :], in_=rs[:, :]), [rs], [rs])
            o3 = o[:, :].rearrange("p (h d) -> p h d", h=H)
            P.op('dve', lambda g, o3=o3: g.tensor_tensor(out=o3, in0=o3, in1=mv[:, :, 0:1].broadcast_to([128, H, 128]),
                                                         op=ALU.subtract), [o, mv], [o])
            P.op('dve', lambda g, o3=o3: g.tensor_tensor(out=o3, in0=o3, in1=rs[:, :].unsqueeze(2).broadcast_to([128, H, 128]),
                                                         op=ALU.mult), [o, rs], [o])
            P.op('act', lambda g, b=b: g.activation(out=gt[b][:, :], in_=gt[b][:, :], func=AF.Silu), [gt[b]], [gt[b]])
            P.op('pool', lambda g, o=o, b=b: g.tensor_tensor(out=o[:, :], in0=o[:, :], in1=gt[b][:, :], op=ALU.mult),
                 [o, gt[b]], [o])
            P.dma('act', mixed[sl, 1536:2048], o[:, :], reads=[o])
    P.barrier()


def bc_load(P, es, vec_ap, n, name):
    t = P.sb(es, [128, n], F32, name)
    P.dma('sp', t[:, :], vec_ap.partition_broadcast(128), writes=[t])
    return t


def phase_ssd(P, S, src, mixed, C, W, l):
    with ExitStack() as es:
        cw = P.sb(es, [128, 4, 8], F32, 'scw')
        for k in range(4):
            P.dma('sp', cw[:, k, :], W['ssm_conv_w'][l, k, :].rearrange("(c p) -> p c", p=128), writes=[cw],
                  allow_slow_non_contiguous=True)
        cb = P.sb(es, [128, 8], F32, 'scb')
        P.dma('sp', cb[:, :], W['ssm_conv_b'][l, :].rearrange("(c p) -> p c", p=128), writes=[cb],
              allow_slow_non_contiguous=True)
        dtb = bc_load(P, es, W['ssm_dt_bias'][l, :], 8, 'sdtb')
        aneg = bc_load(P, es, W['ssm_a_log'][l, :], 8, 'san')
        dsk = bc_load(P, es, W['ssm_d'][l, :], 8, 'sdk')
        ng = bc_load(P, es, W['ssm_norm_g'][l, :], 512, 'sng')
        P.op('act', lambda g: g.activation(out=aneg[:, :], in_=aneg[:, :], func=AF.Exp), [aneg], [aneg])
        P.op('dve', lambda g: g.tensor_scalar(out=aneg[:, :], in0=aneg[:, :], scalar1=-1.0, scalar2=None, op0=ALU.mult),
             [aneg], [aneg])
        xin = [P.sb(es, [128, 8, 131], F32, 'sxin') for _ in range(2)]
        acc = P.sb(es, [128, 8, 128], F32, 'sacc')
        xc = [P.sb(es, [128, 8, 128], F32, 'sxc') for _ in range(2)]
        xtm = [P.sb(es, [128, 512], F32, 'sxtm') for _ in range(2)]
        btm = [P.sb(es, [128, 256], F32, 'sbtm') for _ in range(2)]
        dtr = [P.sb(es, [128, 8], F32, 'sdtr') for _ in range(2)]
        dtv = [P.sb(es, [128, 8], F32, 'sdtv') for _ in range(2)]
        loga = [P.sb(es, [128, 8], F32, 'sla') for _ in range(2)]
        bsb = P.sb(es, [128, 8], F32, 'sbsb')
        R = P.sb(es, [128, 8, 128], F32, 'sR')
        L = P.sb(es, [128, 8, 128], F32, 'sL')
        Dr = P.sb(es, [128, 8, 128], F32, 'sDr')
        AT = P.sb(es, [128, 8, 128], F32, 'sAT')
        Ct = P.sb(es, [128, 8, 128], F32, 'sCt')
        V = P.sb(es, [128, 8, 64], F32, 'sV')
        Vh = P.sb(es, [128, 8, 64], F32, 'sVh')
        Sall = P.sb(es, [128, 8, 64], F32, 'sS')
        zt = [P.sb(es, [128, 512], F32, 'sz') for _ in range(2)]
        y = [P.sb(es, [128, 512], F32, 'sy') for _ in range(2)]
        sq = P.sb(es, [128, 512], F32, 'ssq')
        ms = P.sb(es, [128, 2], F32, 'sms')
        pT = P.ps(es, [128, 512], F32, 'spT')
        pT2 = P.ps(es, [128, 512], F32, 'spT2')
        pb = P.ps(es, [128, 512], F32, 'spb')
        pbr = [P.ps(es, [128, 512], F32, 'spbr') for _ in range(2)]
        pG = P.ps(es, [128, 512], F32, 'spG')
        pO = P.ps(es, [128, 512], F32, 'spO')
        pS = P.ps(es, [128, 512], F32, 'spS')
        P.op('pool', lambda g: g.memset(Sall[:, :, :], 0.0), [], [Sall])
        for it in range(S // 128):
            b = it % 2
            t0 = it * 128
            sl = slice(t0, t0 + 128)
            xi = xin[b]
            srcv = src['sxbc'].ap().rearrange("(c p) t -> p c t", p=128)
            if t0 == 0:
                P.op('pool', lambda g, xi=xi: g.memset(xi[:, :, 0:3], 0.0), [], [xi])
                P.dma('sp', xi[:, :, 3:131], srcv[:, :, 0:128], writes=[xi])
            else:
                P.dma('sp', xi[:, :, :], srcv[:, :, t0 - 3:t0 + 128], writes=[xi])
            P.dma('sp', dtr[b][:, :], src['sdt'][sl, :], writes=[dtr[b]])
            P.dma('sp', zt[b][:, :], src['sz'][sl, :], writes=[zt[b]])
            for c in range(8):
                e = 'dve'
                P.op(e, lambda g, c=c: g.tensor_scalar(out=acc[:, c, :], in0=xi[:, c, 0:128], scalar1=cw[:, 0, c:c + 1],
                                                       scalar2=None, op0=ALU.mult), [xi, cw], [acc])
                for k in range(1, 4):
                    P.op(e, lambda g, c=c, k=k: g.scalar_tensor_tensor(
                        out=acc[:, c, :], in0=xi[:, c, k:k + 128], scalar=cw[:, k, c:c + 1], in1=acc[:, c, :],
                        op0=ALU.mult, op1=ALU.add), [xi, cw, acc], [acc])
            X = xc[b]
            for c in range(8):
                P.op('act', lambda g, c=c, X=X: g.activation(out=X[:, c, :], in_=acc[:, c, :], func=AF.Silu,
                                                            bias=cb[:, c:c + 1], scale=1.0), [acc, cb], [X])
            for c in range(4):
                P.op('pe', lambda g, c=c, X=X: g.transpose(out=pT[:, c * 128:(c + 1) * 128], in_=X[:, c, :],
                                                          identity=C['ident']), [X, C.t], [pT], inc=(c == 3))
            P.op('act', lambda g, b=b: g.copy(out=xtm[b][:, :], in_=pT[:, :]), [pT], [xtm[b]])
            for c in range(2):
                P.op('pe', lambda g, c=c, X=X: g.transpose(out=pT2[:, c * 128:(c + 1) * 128], in_=X[:, 4 + c, :],
                                                          identity=C['ident']), [X, C.t], [pT2], inc=(c == 1))
            P.op('act', lambda g, b=b: g.copy(out=btm[b][:, :], in_=pT2[:, 0:256]), [pT2], [btm[b]])
            P.op('dve', lambda g, b=b: g.tensor_tensor(out=dtv[b][:, :], in0=dtr[b][:, :], in1=dtb[:, :], op=ALU.add),
                 [dtr[b], dtb], [dtv[b]])
            P.op('act', lambda g, b=b: g.activation(out=dtv[b][:, :], in_=dtv[b][:, :], func=AF.Exp), [dtv[b]], [dtv[b]])
            P.op('act', lambda g, b=b: g.activation(out=dtv[b][:, :], in_=dtv[b][:, :], func=AF.Ln, bias=1.0, scale=1.0),
                 [dtv[b]], [dtv[b]])
            P.op('dve', lambda g, b=b: g.tensor_tensor(out=loga[b][:, :], in0=dtv[b][:, :], in1=aneg[:, :], op=ALU.mult),
                 [dtv[b], aneg], [loga[b]])
            P.op('pe', lambda g, b=b: g.matmul(pb[:, 0:8], lhsT=C['U'], rhs=loga[b][:, :], start=True, stop=True),
                 [C.t, loga[b]], [pb])
            P.op('act', lambda g: g.copy(out=bsb[:, :], in_=pb[:, 0:8]), [pb], [bsb])
            P.op('dve', lambda g, b=b: g.tensor_tensor(
                out=R[:, :, :], in0=C['U'].unsqueeze(1).broadcast_to([128, 8, 128]),
                in1=loga[b][:, :].unsqueeze(2).broadcast_to([128, 8, 128]), op=ALU.mult), [C.t, loga[b]], [R])
            for k in range(2):
                P.op('pe', lambda g, k=k: g.matmul(pbr[k][:, :], lhsT=C['ones'],
                                                   rhs=R[:, 4 * k:4 * k + 4, :].rearrange("p h i -> p (h i)"),
                                                   start=True, stop=True), [C.t, R], [pbr[k]])
            for k in range(2):
                P.op('dve', lambda g, k=k: g.tensor_tensor(
                    out=L[:, 4 * k:4 * k + 4, :], in0=pbr[k][:, :].rearrange("p (h i) -> p h i", h=4),
                    in1=bsb[:, 4 * k:4 * k + 4].unsqueeze(2).broadcast_to([128, 4, 128]), op=ALU.subtract),
                    [pbr[k], bsb], [L])
                P.op('act', lambda g, k=k: g.activation(out=Dr[:, 4 * k:4 * k + 4, :].rearrange("p h i -> p (h i)"),
                                                        in_=pbr[k][:, :], func=AF.Exp), [pbr[k]], [Dr])
            P.op('pool', lambda g: g.tensor_tensor(out=L[:, :, :], in0=L[:, :, :],
                                                   in1=C['negmask'].unsqueeze(1).broadcast_to([128, 8, 128]), op=ALU.add),
                 [L, C.t], [L])
            P.op('act', lambda g: g.activation(out=L[:, :, :], in_=L[:, :, :], func=AF.Exp), [L], [L])
            for g_ in range(2):
                P.op('pe', lambda g, g_=g_, X=X: g.matmul(pG[:, g_ * 128:(g_ + 1) * 128], lhsT=X[:, 4 + g_, :],
                                                          rhs=X[:, 6 + g_, :], start=True, stop=True), [X], [pG], inc=(g_ == 1))
            P.op('dve', lambda g: g.tensor_tensor(
                out=AT[:, :, :].rearrange("p (g h) i -> p g h i", g=2), in0=L[:, :, :].rearrange("p (g h) i -> p g h i", g=2),
                in1=pG[:, 0:256].rearrange("p (g i) -> p g i", g=2).unsqueeze(2).broadcast_to([128, 2, 4, 128]),
                op=ALU.mult), [L, pG], [AT])
            P.op('pool', lambda g, X=X: g.tensor_tensor(
                out=Ct[:, :, :].rearrange("p (g h) i -> p g h i", g=2), in0=Dr[:, :, :].rearrange("p (g h) i -> p g h i", g=2),
                in1=X[:, 6:8, :].unsqueeze(2).broadcast_to([128, 2, 4, 128]), op=ALU.mult), [Dr, X], [Ct])
            P.op('dve', lambda g, b=b: g.tensor_tensor(
                out=V[:, :, :], in0=xtm[b][:, :].rearrange("p (h d) -> p h d", h=8),
                in1=dtv[b][:, :].unsqueeze(2).broadcast_to([128, 8, 64]), op=ALU.mult), [xtm[b], dtv[b]], [V])
            P.op('pool', lambda g: g.tensor_tensor(out=Vh[:, :, :], in0=V[:, :, :],
                                                   in1=L[:, :, 127:128].broadcast_to([128, 8, 64]), op=ALU.mult),
                 [V, L], [Vh])
            for h in range(8):
                hs = slice(h * 64, (h + 1) * 64)
                P.op('pe', lambda g, h=h, hs=hs: g.matmul(pO[:, hs], lhsT=AT[:, h, :], rhs=V[:, h, :], start=True, stop=False),
                     [AT, V], [pO], inc=False)
                P.op('pe', lambda g, h=h, hs=hs: g.matmul(pO[:, hs], lhsT=Ct[:, h, :], rhs=Sall[:, h, :], start=False, stop=True),
                     [Ct, Sall], [pO], inc=(h == 7))
            for h in range(8):
                hs = slice(h * 64, (h + 1) * 64)
                P.op('pe', lambda g, h=h, hs=hs, b=b: g.matmul(pS[:, hs], lhsT=btm[b][:, (h // 4) * 128:(h // 4 + 1) * 128],
                                                              rhs=Vh[:, h, :], start=True, stop=True),
                     [btm[b], Vh], [pS], inc=(h == 7))
            P.op('dve', lambda g: g.tensor_tensor(out=Sall[:, :, :], in0=Sall[:, :, :],
                                                  in1=Dr[:, :, 127:128].broadcast_to([128, 8, 64]), op=ALU.mult),
                 [Sall, Dr], [Sall])
            P.op('dve', lambda g: g.tensor_tensor(out=Sall[:, :, :].rearrange("p h d -> p (h d)"),
                                                  in0=Sall[:, :, :].rearrange("p h d -> p (h d)"), in1=pS[:, :], op=ALU.add),
                 [Sall, pS], [Sall])
            Y = y[b]
            P.op('pool', lambda g, b=b, Y=Y: g.tensor_tensor(
                out=Y[:, :].rearrange("p (h d) -> p h d", h=8), in0=xtm[b][:, :].rearrange("p (h d) -> p h d", h=8),
                in1=dsk[:, :].unsqueeze(2).broadcast_to([128, 8, 64]), op=ALU.mult), [xtm[b], dsk], [Y])
            P.op('dve', lambda g, Y=Y: g.tensor_tensor(out=Y[:, :], in0=Y[:, :], in1=pO[:, :], op=ALU.add), [Y, pO], [Y])
            P.op('act', lambda g, b=b: g.activation(out=zt[b][:, :], in_=zt[b][:, :], func=AF.Silu), [zt[b]], [zt[b]])
            P.op('dve', lambda g, b=b, Y=Y: g.tensor_tensor(out=Y[:, :], in0=Y[:, :], in1=zt[b][:, :], op=ALU.mult),
                 [Y, zt[b]], [Y])
            group_rms(P, Y, sq, ms, 2, 256, 1e-6)
            P.op('pool', lambda g, Y=Y: g.tensor_tensor(out=Y[:, :], in0=Y[:, :], in1=ng[:, :], op=ALU.mult), [Y, ng], [Y])
            P.dma('act', mixed[sl, 1024:1536], Y[:, :], reads=[Y])
        P.barrier()


def group_rms(P, Y, sq, ms, ng, gw, eps):
    n = ng * gw
    P.op('pool', lambda g: g.tensor_tensor(out=sq[:, :n], in0=Y[:, :n], in1=Y[:, :n], op=ALU.mult), [Y], [sq])
    P.op('dve', lambda g: g.tensor_reduce(out=ms[:, :ng], in_=sq[:, :n].rearrange("p (g d) -> p g d", g=ng), axis=AX.X,
                                          op=ALU.add), [sq], [ms])
    P.op('act', lambda g: g.activation(out=ms[:, :ng], in_=ms[:, :ng], func=AF.Sqrt, bias=eps, scale=1.0 / gw), [ms], [ms])
    P.op('dve', lambda g: g.reciprocal(out=ms[:, :ng], in_=ms[:, :ng]), [ms], [ms])
    P.op('dve', lambda g: g.tensor_tensor(out=Y[:, :n].rearrange("p (g d) -> p g d", g=ng),
                                          in0=Y[:, :n].rearrange("p (g d) -> p g d", g=ng),
                                          in1=ms[:, :ng].unsqueeze(2).broadcast_to([128, ng, gw]), op=ALU.mult), [Y, ms], [Y])


def phase_hgrn(P, S, src, mixed, C, W, l, depth):
    H = 4
    with ExitStack() as es:
        lg = P.sb(es, [128, depth, H], F32, 'hlg')
        for m in range(depth):
            P.dma('sp', lg[:, m, :], W['hgrn_lb_logits'][m, :].rearrange("(h p) -> p h", p=128), writes=[lg],
                  allow_slow_non_contiguous=True)
        lb = P.sb(es, [128, H], F32, 'hlb')
        oml = P.sb(es, [128, H], F32, 'homl')
        ssum = P.sb(es, [128, H], F32, 'hss')
        P.op('act', lambda g: g.activation(out=lg[:, :, :], in_=lg[:, :, :], func=AF.Exp), [lg], [lg])
        P.op('pool', lambda g: g.memset(lb[:, :], 0.0), [], [lb])
        P.op('pool', lambda g: g.memset(ssum[:, :], 0.0), [], [ssum])
        for m in range(depth):
            P.op('dve', lambda g, m=m: g.tensor_tensor(out=ssum[:, :], in0=ssum[:, :], in1=lg[:, m, :], op=ALU.add),
                 [ssum, lg], [ssum])
            if 1 <= m <= l:
                P.op('dve', lambda g, m=m: g.tensor_tensor(out=lb[:, :], in0=lb[:, :], in1=lg[:, m, :], op=ALU.add),
                     [lb, lg], [lb])
        P.op('dve', lambda g: g.reciprocal(out=ssum[:, :], in_=ssum[:, :]), [ssum], [ssum])
        P.op('dve', lambda g: g.tensor_tensor(out=lb[:, :], in0=lb[:, :], in1=ssum[:, :], op=ALU.mult), [lb, ssum], [lb])
        P.op('dve', lambda g: g.tensor_scalar(out=oml[:, :], in0=lb[:, :], scalar1=-1.0, scalar2=1.0, op0=ALU.mult,
                                              op1=ALU.add), [lb], [oml])
        ng = bc_load(P, es, W['hgrn_norm_g'][l, :], 512, 'hng')
        qf = [P.sb(es, [128, H, 128], F32, 'hq') for _ in range(2)]
        zf = [P.sb(es, [128, H, 128], F32, 'hz') for _ in range(2)]
        vt = [P.sb(es, [128, 512], F32, 'hv') for _ in range(2)]
        gt = [P.sb(es, [128, 512], F32, 'hg') for _ in range(2)]
        sig = P.sb(es, [128, H, 128], F32, 'hsig')
        logf = P.sb(es, [128, H, 128], F32, 'hlf')
        kf = P.sb(es, [128, H, 128], F32, 'hkf')
        bT = P.sb(es, [128, H, 128], F32, 'hbT')
        EB = P.sb(es, [128, H, 128], F32, 'hEB')
        EBn = P.sb(es, [128, H, 128], F32, 'hEBn')
        qtil = P.sb(es, [128, H, 128], F32, 'hqt')
        ktil = P.sb(es, [128, H, 128], F32, 'hkt')
        lftm = P.sb(es, [128, 512], F32, 'hlftm')
        ed = P.sb(es, [128, 512], F32, 'hed')
        khat = P.sb(es, [128, 512], F32, 'hkh')
        khc = [P.sb(es, [128, 512], F32, 'hkhc') for _ in range(4)]
        qz = [P.sb(es, [128, H, 128], F32, 'hqz') for _ in range(4)]
        AT = P.sb(es, [128, H, 128], F32, 'hAT')
        Sr = [P.sb(es, [128, H, 128], F32, 'hS') for _ in range(5)]
        o1 = P.sb(es, [128, 512], F32, 'ho1')
        o = [P.sb(es, [128, 512], F32, 'ho') for _ in range(2)]
        sq = P.sb(es, [128, 512], F32, 'hsq')
        ms = P.sb(es, [128, 4], F32, 'hms')
        pTl = P.ps(es, [128, 512], F32, 'hpTl')
        pTk = P.ps(es, [128, 512], F32, 'hpTk')
        pD = P.ps(es, [128, 512], F32, 'hpD')
        pA = P.ps(es, [128, 512], F32, 'hpA')
        pO1 = P.ps(es, [128, 512], F32, 'hpO1')
        pO2 = P.ps(es, [128, 512], F32, 'hpO2')
        pKV = [P.ps(es, [128, 512], F32, 'hpKV') for _ in range(2)]
        for c in range(4):
            P.op('pool', lambda g, c=c: g.memset(qz[c][:, :, :], 0.0), [], [qz[c]])
        P.op('pool', lambda g: g.memset(Sr[0][:, :, :], 0.0), [], [Sr[0]])
        isr = 0
        ikv = 0
        for it in range(S // 128):
            b = it % 2
            t0 = it * 128
            sl = slice(t0, t0 + 128)
            P.dma('sp', qf[b][:, :, :], src['hq'].ap().rearrange("(h p) t -> p h t", p=128)[:, :, sl], writes=[qf[b]])
            P.dma('sp', zf[b][:, :, :], src['hfF'].ap().rearrange("(h p) t -> p h t", p=128)[:, :, sl], writes=[zf[b]])
            P.dma('sp', vt[b][:, :], src['hi'][sl, :], writes=[vt[b]])
            P.dma('sp', gt[b][:, :], src['hg'][sl, :], writes=[gt[b]])
            Z, Q = zf[b], qf[b]
            lbb = lb[:, :].unsqueeze(2).broadcast_to([128, H, 128])
            omb = oml[:, :].unsqueeze(2).broadcast_to([128, H, 128])
            P.op('act', lambda g, Z=Z: g.activation(out=sig[:, :, :], in_=Z[:, :, :], func=AF.Sigmoid), [Z], [sig])
            P.op('dve', lambda g: g.tensor_tensor(out=sig[:, :, :], in0=sig[:, :, :], in1=omb, op=ALU.mult), [sig, oml], [sig])
            P.op('dve', lambda g: g.tensor_tensor(out=sig[:, :, :], in0=sig[:, :, :], in1=lbb, op=ALU.add), [sig, lb], [sig])
            P.op('act', lambda g: g.activation(out=logf[:, :, :], in_=sig[:, :, :], func=AF.Ln), [sig], [logf])
            P.op('act', lambda g, Z=Z: g.activation(out=kf[:, :, :], in_=Z[:, :, :], func=AF.Sigmoid, scale=-1.0), [Z], [kf])
            P.op('pool', lambda g: g.tensor_tensor(out=kf[:, :, :], in0=kf[:, :, :], in1=omb, op=ALU.mult), [kf, oml], [kf])
            P.op('dve', lambda g: g.tensor_tensor_scan(out=bT[:, :, :].rearrange("p h t -> p (h t)"), data0=C['notstart'],
                                                       data1=logf[:, :, :].rearrange("p h t -> p (h t)"), initial=0.0,
                                                       op0=ALU.mult, op1=ALU.add), [logf, C.t], [bT])
            P.op('act', lambda g: g.activation(out=EB[:, :, :], in_=bT[:, :, :], func=AF.Exp), [bT], [EB])
            P.op('act', lambda g: g.activation(out=EBn[:, :, :], in_=bT[:, :, :], func=AF.Exp, scale=-1.0), [bT], [EBn])
            P.op('act', lambda g, Q=Q: g.activation(out=Q[:, :, :], in_=Q[:, :, :], func=AF.Silu), [Q], [Q])
            P.op('dve', lambda g, Q=Q: g.tensor_tensor(out=qtil[:, :, :], in0=Q[:, :, :], in1=EB[:, :, :], op=ALU.mult),
                 [Q, EB], [qtil])
            P.op('pool', lambda g: g.tensor_tensor(out=ktil[:, :, :], in0=kf[:, :, :], in1=EBn[:, :, :], op=ALU.mult),
                 [kf, EBn], [ktil])
            for c in range(4):
                cs = slice(32 * c, 32 * c + 32)
                P.op('pool', lambda g, c=c, cs=cs: g.tensor_copy(out=qz[c][:, :, cs], in_=qtil[:, :, cs]), [qtil], [qz[c]])
            for h in range(H):
                P.op('pe', lambda g, h=h: g.transpose(out=pTl[:, h * 128:(h + 1) * 128], in_=logf[:, h, :], identity=C['ident']),
                     [logf, C.t], [pTl], inc=(h == H - 1))
            for h in range(H):
                P.op('pe', lambda g, h=h: g.transpose(out=pTk[:, h * 128:(h + 1) * 128], in_=kf[:, h, :], identity=C['ident']),
                     [kf, C.t], [pTk], inc=(h == H - 1))
            P.op('act', lambda g: g.copy(out=lftm[:, :], in_=pTl[:, :]), [pTl], [lftm])
            P.op('pe', lambda g: g.matmul(pD[:, :], lhsT=C['V32'], rhs=lftm[:, :], start=True, stop=True), [C.t, lftm], [pD])
            P.op('act', lambda g: g.activation(out=ed[:, :], in_=pD[:, :], func=AF.Exp), [pD], [ed])
            P.op('dve', lambda g: g.tensor_tensor(out=khat[:, :], in0=ed[:, :], in1=pTk[:, :], op=ALU.mult), [ed, pTk], [khat])
            for c in range(4):
                P.op('pool', lambda g, c=c: g.tensor_scalar(out=khc[c][:, :], in0=khat[:, :], scalar1=C.sl('bm', c, c + 1),
                                                            scalar2=None, op0=ALU.mult), [khat, C.t], [khc[c]])
            for h in range(H):
                P.op('pe', lambda g, h=h: g.matmul(pA[:, h * 128:(h + 1) * 128], lhsT=ktil[:, h, :], rhs=qtil[:, h, :],
                                                   start=True, stop=True), [ktil, qtil], [pA], inc=(h == H - 1))
            P.op('dve', lambda g: g.tensor_tensor(out=AT[:, :, :], in0=C['M32'].unsqueeze(1).broadcast_to([128, H, 128]),
                                                  in1=pA[:, :].rearrange("p (h i) -> p h i", h=H), op=ALU.mult),
                 [C.t, pA], [AT])
            for h in range(H):
                hs = slice(h * 128, (h + 1) * 128)
                P.op('pe', lambda g, h=h, hs=hs, b=b: g.matmul(pO1[:, hs], lhsT=AT[:, h, :], rhs=vt[b][:, hs], start=True,
                                                              stop=True), [AT, vt[b]], [pO1], inc=(h == H - 1))
            P.op('act', lambda g: g.copy(out=o1[:, :], in_=pO1[:, :]), [pO1], [o1])
            Ss = [Sr[(isr + c) % 5] for c in range(5)]
            isr += 4
            for c in range(4):
                Sc, Sn = Ss[c], Ss[c + 1]
                kv = pKV[ikv % 2]
                ikv += 1
                for h in range(H):
                    hs = slice(h * 128, (h + 1) * 128)
                    P.op('pe', lambda g, h=h, hs=hs, c=c, kv=kv, b=b: g.matmul(kv[:, hs], lhsT=khc[c][:, hs], rhs=vt[b][:, hs],
                                                                             start=True, stop=True),
                         [khc[c], vt[b]], [kv], inc=(h == H - 1))
                dec = EB[:, :, 32 * c + 31:32 * c + 32].broadcast_to([128, H, 128])
                P.op('dve', lambda g, Sc=Sc, Sn=Sn, dec=dec: g.tensor_tensor(out=Sn[:, :, :], in0=Sc[:, :, :], in1=dec,
                                                                             op=ALU.mult), [Sc, EB], [Sn])
                P.op('dve', lambda g, Sn=Sn, kv=kv: g.tensor_tensor(out=Sn[:, :, :].rearrange("p h v -> p (h v)"),
                                                                    in0=Sn[:, :, :].rearrange("p h v -> p (h v)"),
                                                                    in1=kv[:, :], op=ALU.add), [Sn, kv], [Sn])
            for h in range(H):
                hs = slice(h * 128, (h + 1) * 128)
                for c in range(4):
                    P.op('pe', lambda g, h=h, hs=hs, c=c: g.matmul(pO2[:, hs], lhsT=qz[c][:, h, :], rhs=Ss[c][:, h, :],
                                                                 start=(c == 0), stop=(c == 3)),
                         [qz[c], Ss[c]], [pO2], inc=(h == H - 1 and c == 3))
            O = o[b]
            P.op('dve', lambda g, O=O: g.tensor_tensor(out=O[:, :], in0=o1[:, :], in1=pO2[:, :], op=ALU.add), [o1, pO2], [O])
            group_rms(P, O, sq, ms, 4, 128, 1e-6)
            P.op('pool', lambda g, O=O: g.tensor_tensor(out=O[:, :], in0=O[:, :], in1=ng[:, :], op=ALU.mult), [O, ng], [O])
            P.op('act', lambda g, b=b: g.activation(out=gt[b][:, :], in_=gt[b][:, :], func=AF.Silu), [gt[b]], [gt[b]])
            P.op('pool', lambda g, O=O, b=b: g.tensor_tensor(out=O[:, :], in0=O[:, :], in1=gt[b][:, :], op=ALU.mult),
                 [O, gt[b]], [O])
            P.dma('act', mixed[sl, 0:512], O[:, :], reads=[O])
        P.barrier()


NB = 8192
NB0 = 2560
NSA_TOP = 16


def _t5_bucket_np(n):
    n = np.maximum(n, 0)
    nf = np.maximum(n, 1).astype(np.float32)
    large = 16 + (np.log(nf / np.float32(16)) / np.float32(np.log(2048 / 16)) * np.float32(16)).astype(np.int32)
    return np.where(n < 16, n, np.minimum(large, 31))


def nsa_host_consts(S):
    dist = np.arange(NB) - NB0
    bk = _t5_bucket_np(dist)
    oh = np.zeros((33, NB), np.float32)
    oh[bk, np.arange(NB)] = 1.0
    oh[:, dist < 0] = 0.0
    oh[32, dist < 0] = 1.0
    ohw = oh.copy()
    ohw[:, dist >= 512] = 0.0
    ohw[32, dist >= 512] = 1.0
    nbp = S // 16
    nslc = S // 64
    wt = np.zeros((nbp, nslc), np.float32)
    for j in range(nslc):
        for m, w in ((4 * j - 1, 0.5), (4 * j, 1.0), (4 * j + 1, 1.0), (4 * j + 2, 1.0), (4 * j + 3, 0.5)):
            if 0 <= m < nbp - 1:
                wt[m, j] = w
    return {'c_oh': oh, 'c_ohw': ohw, 'c_wt': wt}


def phase_nsa_tables(P, rel_bias, cin, fh, fhw):
    with ExitStack() as es:
        tb = P.sb(es, [33, 8], F32, 'ntb')
        r31 = P.sb(es, [32, 8], F32, 'nr31')
        oh = P.sb(es, [33, NB], F32, 'noh')
        ob = [P.sb(es, [8, 512], F32, 'nob') for _ in range(2)]
        pp = [P.ps(es, [128, 512], F32, 'npp') for _ in range(2)]
        P.op('pool', lambda g: g.memset(tb[:, :], -30000.0), [], [tb])
        P.dma('sp', tb[0:32, :], rel_bias[:, :], writes=[tb])
        P.dma('sp', r31[:, :], rel_bias[31, :].partition_broadcast(32), writes=[r31])
        P.op('dve', lambda g: g.tensor_tensor(out=tb[0:32, :], in0=tb[0:32, :], in1=r31[:, :], op=ALU.subtract),
             [tb, r31], [tb])
        i = 0
        for (src, dst) in ((cin['oh'], fh), (cin['ohw'], fhw)):
            for c0 in range(0, NB, 2048):
                P.dma('sp', oh[:, c0:c0 + 2048], src[:, c0:c0 + 2048], writes=[oh])
            for n0 in range(0, NB, 512):
                p_, o_ = pp[i % 2], ob[i % 2]
                i += 1
                P.op('pe', lambda g, p_=p_, n0=n0: g.matmul(p_[0:8, :], lhsT=tb[:, :], rhs=oh[:, n0:n0 + 512], start=True,
                                                            stop=True), [tb, oh], [p_])
                P.op('act', lambda g, p_=p_, o_=o_: g.copy(out=o_[:, :], in_=p_[0:8, :]), [p_], [o_])
                P.dma('act', dst[:, n0:n0 + 512], o_[:, :], reads=[o_])
        P.barrier()


def phase_nsa_compress(P, S, src, W, l, kcT, vc):
    NBP = S // 16
    nblk = NBP - 1
    for kv in range(2):
        with ExitStack() as es:
            nm = 'k' if kv == 0 else 'v'
            srcF = src['nkc'] if kv == 0 else src['nvc']
            w1s = P.sb(es, [64, 32, 256], F32, 'cw1s')
            w1b = P.sb(es, [64, 32, 256], BF16, 'cw1b')
            w2s = P.sb(es, [128, 2, 64], F32, 'cw2s')
            w2b = P.sb(es, [128, 2, 64], BF16, 'cw2b')
            peT = P.sb(es, [64, 32], F32, 'cpe')
            P.dma('sp', w1s[:, :, :], W['nsa_w1_' + nm][l, :, :].rearrange("(t d) h -> d t h", d=64), writes=[w1s])
            P.dma('sp', w2s[:, :, :], W['nsa_w2_' + nm][l, :, :].rearrange("(c p) d -> p c d", p=128), writes=[w2s])
            P.dma('sp', peT[:, :], W['nsa_pe_' + nm][l, :, :].rearrange("t d -> d t"), writes=[peT],
                  allow_slow_non_contiguous=True)
            for t8 in range(4):
                cast_rr(P, t8, w1b[:, t8 * 8:(t8 + 1) * 8, :], w1s[:, t8 * 8:(t8 + 1) * 8, :], [w1s], [w1b])
            P.op('dve', lambda g: g.tensor_copy(out=w2b[:, :, :], in_=w2s[:, :, :]), [w2s], [w2b])
            stg = [P.sb(es, [64, 2048], F32, 'cstg') for _ in range(2)]
            kA = P.sb(es, [64, S], BF16, 'ckA')
            kB = P.sb(es, [64, S], BF16, 'ckB')
            hT = [P.sb(es, [128, 2, 512], BF16, 'chT') for _ in range(2)]
            ko = [P.sb(es, [64, 512], F32, 'cko') for _ in range(2)]
            vo = [P.sb(es, [128, 64], F32, 'cvo') for _ in range(2)]
            zz = P.sb(es, [128, 64], F32, 'czz')
            ph = [P.ps(es, [128, 512], F32, 'cph') for _ in range(4)]
            pk = [P.ps(es, [128, 512], F32, 'cpk') for _ in range(2)]
            P.op('pool', lambda g: g.memset(zz[:, :], 0.0), [], [zz])
            iv = 0
            for g_ in range(2):
                SEG = min(2048, S)
                for i0 in range(0, S, SEG):
                    st = stg[(i0 // SEG) % 2]
                    P.dma('sp', st[:, :SEG], srcF[64 * g_:64 * g_ + 64, i0:i0 + SEG], writes=[st])
                    for (dst, po, e) in ((kA, 0, 'dve'), (kB, 16, 'pool')):
                        P.op(e, lambda g, dst=dst, po=po, st=st, i0=i0: g.tensor_tensor(
                            out=dst[:, i0:i0 + SEG].rearrange("d (m r) -> d m r", r=16),
                            in0=st[:, :SEG].rearrange("d (m r) -> d m r", r=16),
                            in1=peT[:, po:po + 16].unsqueeze(1).broadcast_to([64, SEG // 16, 16]), op=ALU.add),
                            [st, peT], [dst])
                kA3 = kA[:, :].rearrange("d (m r) -> d m r", r=16)
                kB3 = kB[:, :].rearrange("d (m r) -> d m r", r=16)
                for ib, b0 in enumerate(range(0, nblk, 512)):
                    N = min(512, nblk - b0)
                    H_ = hT[ib % 2]
                    for hc in range(2):
                        p_ = ph[(2 * ib + hc) % 4]
                        for t in range(32):
                            rhs = kA3[:, b0:b0 + N, t] if t < 16 else kB3[:, b0 + 1:b0 + 1 + N, t - 16]
                            P.op('pe', lambda g, p_=p_, t=t, hc=hc, rhs=rhs, N=N: g.matmul(
                                p_[:, :N], lhsT=w1b[:, t, hc * 128:(hc + 1) * 128], rhs=rhs, start=(t == 0), stop=(t == 31)),
                                [w1b, kA, kB], [p_], inc=(t == 31))
                        P.op('act', lambda g, p_=p_, hc=hc, H_=H_, N=N: g.activation(out=H_[:, hc, :N], in_=p_[:, :N],
                                                                                    func=AF.Silu), [p_], [H_])
                    if kv == 0:
                        q_ = pk[ib % 2]
                        for hc in range(2):
                            P.op('pe', lambda g, q_=q_, hc=hc, H_=H_, N=N: g.matmul(q_[0:64, :N], lhsT=w2b[:, hc, :],
                                                                                 rhs=H_[:, hc, :N], start=(hc == 0),
                                                                                 stop=(hc == 1)), [w2b, H_], [q_], inc=(hc == 1))
                        o_ = ko[ib % 2]
                        P.op('dve', lambda g, q_=q_, o_=o_, N=N: g.tensor_copy(out=o_[:, :N], in_=q_[0:64, :N]), [q_], [o_])
                        P.dma('act', kcT[64 * g_:64 * g_ + 64, b0:b0 + N], o_[:, :N], reads=[o_])
                    else:
                        for s0 in range(0, N, 128):
                            n = min(128, N - s0)
                            q_ = pk[iv % 2]
                            o_ = vo[iv % 2]
                            iv += 1
                            for hc in range(2):
                                P.op('pe', lambda g, q_=q_, hc=hc, H_=H_, s0=s0, n=n: g.matmul(
                                    q_[:n, 0:64], lhsT=H_[:, hc, s0:s0 + n], rhs=w2b[:, hc, :], start=(hc == 0), stop=(hc == 1)),
                                    [w2b, H_], [q_], inc=(hc == 1))
                            P.op('dve', lambda g, q_=q_, o_=o_, n=n: g.tensor_copy(out=o_[:n, :], in_=q_[:n, 0:64]), [q_], [o_])
                            P.dma('act', vc[b0 + s0:b0 + s0 + n, 64 * g_:64 * g_ + 64], o_[:n, :], reads=[o_])
                if kv == 0:
                    P.dma('act', kcT[64 * g_:64 * g_ + 64, nblk:NBP], zz[0:64, 0:1], reads=[zz], allow_slow_non_contiguous=True)
                else:
                    P.dma('act', vc[nblk:NBP, 64 * g_:64 * g_ + 64], zz[0:1, :], reads=[zz])
            P.barrier()


def phase_nsa_attn(P, S, src, W, l, C, cin, kcT_d, vc_d, fh, fhw, mixed):
    NQ = S // 128
    NBP = S // 16
    NCC = max(1, NBP // 128)
    NSLC = S // 64
    ntop = min(NSA_TOP, NSLC)
    ND = 14
    with ExitStack() as es:
        ng = bc_load(P, es, W['nsa_norm_g'][l, :], 512, 'ang')
        Jb = P.sb(es, [128, 128], BF16, 'aJb')
        idb = P.sb(es, [128, 128], BF16, 'aidb')
        P.op('dve', lambda g: g.tensor_copy(out=Jb[:, :], in_=C['J']), [C.t], [Jb])
        P.op('dve', lambda g: g.tensor_copy(out=idb[:, :], in_=C['ident']), [C.t], [idb])
        I4 = P.sb(es, [128, 512], BF16, 'aI4')
        for q4 in range(4):
            P.op('dve', lambda g, q4=q4: g.tensor_copy(out=I4[:, q4 * 128:(q4 + 1) * 128], in_=C['ident']), [C.t], [I4])
        ksT = P.sb(es, [128, S], BF16, 'aks')
        vs1 = P.sb(es, [128, NQ, 2, 65], BF16, 'avs')
        kcT = P.sb(es, [128, NBP], F32, 'akc')
        vc1 = P.sb(es, [128, NCC, 2, 65], F32, 'avc')
        wt = P.sb(es, [128, NCC, NSLC], F32, 'awt')
        stg = [P.sb(es, [128, 2048], F32, 'astg') for _ in range(2)]
        P.op('pool', lambda g: g.memset(vs1[:, :, :, :], 1.0), [], [vs1])
        P.op('pool', lambda g: g.memset(vc1[:, :, :, :], 1.0), [], [vc1])
        SEG = min(2048, S)
        for i, i0 in enumerate(range(0, S, SEG)):
            st = stg[i % 2]
            P.dma('sp', st[:, :SEG], src['nks'][:, i0:i0 + SEG], writes=[st])
            cast_rr(P, i, ksT[:, i0:i0 + SEG], st[:, :SEG], [st], [ksT])
        for i, i0 in enumerate(range(0, S, 2048)):
            n = min(2048, S - i0)
            st = stg[i % 2]
            P.dma('sp', st[:, :n // 128 * 128].rearrange("p (c f) -> p c f", f=128),
                  src['nvs'].ap().rearrange("(c p) f -> p c f", p=128)[:, i0 // 128:(i0 + n) // 128, :], writes=[st])
            P.op('dve', lambda g, st=st, i0=i0, n=n: g.tensor_copy(
                out=vs1[:, i0 // 128:(i0 + n) // 128, :, 0:64],
                in_=st[:, :n].rearrange("p (c g d) -> p c g d", g=2, d=64)), [st], [vs1])
        P.dma('sp', kcT[:, :], kcT_d[:, :], writes=[kcT])
        if NBP >= 128:
            for g_ in range(2):
                P.dma('sp', vc1[:, :, g_, 0:64], vc_d.ap()[:, 64 * g_:64 * g_ + 64].rearrange("(c p) d -> p c d", p=128),
                      writes=[vc1])
            P.dma('sp', wt[:, :, :], cin['wt'].ap().rearrange("(c p) j -> p c j", p=128), writes=[wt])
        else:
            P.op('pool', lambda g: g.memset(wt[:, :, :], 0.0), [], [wt])
            P.op('pool', lambda g: g.memset(kcT[:, :], 0.0), [], [kcT])
            P.dma('sp', kcT[:, :NBP], kcT_d[:, :], writes=[kcT])
            P.dma('sp', vc1[:NBP, 0, :, 0:64], vc_d.ap().rearrange("p (g d) -> p g d", g=2), writes=[vc1])
            P.dma('sp', wt[:NBP, 0, :], cin['wt'][:, :], writes=[wt])
        hks = [[P.sb(es, [128, 4, 128], BF16, 'ahk') for _ in range(2)] for _ in range(ND)]
        hkw = [[P.sb(es, [128, 4, 128], BF16, 'ahkw') for _ in range(2)] for _ in range(5)]
        hst = [P.sb(es, [128, 4, 128], F32, 'ahst') for _ in range(2)]
        i = 0
        for (tiles, tab) in ((hks, fh), (hkw, fhw)):
            for dl, pair in enumerate(tiles):
                for g_ in range(2):
                    h_ = hst[i % 2]
                    for hh in range(4):
                        off = (4 * g_ + hh) * NB + NB0 + 128 * dl - 127
                        P.dma('sp', h_[:, hh, :], bass.AP(tensor=tab, offset=off, ap=[[1, 128], [1, 128]]), writes=[h_])
                    cast_rr(P, i, pair[g_][:, :, :], h_[:, :, :], [h_], [pair[g_]])
                    i += 1
        q32 = [P.sb(es, [128, 4, 128], F32, 'aq32') for _ in range(2)]
        qb = [P.sb(es, [128, 4, 128], BF16, 'aqb') for _ in range(2)]
        gsb = [P.sb(es, [128, 24], F32, 'ags') for _ in range(2)]
        kws = [P.sb(es, [128, 128], F32, 'akws') for _ in range(2)]
        vws = [P.sb(es, [128, 128], F32, 'avws') for _ in range(2)]
        kw = [P.sb(es, [128, 128], BF16, 'akw') for _ in range(6)]
        vw = [P.sb(es, [128, 2, 65], BF16, 'avw') for _ in range(6)]
        for t in vw:
            P.op('pool', lambda g, t=t: g.memset(t[:, :, :], 1.0), [], [t])
        PC = [P.sb(es, [128, 512], F32, 'aPC') for _ in range(min(NCC, 8))]
        hk32 = [P.sb(es, [128, 4, 128], F32, 'ahk32') for _ in range(2)]
        PT = [P.sb(es, [128, 512], BF16, 'aPT') for _ in range(4)]
        Mx = [P.sb(es, [128, 128], BF16, 'aMx') for _ in range(4)]
        Mb = [P.sb(es, [128, NSLC], BF16, 'aMb') for _ in range(2)]
        rd = P.sb(es, [128, 4], F32, 'ard')
        cf = P.sb(es, [128, 4], F32, 'acf')
        sc = P.sb(es, [128, NSLC], F32, 'asc')
        sc2 = P.sb(es, [128, NSLC], F32, 'asc2')
        m8 = P.sb(es, [128, 8], F32, 'am8')
        mk = [P.sb(es, [128, NSLC], BF16, 'amk') for _ in range(2)]
        oall = [P.sb(es, [128, 8, 64], F32, 'aoa') for _ in range(2)]
        sq = P.sb(es, [128, 512], F32, 'asq')
        ms = P.sb(es, [128, 2], F32, 'ams')
        pS = [P.ps(es, [128, 512], F32, 'apS') for _ in range(4)]
        pO = [P.ps(es, [128, 512], F32, 'apO') for _ in range(4)]
        isc = 0
        ihk = 0
        for T in range(NQ):
            if max(P.cnt.values()) > 30000 or max(d['n'] for d in P.dq.values()) > 15000:
                P.barrier()
            b = T % 2
            t0 = T * 128
            sl = slice(t0, t0 + 128)
            for g_ in range(2):
                P.dma('sp', q32[b][64 * g_:64 * g_ + 64, :, :],
                      src['nq'].ap()[256 * g_:256 * g_ + 256, sl].rearrange("(h d) t -> d h t", d=64), writes=[q32[b]])
            P.dma('sp', gsb[b][:, :], src['ngate'][sl, :], writes=[gsb[b]])
            P.dma('sp', kws[b][:, :], src['nkw'][:, sl], writes=[kws[b]])
            P.dma('sp', vws[b][:, :], src['nvw'][sl, :], writes=[vws[b]])
            P.op('act', lambda g, b=b: g.mul(out=q32[b][:, :, :], in_=q32[b][:, :, :], mul=0.125), [q32[b]], [q32[b]])
            P.op('dve', lambda g, b=b: g.tensor_copy(out=qb[b][:, :, :], in_=q32[b][:, :, :]), [q32[b]], [qb[b]])
            P.op('act', lambda g, b=b: g.activation(out=gsb[b][:, :], in_=gsb[b][:, :], func=AF.Sigmoid), [gsb[b]], [gsb[b]])
            P.op('pool', lambda g, b=b, T=T: g.tensor_copy(out=kw[T % 6][:, :], in_=kws[b][:, :]), [kws[b]], [kw[T % 6]])
            P.op('pool', lambda g, b=b, T=T: g.tensor_copy(out=vw[T % 6][:, :, 0:64],
                                                          in_=vws[b][:, :].rearrange("p (g d) -> p g d", g=2)),
                 [vws[b]], [vw[T % 6]])
            OA = oall[b]
            gs3 = gsb[b][:, :].rearrange("p (h k) -> p h k", k=3)
            for g_ in range(2):
                ks = slice(64 * g_, 64 * g_ + 64)
                Q32 = q32[b][ks, :, :].rearrange("d h t -> d (h t)")
                QB = qb[b][ks, :, :].rearrange("d h t -> d (h t)")
                cmax = min(NCC - 1, (8 * T + 6) // 128)
                for c in range(cmax + 1):
                    p_ = pS[isc % 4]
                    isc += 1
                    near = c >= cmax - 1
                    P.op('pe', lambda g, p_=p_, c=c, ks=ks, Q32=Q32, near=near: g.matmul(
                        p_[:, :], lhsT=kcT[ks, c * 128:(c + 1) * 128], rhs=Q32, start=True, stop=(not near)),
                        [kcT, q32[b]], [p_], inc=(not near))
                    if near:
                        h_ = hk32[ihk % 2]
                        ihk += 1
                        for hh in range(4):
                            off = (4 * g_ + hh) * NB + NB0 + t0 - 16 * (128 * c) - 2063
                            P.dma('sp', h_[:, hh, :], bass.AP(tensor=fh, offset=off, ap=[[16, 128], [1, 128]]), writes=[h_])
                        P.op('pe', lambda g, p_=p_, h_=h_: g.matmul(p_[:, :], lhsT=C['J'],
                                                                    rhs=h_[:, :, :].rearrange("p h i -> p (h i)"),
                                                                    start=False, stop=True), [C.t, h_], [p_])
                    P.op('act', lambda g, p_=p_, c=c: g.activation(out=PC[c][:, :], in_=p_[:, :], func=AF.Exp), [p_], [PC[c]])
                OCp = pO[0]
                for hh in range(4):
                    for c in range(cmax + 1):
                        P.op('pe', lambda g, hh=hh, c=c, g_=g_: g.matmul(
                            OCp[:, hh * 65:(hh + 1) * 65], lhsT=PC[c][:, hh * 128:(hh + 1) * 128], rhs=vc1[:, c, g_, :],
                            start=(c == 0), stop=(c == cmax)), [PC[c], vc1], [OCp], inc=(hh == 3 and c == cmax))
                for hh in range(4):
                    ip = pO[1 + hh // 2]
                    for c in range(cmax + 1):
                        P.op('pe', lambda g, hh=hh, c=c, ip=ip: g.matmul(
                            ip[:, (hh % 2) * NSLC:(hh % 2 + 1) * NSLC], lhsT=PC[c][:, hh * 128:(hh + 1) * 128], rhs=wt[:, c, :],
                            start=(c == 0), stop=(c == cmax)), [PC[c], wt], [ip], inc=(hh % 2 == 1 and c == cmax))
                oc3 = OCp[:, 0:260].rearrange("p (h e) -> p h e", e=65)
                P.op('dve', lambda g, oc3=oc3: g.tensor_scalar(out=rd[:, :], in0=oc3[:, :, 64], scalar1=1e-30, scalar2=None,
                                                              op0=ALU.max), [OCp], [rd])
                P.op('dve', lambda g: g.reciprocal(out=rd[:, :], in_=rd[:, :]), [rd], [rd])
                P.op('dve', lambda g, g_=g_, gs3=gs3: g.tensor_tensor(out=cf[:, :], in0=rd[:, :], in1=gs3[:, 4 * g_:4 * g_ + 4, 0],
                                                                    op=ALU.mult), [rd, gsb[b]], [cf])
                P.op('dve', lambda g, oc3=oc3, g_=g_: g.tensor_tensor(
                    out=OA[:, 4 * g_:4 * g_ + 4, :], in0=oc3[:, :, 0:64], in1=cf[:, :].unsqueeze(2).broadcast_to([128, 4, 64]),
                    op=ALU.mult), [OCp, cf], [OA])
                for hh in range(4):
                    ip = pO[1 + hh // 2]
                    src_ap = ip[:, (hh % 2) * NSLC:(hh % 2 + 1) * NSLC]
                    if hh == 0:
                        P.op('dve', lambda g, src_ap=src_ap: g.tensor_scalar(out=sc[:, :], in0=src_ap, scalar1=rd[:, 0:1],
                                                                            scalar2=None, op0=ALU.mult), [ip, rd], [sc])
                    else:
                        P.op('dve', lambda g, src_ap=src_ap, hh=hh: g.scalar_tensor_tensor(
                            out=sc[:, :], in0=src_ap, scalar=rd[:, hh:hh + 1], in1=sc[:, :], op0=ALU.mult, op1=ALU.add),
                            [ip, rd, sc], [sc])
                c0 = 2 * T
                if c0 + 2 < NSLC:
                    P.op('pool', lambda g, c0=c0: g.memset(sc[:, c0 + 2:NSLC], -1.0), [], [sc])
                P.op('pool', lambda g, c0=c0: g.memset(sc[0:64, c0 + 1:c0 + 2], -1.0), [], [sc])
                P.op('pool', lambda g, c0=c0: g.memset(sc[0:64, c0:c0 + 1], 5.0), [], [sc])
                if c0 - 1 >= 0:
                    P.op('pool', lambda g, c0=c0: g.memset(sc[0:64, c0 - 1:c0], 5.0), [], [sc])
                P.op('pool', lambda g, c0=c0: g.memset(sc[64:128, c0:c0 + 2], 5.0), [], [sc])
                P.op('pool', lambda g: g.memset(sc[:, 0:1], 5.0), [], [sc])
                M = mk[g_]
                if NSLC > ntop:
                    P.op('dve', lambda g: g.max(out=m8[:, :], in_=sc[:, :]), [sc], [m8])
                    P.op('dve', lambda g: g.match_replace(out=sc2[:, :], in_to_replace=m8[:, :], in_values=sc[:, :],
                                                          imm_value=-1e30), [sc, m8], [sc2])
                    P.op('dve', lambda g: g.max(out=m8[:, :], in_=sc2[:, :]), [sc2], [m8])
                    P.op('dve', lambda g, M=M: g.tensor_scalar(out=M[:, :], in0=sc[:, :], scalar1=m8[:, 7:8], scalar2=None,
                                                              op0=ALU.is_ge), [sc, m8], [M])
                else:
                    P.op('pool', lambda g, M=M: g.memset(M[:, :], 1.0), [], [M])
                MB = Mb[g_]
                P.op('dve', lambda g, M=M, MB=MB: g.tensor_scalar(out=MB[:, :], in0=M[:, :], scalar1=30000.0, scalar2=-30000.0,
                                                                 op0=ALU.mult, op1=ALU.add), [M], [MB])
                def slc_a(kc, i_):
                    p_ = pS[i_ % 4]
                    pt_ = PT[i_ % 4]
                    mx = Mx[i_ % 4]
                    dl = T - kc
                    near = dl < ND
                    P.op('pool', lambda g: g.tensor_copy(
                        out=mx[:, :].rearrange("p (j r) -> p j r", r=64),
                        in_=MB[:, 2 * kc:2 * kc + 2].unsqueeze(2).broadcast_to([128, 2, 64])), [MB], [mx])
                    P.op('pe', lambda g: g.matmul(
                        p_[:, :], lhsT=ksT[ks, kc * 128:(kc + 1) * 128], rhs=QB, start=True, stop=False),
                        [ksT, qb[b]], [p_], inc=False)
                    if near:
                        P.op('pe', lambda g: g.matmul(
                            p_[:, :], lhsT=Jb[:, :], rhs=hks[dl][g_][:, :, :].rearrange("p h i -> p (h i)"), start=False,
                            stop=False), [Jb, hks[dl][g_]], [p_], inc=False)
                    P.op('pe', lambda g: g.matmul(p_[:, :], lhsT=mx[:, :], rhs=I4[:, :], start=False, stop=True),
                         [mx, I4], [p_])
                    P.op('act', lambda g: g.activation(out=pt_[:, :], in_=p_[:, :], func=AF.Exp), [p_], [pt_])
                    return pt_

                def slc_b(kc, pt_):
                    for hh in range(4):
                        P.op('pe', lambda g, hh=hh: g.matmul(
                            pO[hh][:, 0:65], lhsT=pt_[:, hh * 128:(hh + 1) * 128], rhs=vs1[:, kc, g_, :], start=(kc == 0),
                            stop=(kc == T)), [pt_, vs1], [pO[hh]], inc=(kc == T))

                pend = []
                for kk in range(min(2, T + 1)):
                    pend.append(slc_a(kk, isc))
                    isc += 1
                for kc in range(T + 1):
                    if kc + 2 <= T:
                        pend.append(slc_a(kc + 2, isc))
                        isc += 1
                    slc_b(kc, pend.pop(0))
                for hh in range(4):
                    P.op('dve', lambda g, hh=hh: g.reciprocal(out=rd[:, hh:hh + 1], in_=pO[hh][:, 64:65]), [pO[hh]], [rd])
                P.op('dve', lambda g, g_=g_, gs3=gs3: g.tensor_tensor(out=cf[:, :], in0=rd[:, :], in1=gs3[:, 4 * g_:4 * g_ + 4, 1],
                                                                    op=ALU.mult), [rd, gsb[b]], [cf])
                for hh in range(4):
                    P.op('dve', lambda g, hh=hh, g_=g_: g.scalar_tensor_tensor(
                        out=OA[:, 4 * g_ + hh, :], in0=pO[hh][:, 0:64], scalar=cf[:, hh:hh + 1], in1=OA[:, 4 * g_ + hh, :],
                        op0=ALU.mult, op1=ALU.add), [pO[hh], cf, OA], [OA])
                k0 = max(0, T - 4)
                def win_a(kc, i_):
                    p_ = pS[i_ % 4]
                    pt_ = PT[i_ % 4]
                    dl = T - kc
                    P.op('pe', lambda g: g.matmul(
                        p_[:, :], lhsT=kw[kc % 6][ks, :], rhs=QB, start=True, stop=False), [kw[kc % 6], qb[b]], [p_], inc=False)
                    P.op('pe', lambda g: g.matmul(
                        p_[:, :], lhsT=Jb[:, :], rhs=hkw[dl][g_][:, :, :].rearrange("p h i -> p (h i)"), start=False, stop=True),
                        [Jb, hkw[dl][g_]], [p_])
                    P.op('act', lambda g: g.activation(out=pt_[:, :], in_=p_[:, :], func=AF.Exp), [p_], [pt_])
                    return pt_

                def win_b(kc, pt_):
                    for hh in range(4):
                        P.op('pe', lambda g, hh=hh: g.matmul(
                            pO[hh][:, 0:65], lhsT=pt_[:, hh * 128:(hh + 1) * 128], rhs=vw[kc % 6][:, g_, :], start=(kc == k0),
                            stop=(kc == T)), [pt_, vw[kc % 6]], [pO[hh]], inc=(kc == T))

                pend = win_a(k0, isc)
                isc += 1
                for kc in range(k0, T + 1):
                    nxt = None
                    if kc < T:
                        nxt = win_a(kc + 1, isc)
                        isc += 1
                    win_b(kc, pend)
                    pend = nxt
                for hh in range(4):
                    P.op('dve', lambda g, hh=hh: g.reciprocal(out=rd[:, hh:hh + 1], in_=pO[hh][:, 64:65]), [pO[hh]], [rd])
                P.op('dve', lambda g, g_=g_, gs3=gs3: g.tensor_tensor(out=cf[:, :], in0=rd[:, :], in1=gs3[:, 4 * g_:4 * g_ + 4, 2],
                                                                    op=ALU.mult), [rd, gsb[b]], [cf])
                for hh in range(4):
                    P.op('dve', lambda g, hh=hh, g_=g_: g.scalar_tensor_tensor(
                        out=OA[:, 4 * g_ + hh, :], in0=pO[hh][:, 0:64], scalar=cf[:, hh:hh + 1], in1=OA[:, 4 * g_ + hh, :],
                        op0=ALU.mult, op1=ALU.add), [pO[hh], cf, OA], [OA])
            O2 = OA[:, :, :].rearrange("p h d -> p (h d)")
            OAt = OA
            group_rms_ap(P, OAt, O2, sq, ms, 1, 512, 1e-6)
            P.op('pool', lambda g, O2=O2: g.tensor_tensor(out=O2, in0=O2, in1=ng[:, :], op=ALU.mult), [OAt, ng], [OAt])
            P.dma('act', mixed[sl, 512:1024], O2, reads=[OAt])
        P.barrier()


def group_rms_ap(P, Yt, Y, sq, ms, ng, gw, eps):
    n = ng * gw
    P.op('pool', lambda g: g.tensor_tensor(out=sq[:, :n], in0=Y, in1=Y, op=ALU.mult), [Yt], [sq])
    P.op('dve', lambda g: g.tensor_reduce(out=ms[:, :ng], in_=sq[:, :n].rearrange("p (g d) -> p g d", g=ng), axis=AX.X,
                                          op=ALU.add), [sq], [ms])
    P.op('act', lambda g: g.activation(out=ms[:, :ng], in_=ms[:, :ng], func=AF.Sqrt, bias=eps, scale=1.0 / gw), [ms], [ms])
    P.op('dve', lambda g: g.reciprocal(out=ms[:, :ng], in_=ms[:, :ng]), [ms], [ms])
    P.op('dve', lambda g: g.tensor_scalar(out=Y, in0=Y, scalar1=ms[:, 0:1], scalar2=None, op0=ALU.mult), [Yt, ms], [Yt])


def phase_wout(P, S, h_in, mixed, woutt, ln_g, ln_b, x_out, C):
    KC = D // 128
    with ExitStack() as es:
        gt = load_bcast_rows(P, es, ln_g, D, 'wlg')
        bt = load_bcast_rows(P, es, ln_b, D, 'wlb')
        wo = P.sb(es, [128, KC, D], BF16, 'wwo')
        for kc in range(KC):
            P.dma('sp', wo[:, kc, :], woutt[:, kc, :], writes=[wo])
        M = [P.sb(es, [128, D], F32, 'wm') for _ in range(2)]
        Hh = [P.sb(es, [128, D], F32, 'wh') for _ in range(2)]
        MT = [P.sb(es, [128, KC, 128], BF16, 'wmT') for _ in range(2)]
        y = [P.sb(es, [128, D], F32, 'wy') for _ in range(2)]
        st = [P.sb(es, [128, 4, 6], F32, 'wst') for _ in range(2)]
        mv = [P.sb(es, [128, 2], F32, 'wmv') for _ in range(2)]
        rs = [P.sb(es, [128, 1], F32, 'wrs') for _ in range(2)]
        pt = [P.ps(es, [128, 512], F32, 'wpt') for _ in range(2)]
        po = [P.ps(es, [128, 512], F32, 'wpo') for _ in range(4)]
        ic = [0]
        ip = 0
        for it in range(S // 128):
            b = it % 2
            sl = slice(it * 128, (it + 1) * 128)
            P.dma('sp', M[b][:, :], mixed[sl, :], writes=[M[b]])
            P.dma('sp', Hh[b][:, :], h_in[sl, :], writes=[Hh[b]])
            build_xT(P, M[b], MT[b], 0, C['ident'], C.t, pt, ic)
            for cg in range(4):
                p_ = po[ip % 4]
                ip += 1
                for kc in range(KC):
                    P.op('pe', lambda g, p_=p_, kc=kc, cg=cg, b=b: g.matmul(
                        p_[:, :], lhsT=MT[b][:, kc, :], rhs=wo[:, kc, cg * 512:(cg + 1) * 512], start=(kc == 0),
                        stop=(kc == KC - 1)), [MT[b], wo], [p_], inc=(kc == KC - 1))
                P.op('dve', lambda g, p_=p_, cg=cg, b=b: g.scalar_tensor_tensor(
                    out=Hh[b][:, cg * 512:(cg + 1) * 512], in0=Hh[b][:, cg * 512:(cg + 1) * 512], scalar=ALPHA, in1=p_[:, :],
                    op0=ALU.mult, op1=ALU.add), [p_, Hh[b]], [Hh[b]])
            layer_norm_rows(P, Hh[b], y[b], gt, bt, st[b], mv[b], rs[b], 1.0, 1e-5)
            P.dma('act', x_out[sl, :], y[b][:, :], reads=[y[b]])
        P.barrier()


WSHAPES = {'ln1_g': [D], 'ln1_b': [D], 'ffn1_w1': [D, DFF], 'ffn1_w3': [D, DFF], 'ffn1_w2': [DFF, D],
           'ln2_g': [D], 'ln2_b': [D], 'w_in': [D, DIN], 'w_out': [D, D],
           'hgrn_lb_logits': [512], 'hgrn_norm_g': [512],
           'nsa_pe_k': [32, 64], 'nsa_w1_k': [2048, 256], 'nsa_w2_k': [256, 64], 'nsa_pe_v': [32, 64],
           'nsa_w1_v': [2048, 256], 'nsa_w2_v': [256, 64], 'nsa_norm_g': [512],
           'ssm_conv_w': [4, 1024], 'ssm_conv_b': [1024], 'ssm_dt_bias': [8], 'ssm_a_log': [8], 'ssm_d': [8],
           'ssm_norm_g': [512], 'ln3_g': [D], 'ln3_b': [D], 'ffn2_w1': [D, DFF], 'ffn2_w3': [D, DFF], 'ffn2_w2': [DFF, D]}


def build(S, dff=DFF, depth=DEPTH, only='all'):
    nc = bass.Bass("TRN2", target_bir_lowering=False)
    dt = nc.dram_tensor
    x = dt("x", [S, D], F32, kind="ExternalInput")
    out = dt("out", [S, D], F32, kind="ExternalOutput")
    cin = {'pack': dt("c_pack", [128, NCONST], F32, kind="ExternalInput"),
           'cos': dt("c_cos", [S, 64], F32, kind="ExternalInput"),
           'sin': dt("c_sin", [S, 64], F32, kind="ExternalInput"),
           'oh': dt("c_oh", [33, NB], F32, kind="ExternalInput"),
           'ohw': dt("c_ohw", [33, NB], F32, kind="ExternalInput"),
           'wt': dt("c_wt", [S // 16, S // 64], F32, kind="ExternalInput")}
    W = {}
    for nm, shp in WSHAPES.items():
        shp = [dff if v == DFF else v for v in shp]
        W[nm] = dt(nm, [depth] + shp, F32, kind="ExternalInput")
    W['rel_bias'] = dt('rel_bias', [32, 8], F32, kind="ExternalInput")
    HC, KC = dff // 128, D // 128
    w1t = dt("w1t", [HC, 128, KC, 128], BF16)
    w3t = dt("w3t", [HC, 128, KC, 128], BF16)
    w2t = dt("w2t", [KC, 128, HC, 128], BF16)
    wint = dt("wint", [128, KC, DIN], BF16)
    woutt = dt("woutt", [128, KC, D], BF16)
    xa = dt("xa", [S, D], F32)
    xb = dt("xb", [S, D], F32)
    xc = dt("xc", [S, D], F32)
    mixed = dt("mixed", [S, D], F32)
    fh = dt("fh", [8, NB], F32)
    fhw = dt("fhw", [8, NB], F32)
    kcT_d = dt("kcT_d", [128, S // 16], F32)
    vc_d = dt("vc_d", [S // 16, 128], F32)
    proj = {}
    for (nm, c0, n, mode) in GROUPS:
        proj[nm] = dt("pj_" + nm, [n, S] if mode == 'FM' else [S, n], F32)
    P = Prog(nc)
    with ExitStack() as es:
        C = Consts(P, es, cin)
        if only == 'ffn':
            phase_convert_ffn(P, W['ffn1_w1'][0, :, :], W['ffn1_w3'][0, :, :], W['ffn1_w2'][0, :, :], w1t, w3t, w2t, dff)
            phase_ffn(P, S, x, out, w1t, w3t, w2t, W['ln1_g'][0, :], W['ln1_b'][0, :], C, dff)
        elif only.startswith('mix'):
            which = only.split(':')[1].split(',') if ':' in only else ['ret']
            with ExitStack() as es2:
                t = P.sb(es2, [128, D], F32, 'zz')
                P.op('pool', lambda g: g.memset(t[:, :], 0.0), [], [t])
                for i in range(S // 128):
                    P.dma('sp', mixed[i * 128:(i + 1) * 128, :], t[:, :], reads=[t])
                P.barrier()
            phase_convert_rows(P, W['w_in'][0, :, :], wint, DIN)
            phase_proj(P, S, x, wint, proj, C)
            if 'ret' in which:
                phase_retention(P, S, proj, mixed, C, cin)
            if 'ssd' in which:
                phase_ssd(P, S, proj, mixed, C, W, 0)
            if 'hgrn' in which:
                phase_hgrn(P, S, proj, mixed, C, W, LBL, depth)
            if 'nsa' in which:
                phase_nsa_tables(P, W['rel_bias'], cin, fh, fhw)
                phase_nsa_compress(P, S, proj, W, 0, kcT_d, vc_d)
                phase_nsa_attn(P, S, proj, W, 0, C, cin, kcT_d, vc_d, fh, fhw, mixed)
            with ExitStack() as es2:
                t = P.sb(es2, [128, D], F32, 'cp')
                for i in range(S // 128):
                    P.dma('sp', t[:, :], mixed[i * 128:(i + 1) * 128, :], writes=[t])
                    P.dma('sp', out[i * 128:(i + 1) * 128, :], t[:, :], reads=[t])
                P.barrier()
        else:
            phase_nsa_tables(P, W['rel_bias'], cin, fh, fhw)
            cur = x
            for l in range(depth):
                last = (l == depth - 1)
                phase_convert_ffn(P, W['ffn1_w1'][l, :, :], W['ffn1_w3'][l, :, :], W['ffn1_w2'][l, :, :], w1t, w3t, w2t, dff)
                phase_ffn(P, S, cur, xa, w1t, w3t, w2t, W['ln1_g'][l, :], W['ln1_b'][l, :], C, dff)
                phase_convert_rows(P, W['w_in'][l, :, :], wint, DIN)
                phase_proj(P, S, xa, wint, proj, C)
                phase_hgrn(P, S, proj, mixed, C, W, l, depth)
                phase_ssd(P, S, proj, mixed, C, W, l)
                phase_retention(P, S, proj, mixed, C, cin)
                phase_nsa_compress(P, S, proj, W, l, kcT_d, vc_d)
                phase_nsa_attn(P, S, proj, W, l, C, cin, kcT_d, vc_d, fh, fhw, mixed)
                phase_convert_rows(P, W['w_out'][l, :, :], woutt, D)
                phase_wout(P, S, xa, mixed, woutt, W['ln2_g'][l, :], W['ln2_b'][l, :], xb, C)
                phase_convert_ffn(P, W['ffn2_w1'][l, :, :], W['ffn2_w3'][l, :, :], W['ffn2_w2'][l, :, :], w1t, w3t, w2t, dff)
                phase_ffn(P, S, xb, out if last else xc, w1t, w3t, w2t, W['ln3_g'][l, :], W['ln3_b'][l, :], C, dff)
                cur = xc
    return nc


_NC_CACHE = {}


def kernel(**inputs):
    x = np.asarray(inputs['x'])
    B, S, _ = x.shape
    dff = int(np.asarray(inputs['ffn1_w1']).shape[-1])
    depth = int(np.asarray(inputs['ffn1_w1']).shape[0])
    key = (S, dff, depth)
    if key not in _NC_CACHE:
        _NC_CACHE[key] = build(S, dff=dff, depth=depth)
    nc = _NC_CACHE[key]
    consts = host_consts(S)
    base = {k: np.ascontiguousarray(np.asarray(inputs[k], dtype=np.float32)) for k in list(WSHAPES) + ['rel_bias']}
    base.update(consts)
    in_maps = []
    for b in range(B):
        m = dict(base)
        m['x'] = np.ascontiguousarray(x[b], dtype=np.float32)
        in_maps.append(m)
    res = run_bass_kernel_spmd(nc, in_maps, core_ids=list(range(B)))
    return np.stack([np.asarray(r['out']) for r in res.results], axis=0).astype(np.float32)
```

```python
import numpy as np
import ml_dtypes
from contextlib import ExitStack
import concourse.bass as bass
import concourse.mybir as mybir
from concourse.bass_utils import run_bass_kernel_spmd

F32 = mybir.dt.float32
BF16 = mybir.dt.bfloat16
AF = mybir.ActivationFunctionType
ALU = mybir.AluOpType
AX = mybir.AxisListType

D = 2048
DFF = 5632
DEPTH = 2
ALPHA = float((2 * DEPTH) ** 0.25)
KQ = 6


class TB:
    def __init__(self, h, psum=False):
        self.h = h
        self.w = None
        self.r = {}
        self.psum = psum

    def __getitem__(self, k):
        return self.h[k]


class Prog:
    def __init__(self, nc):
        self.nc = nc
        self.eng = {'pe': nc.tensor, 'dve': nc.vector, 'act': nc.scalar, 'pool': nc.gpsimd, 'sp': nc.sync}
        self.sem = {k: nc.alloc_semaphore('s_' + k) for k in self.eng}
        self.cnt = {k: 0 for k in self.eng}
        self.waited = {k: {} for k in self.eng}
        self.dq = {q: dict(sems=[nc.alloc_semaphore(f'd_{q}{i}') for i in range(KQ)], n=0)
                   for q in ('sp', 'act', 'pool')}
        self.arrive = nc.alloc_semaphore('arrive')
        self.go = nc.alloc_semaphore('go')
        self.epoch = 0
        self.uid = 0
        self.rr = 0

    def sb(self, es, shape, dt, name='t'):
        self.uid += 1
        return TB(es.enter_context(self.nc.sbuf_tensor(f'{name}_{self.uid}', list(shape), dt)))

    def ps(self, es, shape, dt, name='p'):
        self.uid += 1
        return TB(es.enter_context(self.nc.psum_tensor(f'{name}_{self.uid}', list(shape), dt)), psum=True)

    def _wait(self, e, tok):
        if tok is None or tok[-1] != self.epoch:
            return
        if tok[0] == 'c':
            key, sem, v = tok[1], self.sem[tok[1]], tok[2]
            if key == e and e == 'pe':
                return
        else:
            key, sem, v = tok[2], tok[1], tok[3]
        if self.waited[e].get(key, 0) >= v:
            return
        self.eng[e].wait_ge(sem, v)
        self.waited[e][key] = v

    def _deps(self, e, reads, writes, is_dma=False):
        for b in reads:
            self._wait(e, b.w)
            if b.psum:
                for k, tok in b.r.items():
                    if k != e:
                        self._wait(e, tok)
        for b in writes:
            self._wait(e, b.w)
            for k, tok in b.r.items():
                self._wait(e, tok)

    def op(self, e, fn, reads=(), writes=(), inc=True):
        self._deps(e, reads, writes)
        ins = fn(self.eng[e])
        if inc:
            self.cnt[e] += 1
            assert self.cnt[e] < 60000, "semaphore overflow: add a barrier"
            ins.then_inc(self.sem[e], 1)
            tok = ('c', e, self.cnt[e], self.epoch)
        else:
            tok = ('c', e, self.cnt[e] + 1, self.epoch)
        for b in reads:
            b.r[e] = tok
        for b in writes:
            b.w = tok
            b.r = {}
        return ins

    def dma(self, q, out, in_, reads=(), writes=(), **kw):
        d = self.dq[q]
        i = d['n']
        slot = i % KQ
        sem = d['sems'][slot]
        v = 16 * (i // KQ + 1)
        assert v < 60000, "dma semaphore overflow: add a barrier"
        if i >= KQ:
            self._wait(q, ('d', sem, (q, slot), v - 16, self.epoch))
        self._deps(q, reads, writes, is_dma=True)
        self.eng[q].dma_start(out=out, in_=in_, **kw).then_inc(sem, 16)
        d['n'] += 1
        tok = ('d', sem, (q, slot), v, self.epoch)
        for b in reads:
            b.r[('d', q, slot)] = tok
        for b in writes:
            b.w = tok
            b.r = {}

    def dmaq(self):
        self.rr += 1
        return ('sp', 'sp')[self.rr % 2]

    def barrier(self):
        names = list(self.eng)
        for e in names:
            for w in names:
                if w != e and self.cnt[w] > 0:
                    self._wait(e, ('c', w, self.cnt[w], self.epoch))
            for q, d in self.dq.items():
                for slot in range(KQ):
                    n_used = (d['n'] - slot + KQ - 1) // KQ
                    if n_used > 0:
                        self._wait(e, ('d', d['sems'][slot], (q, slot), 16 * n_used, self.epoch))
        self.epoch += 1
        assert self.epoch < 4000
        for e in names:
            if e != 'pool':
                self.eng[e].sem_inc(self.arrive, 1)
        g = self.eng['pool']
        g.wait_ge(self.arrive, 4 * self.epoch)
        for s in self.sem.values():
            g.sem_clear(s)
        for d in self.dq.values():
            for s in d['sems']:
                g.sem_clear(s)
            d['n'] = 0
        g.sem_inc(self.go, 1)
        for e in names:
            self.eng[e].wait_ge(self.go, self.epoch)
        self.cnt = {k: 0 for k in self.eng}
        self.waited = {k: {} for k in self.eng}


def cast_rr(P, i, out_ap, in_ap, reads, writes):
    e = ('dve', 'pool', 'act')[i % 3]
    if e == 'act':
        P.op('act', lambda g: g.copy(out=out_ap, in_=in_ap), reads, writes)
    else:
        P.op(e, lambda g: g.tensor_copy(out=out_ap, in_=in_ap), reads, writes)


def phase_convert_ffn(P, w1, w3, w2, w1t, w3t, w2t, dff):
    nc = P.nc
    HC = dff // 128
    KC = D // 128
    with ExitStack() as es:
        nb = 2
        fin = [P.sb(es, [128, max(dff, D)], F32, 'cvf') for _ in range(nb)]
        fo = [P.sb(es, [128, max(dff, D)], BF16, 'cvb') for _ in range(nb)]
        i = 0
        for (w, wt) in ((w1, w1t), (w3, w3t)):
            for kc in range(KC):
                a, b = fin[i % nb], fo[i % nb]
                for c0 in range(0, dff, 1024):
                    c1 = min(dff, c0 + 1024)
                    P.dma(P.dmaq(), a[:, c0:c1], w[kc * 128:(kc + 1) * 128, c0:c1], writes=[a])
                for c0 in range(0, dff, 2048):
                    c1 = min(dff, c0 + 2048)
                    cast_rr(P, i + c0 // 2048, b[:, c0:c1], a[:, c0:c1], [a], [b])
                for h0 in range(0, HC, 4):
                    h1 = min(HC, h0 + 4)
                    P.dma(P.dmaq(), wt[h0:h1, :, kc, :].rearrange("h p n -> p h n"),
                          b[:, h0 * 128:h1 * 128].rearrange("p (h n) -> p h n", n=128), reads=[b])
                i += 1
        for hc in range(HC):
            a, b = fin[i % nb], fo[i % nb]
            P.dma(P.dmaq(), a[:, :D], w2[hc * 128:(hc + 1) * 128, :], writes=[a])
            cast_rr(P, i, b[:, :D], a[:, :D], [a], [b])
            for c0 in range(0, KC, 4):
                P.dma(P.dmaq(), w2t[c0:c0 + 4, :, hc, :].rearrange("c p n -> p c n"),
                      b[:, c0 * 128:(c0 + 4) * 128].rearrange("p (c n) -> p c n", n=128), reads=[b])
            i += 1
    P.barrier()


def load_bcast_rows(P, es, vec_ap, n, name):
    t = P.sb(es, [128, n], F32, name)
    P.dma('sp', t[:, :], vec_ap.partition_broadcast(128), writes=[t])
    return t


def phase_ffn(P, S, x_in, x_out, w1t, w3t, w2t, ln_g, ln_b, consts, dff):
    nc = P.nc
    HC = dff // 128
    KC = D // 128
    NT = 512 if S % 512 == 0 else S
    ntile = S // NT
    nsub = NT // 128
    with ExitStack() as es:
        ident = consts['ident']
        cb = consts.t
        gt = load_bcast_rows(P, es, ln_g, D, 'lng')
        bt = load_bcast_rows(P, es, ln_b, D, 'lnb')
        xs = [[P.sb(es, [128, D], F32, 'xs') for _ in range(nsub)] for _ in range(1)]
        xT = [P.sb(es, [128, KC, NT], BF16, 'xT') for _ in range(2)]
        gT = P.sb(es, [128, HC, NT], BF16, 'gT')
        w1b = [P.sb(es, [128, KC, 128], BF16, 'w1b') for _ in range(3)]
        w3b = [P.sb(es, [128, KC, 128], BF16, 'w3b') for _ in range(3)]
        w2b = [P.sb(es, [128, HC, 128], BF16, 'w2b') for _ in range(2)]
        sg = [P.sb(es, [128, NT], F32, 'sg') for _ in range(2)]
        oT = [P.sb(es, [128, NT], F32, 'oT') for _ in range(2)]
        y = [P.sb(es, [128, D], F32, 'y') for _ in range(2)]
        st = [P.sb(es, [128, 4, 6], F32, 'st') for _ in range(2)]
        mv = [P.sb(es, [128, 2], F32, 'mv') for _ in range(2)]
        rs = [P.sb(es, [128, 1], F32, 'rs') for _ in range(2)]
        pa = [P.ps(es, [128, 512], F32, 'pa') for _ in range(2)]
        pb = [P.ps(es, [128, 512], F32, 'pb') for _ in range(2)]
        po = [P.ps(es, [128, 512], F32, 'po') for _ in range(2)]
        pt = [P.ps(es, [128, 512], F32, 'pt') for _ in range(2)]
        ic = 0
        for it in range(ntile):
            X, XT = xs[0], xT[it % 2]
            t0 = it * NT
            for s in range(nsub):
                P.dma(P.dmaq(), X[s][:, :], x_in[t0 + s * 128: t0 + (s + 1) * 128, :], writes=[X[s]])
            for s in range(nsub):
                for k4 in range(KC // 4):
                    p_ = pt[ic % 2]
                    ic += 1
                    for j in range(4):
                        kc = k4 * 4 + j
                        P.op('pe', lambda g, p_=p_, j=j, kc=kc, s=s: g.transpose(
                            out=p_[:, j * 128:(j + 1) * 128], in_=X[s][:, kc * 128:(kc + 1) * 128],
                            identity=ident), [X[s], cb], [p_], inc=(j == 3))
                    e = 'dve' if (ic % 2) else 'act'
                    outap = XT[:, k4 * 4:(k4 + 1) * 4, s * 128:(s + 1) * 128]
                    inap = p_[:, :].rearrange("p (j n) -> p j n", n=128)
                    if e == 'dve':
                        P.op('dve', lambda g, o=outap, i_=inap: g.tensor_copy(out=o, in_=i_), [p_], [XT])
                    else:
                        P.op('act', lambda g, o=outap, i_=inap: g.copy(out=o, in_=i_), [p_], [XT])
            for hc in range(HC):
                wa, wb = w1b[hc % 3], w3b[hc % 3]
                P.dma('sp', wa[:, :, :], w1t[hc, :, :, :], writes=[wa])
                P.dma('sp', wb[:, :, :], w3t[hc, :, :, :], writes=[wb])
                A, B = pa[hc % 2], pb[hc % 2]
                for kc in range(KC):
                    P.op('pe', lambda g, A=A, wa=wa, kc=kc: g.matmul(
                        A[:, :NT], lhsT=wa[:, kc, :], rhs=XT[:, kc, :], start=(kc == 0), stop=(kc == KC - 1)),
                        [wa, XT], [A], inc=(kc == KC - 1))
                for kc in range(KC):
                    P.op('pe', lambda g, B=B, wb=wb, kc=kc: g.matmul(
                        B[:, :NT], lhsT=wb[:, kc, :], rhs=XT[:, kc, :], start=(kc == 0), stop=(kc == KC - 1)),
                        [wb, XT], [B], inc=(kc == KC - 1))
                S_ = sg[hc % 2]
                P.op('act', lambda g, S_=S_, A=A: g.activation(out=S_[:, :NT], in_=A[:, :NT], func=AF.Silu),
                     [A], [S_])
                P.op('dve', lambda g, S_=S_, B=B, hc=hc: g.tensor_tensor(
                    out=gT[:, hc, :], in0=S_[:, :NT], in1=B[:, :NT], op=ALU.mult), [S_, B], [gT])
            for dc in range(KC):
                w2 = w2b[dc % 2]
                P.dma(P.dmaq(), w2[:, :, :], w2t[dc, :, :, :], writes=[w2])
                O = po[dc % 2]
                for hc in range(HC):
                    P.op('pe', lambda g, O=O, w2=w2, hc=hc: g.matmul(
                        O[:, :NT], lhsT=w2[:, hc, :], rhs=gT[:, hc, :], start=(hc == 0), stop=(hc == HC - 1)),
                        [w2, gT], [O], inc=(hc == HC - 1))
                ot = oT[dc % 2]
                P.op('act', lambda g, ot=ot, O=O: g.copy(out=ot[:, :NT], in_=O[:, :NT]), [O], [ot])
                for s in range(nsub):
                    p_ = pt[ic % 2]
                    ic += 1
                    P.op('pe', lambda g, p_=p_, ot=ot, s=s: g.transpose(
                        out=p_[:, 0:128], in_=ot[:, s * 128:(s + 1) * 128], identity=ident),
                        [ot, cb], [p_])
                    P.op('dve', lambda g, p_=p_, s=s, dc=dc: g.scalar_tensor_tensor(
                        out=X[s][:, dc * 128:(dc + 1) * 128], in0=X[s][:, dc * 128:(dc + 1) * 128], scalar=2.0 * ALPHA,
                        in1=p_[:, 0:128], op0=ALU.mult, op1=ALU.add), [p_, X[s]], [X[s]])
            for s in range(nsub):
                layer_norm_rows(P, X[s], y[s % 2], gt, bt, st[s % 2], mv[s % 2], rs[s % 2], 0.5, 1e-5)
                P.dma(P.dmaq(), x_out[t0 + s * 128: t0 + (s + 1) * 128, :], y[s % 2][:, :], reads=[y[s % 2]])
    P.barrier()


def layer_norm_rows(P, xin, yout, gt, bt, st, mv, rs, prescale, eps):
    nchunk = D // 512
    for c in range(nchunk):
        P.op('dve', lambda g, c=c: g.bn_stats(out=st[:, c, :], in_=xin[:, c * 512:(c + 1) * 512]), [xin], [st])
    P.op('dve', lambda g: g.bn_aggr(out=mv[:, :], in_=st[:, :, :]), [st], [mv])
    P.op('act', lambda g: g.activation(out=rs[:, :], in_=mv[:, 1:2], func=AF.Sqrt,
                                       bias=eps / (prescale * prescale), scale=1.0), [mv], [rs])
    P.op('dve', lambda g: g.reciprocal(out=rs[:, :], in_=rs[:, :]), [rs], [rs])
    P.op('dve', lambda g: g.tensor_scalar(out=yout[:, :], in0=xin[:, :], scalar1=mv[:, 0:1], scalar2=rs[:, 0:1],
                                          op0=ALU.subtract, op1=ALU.mult), [xin, mv, rs], [yout])
    P.op('pool', lambda g: g.tensor_tensor(out=yout[:, :], in0=yout[:, :], in1=gt[:, :], op=ALU.mult),
         [yout, gt], [yout])
    P.op('pool', lambda g: g.tensor_tensor(out=yout[:, :], in0=yout[:, :], in1=bt[:, :], op=ALU.add),
         [yout, bt], [yout])


RET_H = 4
GROUPS = [
    ('hq', 0, 512, 'FM'), ('hfF', 512, 512, 'FM'), ('hi', 1024, 512, 'TM'), ('hg', 1536, 512, 'TM'),
    ('nq', 2048, 512, 'FM'), ('nkc', 2560, 128, 'FM'), ('nvc', 2688, 128, 'FM'), ('nks', 2816, 128, 'FM'),
    ('nvs', 2944, 128, 'TM'), ('nkw', 3072, 128, 'FM'), ('nvw', 3200, 128, 'TM'), ('ngate', 3328, 24, 'TM'),
    ('sz', 3352, 512, 'TM'), ('sxbc', 3864, 1024, 'FM'), ('sdt', 4888, 8, 'TM'),
    ('rq', 4896, 512, 'TM'), ('rk', 5408, 512, 'TM'), ('rv', 5920, 512, 'TM'), ('rg', 6432, 512, 'TM'),
]
DIN = 6944


def _const_table():
    c = {}
    i = np.arange(128)
    c['ident'] = np.eye(128)
    c['U'] = (i[:, None] <= i[None, :]).astype(np.float64)
    c['J'] = np.eye(128)[::-1].copy()
    c['ones'] = np.ones((128, 128))
    c['negmask'] = np.where(i[:, None] <= i[None, :], 0.0, -1e5)
    blk = (i[:, None] // 32) == (i[None, :] // 32)
    c['M32'] = (blk & (i[:, None] <= i[None, :])).astype(np.float64)
    c['U32'] = c['M32'].copy()
    c['B32'] = blk.astype(np.float64)
    c['V32'] = (blk & (i[:, None] > i[None, :])).astype(np.float64)
    ns = np.ones((128, 512)); ns[:, ::32] = 0.0
    c['notstart'] = ns
    c['bm'] = np.stack([(i // 32 == (cc % 4)).astype(np.float64) for cc in range(32)], axis=1)
    lg = np.log(1.0 - 2.0 ** (-5.0 - np.arange(RET_H)))
    L = np.zeros((RET_H, 128, 128))
    for h in range(RET_H):
        L[h] = np.where(i[:, None] <= i[None, :], np.exp(lg[h] * (i[None, :] - i[:, None])), 0.0)
    c['retL'] = np.concatenate(list(L), axis=1)
    c['retG1'] = np.concatenate([np.broadcast_to(np.exp(lg[h] * (i[None, :] + 1)), (128, 128)) for h in range(RET_H)], 1)
    c['retW'] = np.concatenate([np.broadcast_to(np.exp(lg[h] * (127 - i[:, None])) * 128 ** -0.5, (128, 128))
                                for h in range(RET_H)], 1)
    return c, [float(np.exp(lg[h] * 128)) for h in range(RET_H)]


_CT, RET_G128 = _const_table()
COFF = {}
_o = 0
for _k, _v in _CT.items():
    COFF[_k] = (_o, _v.shape[1])
    _o += _v.shape[1]
NCONST = _o


def host_consts(S):
    pack = np.concatenate([v for v in _CT.values()], axis=1).astype(np.float32)
    half = 64
    theta = (1.0 / (np.float32(10000.0) ** np.linspace(0.0, 1.0, half, dtype=np.float32))).astype(np.float32)
    ang = (np.arange(S, dtype=np.float32)[:, None] * theta[None, :]).astype(np.float32)
    d = {'c_pack': pack, 'c_cos': np.cos(ang).astype(np.float32), 'c_sin': np.sin(ang).astype(np.float32)}
    d.update(nsa_host_consts(S))
    return d


class Consts:
    def __init__(self, P, es, cin):
        self.t = P.sb(es, [128, NCONST], F32, 'cpack')
        P.dma('sp', self.t[:, :], cin['pack'][:, :], writes=[self.t])
        self.cin = cin

    def __getitem__(self, k):
        o, n = COFF[k]
        return self.t[:, o:o + n]

    def sl(self, k, a, b):
        o, n = COFF[k]
        return self.t[:, o + a:o + b]


def build_xT(P, X, XT, s, ident, cb, pt, ic):
    KC = D // 128
    for k4 in range(KC // 4):
        p_ = pt[ic[0] % 2]
        ic[0] += 1
        for j in range(4):
            kc = k4 * 4 + j
            P.op('pe', lambda g, p_=p_, j=j, kc=kc: g.transpose(
                out=p_[:, j * 128:(j + 1) * 128], in_=X[:, kc * 128:(kc + 1) * 128], identity=ident),
                [X, cb], [p_], inc=(j == 3))
        outap = XT[:, k4 * 4:(k4 + 1) * 4, s * 128:(s + 1) * 128]
        inap = p_[:, :].rearrange("p (j n) -> p j n", n=128)
        if ic[0] % 2:
            P.op('dve', lambda g, o=outap, i_=inap: g.tensor_copy(out=o, in_=i_), [p_], [XT])
        else:
            P.op('act', lambda g, o=outap, i_=inap: g.copy(out=o, in_=i_), [p_], [XT])


def phase_convert_rows(P, w, wt, ncols):
    KC = D // 128
    with ExitStack() as es:
        fin = [P.sb(es, [128, ncols], F32, 'cvf') for _ in range(2)]
        fo = [P.sb(es, [128, ncols], BF16, 'cvb') for _ in range(2)]
        for kc in range(KC):
            a, b = fin[kc % 2], fo[kc % 2]
            for c0 in range(0, ncols, 1024):
                c1 = min(ncols, c0 + 1024)
                P.dma('sp', a[:, c0:c1], w[kc * 128:(kc + 1) * 128, c0:c1], writes=[a])
            for c0 in range(0, ncols, 2048):
                c1 = min(ncols, c0 + 2048)
                cast_rr(P, kc + c0 // 2048, b[:, c0:c1], a[:, c0:c1], [a], [b])
            P.dma('act', wt[:, kc, :], b[:, :], reads=[b])
    P.barrier()


def phase_proj(P, S, h_in, wint, dst, C):
    KC = D // 128
    NT = 512 if S % 512 == 0 else S
    nsub = NT // 128
    with ExitStack() as es:
        X = [P.sb(es, [128, D], F32, 'px') for _ in range(2)]
        XT = [P.sb(es, [128, KC, NT], BF16, 'pxT') for _ in range(2)]
        wb = [P.sb(es, [128, KC, 512], BF16, 'pw') for _ in range(2)]
        ob = [P.sb(es, [128, 512], F32, 'po') for _ in range(3)]
        pm = [P.ps(es, [128, 512], F32, 'pm') for _ in range(4)]
        pt = [P.ps(es, [128, 512], F32, 'pt') for _ in range(2)]
        ic = [0]
        io = 0
        ipm = 0
        pieces = []
        for (nm, c0, n, mode) in GROUPS:
            step = 128 if mode == 'FM' else 512
            for a in range(0, n, step):
                pieces.append((nm, c0, a, min(step, n - a), mode))
        for it in range(S // NT):
            t0 = it * NT
            xt = XT[it % 2]
            for s in range(nsub):
                x = X[s % 2]
                P.dma('sp', x[:, :], h_in[t0 + s * 128:t0 + (s + 1) * 128, :], writes=[x])
                build_xT(P, x, xt, s, C['ident'], C.t, pt, ic)
            for ip, (nm, c0, a, n, mode) in enumerate(pieces):
                w = wb[ip % 2]
                P.dma('sp', w[:, :, :n], wint[:, :, c0 + a:c0 + a + n], writes=[w])
                if mode == 'FM':
                    pp = pm[ipm % 4]
                    ipm += 1
                    for kc in range(KC):
                        P.op('pe', lambda g, pp=pp, w=w, kc=kc, n=n: g.matmul(
                            pp[:n, :NT], lhsT=w[:, kc, :n], rhs=xt[:, kc, :], start=(kc == 0), stop=(kc == KC - 1)),
                            [w, xt], [pp], inc=(kc == KC - 1))
                    o = ob[io % 3]
                    io += 1
                    ev = 'act' if io % 2 else 'dve'
                    if ev == 'act':
                        P.op('act', lambda g, o=o, pp=pp, n=n: g.copy(out=o[:n, :NT], in_=pp[:n, :NT]), [pp], [o])
                    else:
                        P.op('dve', lambda g, o=o, pp=pp, n=n: g.tensor_copy(out=o[:n, :NT], in_=pp[:n, :NT]), [pp], [o])
                    P.dma('act', dst[nm][a:a + n, t0:t0 + NT], o[:n, :NT], reads=[o])
                else:
                    for s in range(nsub):
                        pp = pm[ipm % 4]
                        ipm += 1
                        for kc in range(KC):
                            P.op('pe', lambda g, pp=pp, w=w, kc=kc, n=n, s=s: g.matmul(
                                pp[:, :n], lhsT=xt[:, kc, s * 128:(s + 1) * 128], rhs=w[:, kc, :n],
                                start=(kc == 0), stop=(kc == KC - 1)), [w, xt], [pp], inc=(kc == KC - 1))
                        o = ob[io % 3]
                        io += 1
                        if io % 2:
                            P.op('act', lambda g, o=o, pp=pp, n=n: g.copy(out=o[:, :n], in_=pp[:, :n]), [pp], [o])
                        else:
                            P.op('dve', lambda g, o=o, pp=pp, n=n: g.tensor_copy(out=o[:, :n], in_=pp[:, :n]), [pp], [o])
                        P.dma('act', dst[nm][t0 + s * 128:t0 + (s + 1) * 128, a:a + n], o[:, :n], reads=[o])
    P.barrier()


import os
RET_STOP = int(os.environ.get('RET_STOP', '99'))
RET_SUB = int(os.environ.get('RET_SUB', '99'))
LBL = int(os.environ.get('LBL', '0'))


def phase_retention(P, S, src, mixed, C, cin):
    H = RET_H
    with ExitStack() as es:
        q = [P.sb(es, [128, 512], F32, 'rq') for _ in range(2)]
        k = [P.sb(es, [128, 512], F32, 'rk') for _ in range(2)]
        v = [P.sb(es, [128, 512], F32, 'rv') for _ in range(2)]
        gt = [P.sb(es, [128, 512], F32, 'rg') for _ in range(2)]
        cs = [P.sb(es, [128, 64], F32, 'rc') for _ in range(2)]
        sn = [P.sb(es, [128, 64], F32, 'rs') for _ in range(2)]
        qr = [P.sb(es, [128, 512], F32, 'rqr') for _ in range(2)]
        kr = [P.sb(es, [128, 512], F32, 'rkr') for _ in range(2)]
        kh = [P.sb(es, [128, 512], F32, 'rkh') for _ in range(2)]
        tA = P.sb(es, [128, 256], F32, 'rtA')
        tB = P.sb(es, [128, 256], F32, 'rtB')
        qT = [P.sb(es, [128, 128], F32, 'rqT') for _ in range(2)]
        qTd = [P.sb(es, [128, 128], F32, 'rqTd') for _ in range(2)]
        kT = [P.sb(es, [128, 128], F32, 'rkT') for _ in range(2)]
        AT = [P.sb(es, [128, 128], F32, 'rAT') for _ in range(2)]
        St = [P.sb(es, [128, 128], F32, 'rS') for _ in range(H)]
        oall = [P.sb(es, [128, 512], F32, 'ro') for _ in range(2)]
        st = P.sb(es, [128, H, 6], F32, 'rst')
        mv = P.sb(es, [128, H, 2], F32, 'rmv')
        rs = P.sb(es, [128, H], F32, 'rrs')
        pT = [P.ps(es, [128, 512], F32, 'rpT') for _ in range(2)]
        pA = [P.ps(es, [128, 512], F32, 'rpA') for _ in range(2)]
        pO = [P.ps(es, [128, 512], F32, 'rpO') for _ in range(2)]
        pS = [P.ps(es, [128, 512], F32, 'rpS') for _ in range(2)]
        for h in range(H):
            P.op('pool', lambda g, h=h: g.memset(St[h][:, :], 0.0), [], [St[h]])
        for a_ in range(2):
            P.op('pool', lambda g, a_=a_: g.memset(AT[a_][:, :], 0.0), [], [AT[a_]])
        ia = 0
        for it in range(S // 128):
            b = it % 2
            t0 = it * 128
            sl = slice(t0, t0 + 128)
            P.dma('sp', q[b][:, :], src['rq'][sl, :], writes=[q[b]])
            P.dma('sp', k[b][:, :], src['rk'][sl, :], writes=[k[b]])
            P.dma('sp', v[b][:, :], src['rv'][sl, :], writes=[v[b]])
            P.dma('sp', gt[b][:, :], src['rg'][sl, :], writes=[gt[b]])
            P.dma('sp', cs[b][:, :], cin['cos'][sl, :], writes=[cs[b]])
            P.dma('sp', sn[b][:, :], cin['sin'][sl, :], writes=[sn[b]])
            for (src_t, dst_t, eng) in ((q[b], qr[b], 'dve'), (k[b], kr[b], 'pool')):
                s4 = src_t[:, :].rearrange("p (h k two) -> p h k two", h=H, two=2)
                d4 = dst_t[:, :].rearrange("p (h k two) -> p h k two", h=H, two=2)
                t1, t2 = s4[:, :, :, 0], s4[:, :, :, 1]
                cosb = cs[b][:, :].unsqueeze(1).broadcast_to([128, H, 64])
                sinb = sn[b][:, :].unsqueeze(1).broadcast_to([128, H, 64])
                a3 = tA[:, :].rearrange("p (h k) -> p h k", h=H)
                b3 = tB[:, :].rearrange("p (h k) -> p h k", h=H)
                P.op(eng, lambda g, a3=a3, t1=t1, cosb=cosb: g.tensor_tensor(out=a3, in0=t1, in1=cosb, op=ALU.mult),
                     [src_t, cs[b]], [tA])
                P.op(eng, lambda g, b3=b3, t2=t2, sinb=sinb: g.tensor_tensor(out=b3, in0=t2, in1=sinb, op=ALU.mult),
                     [src_t, sn[b]], [tB])
                P.op(eng, lambda g, d4=d4, a3=a3, b3=b3: g.tensor_tensor(out=d4[:, :, :, 0], in0=a3, in1=b3,
                                                                        op=ALU.subtract), [tA, tB], [dst_t])
                P.op(eng, lambda g, a3=a3, t1=t1, sinb=sinb: g.tensor_tensor(out=a3, in0=t1, in1=sinb, op=ALU.mult),
                     [src_t, sn[b]], [tA])
                P.op(eng, lambda g, b3=b3, t2=t2, cosb=cosb: g.tensor_tensor(out=b3, in0=t2, in1=cosb, op=ALU.mult),
                     [src_t, cs[b]], [tB])
                P.op(eng, lambda g, d4=d4, a3=a3, b3=b3: g.tensor_tensor(out=d4[:, :, :, 1], in0=a3, in1=b3,
                                                                        op=ALU.add), [tA, tB], [dst_t])
            if RET_STOP == 1:
                P.dma('act', mixed[sl, 1536:2048], qr[b][:, :], reads=[qr[b]])
                P.dma('act', mixed[sl, 1024:1536], kr[b][:, :], reads=[kr[b]])
                continue
            P.op('pool', lambda g, b=b: g.tensor_tensor(out=kh[b][:, :], in0=kr[b][:, :], in1=C['retW'], op=ALU.mult),
                 [kr[b], C.t], [kh[b]])
            P.op('act', lambda g, b=b: g.mul(out=kr[b][:, :], in_=kr[b][:, :], mul=128 ** -0.5), [kr[b]], [kr[b]])
            if RET_STOP == 2:
                P.dma('act', mixed[sl, 1536:2048], kh[b][:, :], reads=[kh[b]])
                P.dma('act', mixed[sl, 1024:1536], kr[b][:, :], reads=[kr[b]])
                continue
            for h in range(H):
                hs = slice(h * 128, (h + 1) * 128)
                a = ia % 2
                ia += 1
                P.op('pe', lambda g, a=a, hs=hs: g.transpose(out=pT[a][:, 0:128], in_=qr[b][:, hs], identity=C['ident']),
                     [qr[b], C.t], [pT[a]], inc=False)
                P.op('pe', lambda g, a=a, hs=hs: g.transpose(out=pT[a][:, 128:256], in_=kr[b][:, hs], identity=C['ident']),
                     [kr[b], C.t], [pT[a]])
                if RET_SUB == 1:
                    continue
                P.op('act', lambda g, a=a: g.copy(out=qT[a][:, :], in_=pT[a][:, 0:128]), [pT[a]], [qT[a]])
                if RET_SUB == 2:
                    continue
                P.op('dve', lambda g, a=a, h=h: g.tensor_tensor(out=qTd[a][:, :], in0=C.sl('retG1', h * 128, (h + 1) * 128),
                                                               in1=qT[a][:, :], op=ALU.mult),
                     [qT[a], C.t], [qTd[a]])
                if RET_SUB == 3:
                    continue
                P.op('act', lambda g, a=a: g.copy(out=kT[a][:, :], in_=pT[a][:, 128:256]), [pT[a]], [kT[a]])
                if RET_STOP == 5:
                    P.op('dve', lambda g, a=a: g.tensor_copy(out=AT[a][:, :], in_=kT[a][:, :]), [kT[a], qT[a], qTd[a]], [AT[a]])
                    continue
                P.op('pe', lambda g, a=a: g.matmul(pA[a][:, 0:128], lhsT=kT[a][:, :], rhs=qT[a][:, :], start=True, stop=True),
                     [kT[a], qT[a]], [pA[a]])
                P.op('dve', lambda g, a=a, h=h: g.tensor_tensor(out=AT[a][:, :], in0=C.sl('retL', h * 128, (h + 1) * 128),
                                                               in1=pA[a][:, 0:128], op=ALU.mult),
                     [pA[a], C.t], [AT[a]])
                if RET_STOP == 4:
                    continue
                po = pO[b]
                P.op('pe', lambda g, a=a, hs=hs, po=po: g.matmul(po[:, hs], lhsT=AT[a][:, :], rhs=v[b][:, hs],
                                                                 start=True, stop=False), [AT[a], v[b]], [po], inc=False)
                P.op('pe', lambda g, a=a, hs=hs, po=po, h=h: g.matmul(po[:, hs], lhsT=qTd[a][:, :], rhs=St[h][:, :],
                                                                      start=False, stop=True), [qTd[a], St[h]], [po])
                P.op('pe', lambda g, a=a, hs=hs: g.matmul(pS[a][:, 0:128], lhsT=kh[b][:, hs], rhs=v[b][:, hs],
                                                          start=True, stop=True), [kh[b], v[b]], [pS[a]])
                P.op('dve', lambda g, a=a, h=h: g.scalar_tensor_tensor(
                    out=St[h][:, :], in0=St[h][:, :], scalar=RET_G128[h], in1=pS[a][:, 0:128], op0=ALU.mult, op1=ALU.add),
                    [St[h], pS[a]], [St[h]])
            if RET_STOP in (4, 5):
                P.dma('act', mixed[sl, 1536:1664], AT[0][:, :], reads=[AT[0]])
                P.dma('act', mixed[sl, 1664:1792], AT[1][:, :], reads=[AT[1]])
                continue
            o = oall[b]
            P.op('act', lambda g, o=o, b=b: g.copy(out=o[:, :], in_=pO[b][:, :]), [pO[b]], [o])
            if RET_STOP == 3:
                P.dma('act', mixed[sl, 1536:2048], o[:, :], reads=[o])
                continue
            for h in range(H):
                P.op('dve', lambda g, h=h, o=o: g.bn_stats(out=st[:, h, :], in_=o[:, h * 128:(h + 1) * 128]), [o], [st])
                P.op('dve', lambda g, h=h: g.bn_aggr(out=mv[:, h, :], in_=st[:, h, :]), [st], [mv])
            P.op('act', lambda g: g.activation(out=rs[:, :], in_=mv[:, :, 1], func=AF.Sqrt, bias=1e-5, scale=1.0),
                 [mv], [rs])
            P.op('dve', lambda g: g.reciprocal(out=rs[:, :], in_=rs[:, :]), [rs], [rs])
            o3 = o[:, :].rearrange("p (h d) -> p h d", h=H)
            P.op('dve', lambda g, o3=o3: g.tensor_tensor(out=o3, in0=o3, in1=mv[:, :, 0:1].broadcast_to([128, H, 128]),
                                                         op=ALU.subtract), [o, mv], [o])
            P.op('dve', lambda g, o3=o3: g.tensor_tensor(out=o3, in0=o3, in1=rs[:, :].unsqueeze(2).broadcast_to([128, H, 128]),
                                                         op=ALU.mult), [o, rs], [o])
            P.op('act', lambda g, b=b: g.activation(out=gt[b][:, :], in_=gt[b][:, :], func=AF.Silu), [gt[b]], [gt[b]])
            P.op('pool', lambda g, o=o, b=b: g.tensor_tensor(out=o[:, :], in0=o[:, :], in1=gt[b][:, :], op=ALU.mult),
                 [o, gt[b]], [o])
            P.dma('act', mixed[sl, 1536:2048], o[:, :], reads=[o])
    P.barrier()


def bc_load(P, es, vec_ap, n, name):
    t = P.sb(es, [128, n], F32, name)
    P.dma('sp', t[:, :], vec_ap.partition_broadcast(128), writes=[t])
    return t


def phase_ssd(P, S, src, mixed, C, W, l):
    with ExitStack() as es:
        cw = P.sb(es, [128, 4, 8], F32, 'scw')
        for k in range(4):
            P.dma('sp', cw[:, k, :], W['ssm_conv_w'][l, k, :].rearrange("(c p) -> p c", p=128), writes=[cw],
                  allow_slow_non_contiguous=True)
        cb = P.sb(es, [128, 8], F32, 'scb')
        P.dma('sp', cb[:, :], W['ssm_conv_b'][l, :].rearrange("(c p) -> p c", p=128), writes=[cb],
              allow_slow_non_contiguous=True)
        dtb = bc_load(P, es, W['ssm_dt_bias'][l, :], 8, 'sdtb')
        aneg = bc_load(P, es, W['ssm_a_log'][l, :], 8, 'san')
        dsk = bc_load(P, es, W['ssm_d'][l, :], 8, 'sdk')
        ng = bc_load(P, es, W['ssm_norm_g'][l, :], 512, 'sng')
        P.op('act', lambda g: g.activation(out=aneg[:, :], in_=aneg[:, :], func=AF.Exp), [aneg], [aneg])
        P.op('dve', lambda g: g.tensor_scalar(out=aneg[:, :], in0=aneg[:, :], scalar1=-1.0, scalar2=None, op0=ALU.mult),
             [aneg], [aneg])
        xin = [P.sb(es, [128, 8, 131], F32, 'sxin') for _ in range(2)]
        acc = P.sb(es, [128, 8, 128], F32, 'sacc')
        xc = [P.sb(es, [128, 8, 128], F32, 'sxc') for _ in range(2)]
        xtm = [P.sb(es, [128, 512], F32, 'sxtm') for _ in range(2)]
        btm = [P.sb(es, [128, 256], F32, 'sbtm') for _ in range(2)]
        dtr = [P.sb(es, [128, 8], F32, 'sdtr') for _ in range(2)]
        dtv = [P.sb(es, [128, 8], F32, 'sdtv') for _ in range(2)]
        loga = [P.sb(es, [128, 8], F32, 'sla') for _ in range(2)]
        bsb = P.sb(es, [128, 8], F32, 'sbsb')
        R = P.sb(es, [128, 8, 128], F32, 'sR')
        L = P.sb(es, [128, 8, 128], F32, 'sL')
        Dr = P.sb(es, [128, 8, 128], F32, 'sDr')
        AT = P.sb(es, [128, 8, 128], F32, 'sAT')
        Ct = P.sb(es, [128, 8, 128], F32, 'sCt')
        V = P.sb(es, [128, 8, 64], F32, 'sV')
        Vh = P.sb(es, [128, 8, 64], F32, 'sVh')
        Sall = P.sb(es, [128, 8, 64], F32, 'sS')
        zt = [P.sb(es, [128, 512], F32, 'sz') for _ in range(2)]
        y = [P.sb(es, [128, 512], F32, 'sy') for _ in range(2)]
        sq = P.sb(es, [128, 512], F32, 'ssq')
        ms = P.sb(es, [128, 2], F32, 'sms')
        pT = P.ps(es, [128, 512], F32, 'spT')
        pT2 = P.ps(es, [128, 512], F32, 'spT2')
        pb = P.ps(es, [128, 512], F32, 'spb')
        pbr = [P.ps(es, [128, 512], F32, 'spbr') for _ in range(2)]
        pG = P.ps(es, [128, 512], F32, 'spG')
        pO = P.ps(es, [128, 512], F32, 'spO')
        pS = P.ps(es, [128, 512], F32, 'spS')
        P.op('pool', lambda g: g.memset(Sall[:, :, :], 0.0), [], [Sall])
        for it in range(S // 128):
            b = it % 2
            t0 = it * 128
            sl = slice(t0, t0 + 128)
            xi = xin[b]
            srcv = src['sxbc'].ap().rearrange("(c p) t -> p c t", p=128)
            if t0 == 0:
                P.op('pool', lambda g, xi=xi: g.memset(xi[:, :, 0:3], 0.0), [], [xi])
                P.dma('sp', xi[:, :, 3:131], srcv[:, :, 0:128], writes=[xi])
            else:
                P.dma('sp', xi[:, :, :], srcv[:, :, t0 - 3:t0 + 128], writes=[xi])
            P.dma('sp', dtr[b][:, :], src['sdt'][sl, :], writes=[dtr[b]])
            P.dma('sp', zt[b][:, :], src['sz'][sl, :], writes=[zt[b]])
            for c in range(8):
                e = 'dve'
                P.op(e, lambda g, c=c: g.tensor_scalar(out=acc[:, c, :], in0=xi[:, c, 0:128], scalar1=cw[:, 0, c:c + 1],
                                                       scalar2=None, op0=ALU.mult), [xi, cw], [acc])
                for k in range(1, 4):
                    P.op(e, lambda g, c=c, k=k: g.scalar_tensor_tensor(
                        out=acc[:, c, :], in0=xi[:, c, k:k + 128], scalar=cw[:, k, c:c + 1], in1=acc[:, c, :],
                        op0=ALU.mult, op1=ALU.add), [xi, cw, acc], [acc])
            X = xc[b]
            for c in range(8):
                P.op('act', lambda g, c=c, X=X: g.activation(out=X[:, c, :], in_=acc[:, c, :], func=AF.Silu,
                                                            bias=cb[:, c:c + 1], scale=1.0), [acc, cb], [X])
            for c in range(4):
                P.op('pe', lambda g, c=c, X=X: g.transpose(out=pT[:, c * 128:(c + 1) * 128], in_=X[:, c, :],
                                                          identity=C['ident']), [X, C.t], [pT], inc=(c == 3))
            P.op('act', lambda g, b=b: g.copy(out=xtm[b][:, :], in_=pT[:, :]), [pT], [xtm[b]])
            for c in range(2):
                P.op('pe', lambda g, c=c, X=X: g.transpose(out=pT2[:, c * 128:(c + 1) * 128], in_=X[:, 4 + c, :],
                                                          identity=C['ident']), [X, C.t], [pT2], inc=(c == 1))
            P.op('act', lambda g, b=b: g.copy(out=btm[b][:, :], in_=pT2[:, 0:256]), [pT2], [btm[b]])
            P.op('dve', lambda g, b=b: g.tensor_tensor(out=dtv[b][:, :], in0=dtr[b][:, :], in1=dtb[:, :], op=ALU.add),
                 [dtr[b], dtb], [dtv[b]])
            P.op('act', lambda g, b=b: g.activation(out=dtv[b][:, :], in_=dtv[b][:, :], func=AF.Exp), [dtv[b]], [dtv[b]])
            P.op('act', lambda g, b=b: g.activation(out=dtv[b][:, :], in_=dtv[b][:, :], func=AF.Ln, bias=1.0, scale=1.0),
                 [dtv[b]], [dtv[b]])
            P.op('dve', lambda g, b=b: g.tensor_tensor(out=loga[b][:, :], in0=dtv[b][:, :], in1=aneg[:, :], op=ALU.mult),
                 [dtv[b], aneg], [loga[b]])
            P.op('pe', lambda g, b=b: g.matmul(pb[:, 0:8], lhsT=C['U'], rhs=loga[b][:, :], start=True, stop=True),
                 [C.t, loga[b]], [pb])
            P.op('act', lambda g: g.copy(out=bsb[:, :], in_=pb[:, 0:8]), [pb], [bsb])
            P.op('dve', lambda g, b=b: g.tensor_tensor(
                out=R[:, :, :], in0=C['U'].unsqueeze(1).broadcast_to([128, 8, 128]),
                in1=loga[b][:, :].unsqueeze(2).broadcast_to([128, 8, 128]), op=ALU.mult), [C.t, loga[b]], [R])
            for k in range(2):
                P.op('pe', lambda g, k=k: g.matmul(pbr[k][:, :], lhsT=C['ones'],
                                                   rhs=R[:, 4 * k:4 * k + 4, :].rearrange("p h i -> p (h i)"),
                                                   start=True, stop=True), [C.t, R], [pbr[k]])
            for k in range(2):
                P.op('dve', lambda g, k=k: g.tensor_tensor(
                    out=L[:, 4 * k:4 * k + 4, :], in0=pbr[k][:, :].rearrange("p (h i) -> p h i", h=4),
                    in1=bsb[:, 4 * k:4 * k + 4].unsqueeze(2).broadcast_to([128, 4, 128]), op=ALU.subtract),
                    [pbr[k], bsb], [L])
                P.op('act', lambda g, k=k: g.activation(out=Dr[:, 4 * k:4 * k + 4, :].rearrange("p h i -> p (h i)"),
                                                        in_=pbr[k][:, :], func=AF.Exp), [pbr[k]], [Dr])
            P.op('pool', lambda g: g.tensor_tensor(out=L[:, :, :], in0=L[:, :, :],
                                                   in1=C['negmask'].unsqueeze(1).broadcast_to([128, 8, 128]), op=ALU.add),
                 [L, C.t], [L])
            P.op('act', lambda g: g.activation(out=L[:, :, :], in_=L[:, :, :], func=AF.Exp), [L], [L])
            for g_ in range(2):
                P.op('pe', lambda g, g_=g_, X=X: g.matmul(pG[:, g_ * 128:(g_ + 1) * 128], lhsT=X[:, 4 + g_, :],
                                                          rhs=X[:, 6 + g_, :], start=True, stop=True), [X], [pG], inc=(g_ == 1))
            P.op('dve', lambda g: g.tensor_tensor(
                out=AT[:, :, :].rearrange("p (g h) i -> p g h i", g=2), in0=L[:, :, :].rearrange("p (g h) i -> p g h i", g=2),
                in1=pG[:, 0:256].rearrange("p (g i) -> p g i", g=2).unsqueeze(2).broadcast_to([128, 2, 4, 128]),
                op=ALU.mult), [L, pG], [AT])
            P.op('pool', lambda g, X=X: g.tensor_tensor(
                out=Ct[:, :, :].rearrange("p (g h) i -> p g h i", g=2), in0=Dr[:, :, :].rearrange("p (g h) i -> p g h i", g=2),
                in1=X[:, 6:8, :].unsqueeze(2).broadcast_to([128, 2, 4, 128]), op=ALU.mult), [Dr, X], [Ct])
            P.op('dve', lambda g, b=b: g.tensor_tensor(
                out=V[:, :, :], in0=xtm[b][:, :].rearrange("p (h d) -> p h d", h=8),
                in1=dtv[b][:, :].unsqueeze(2).broadcast_to([128, 8, 64]), op=ALU.mult), [xtm[b], dtv[b]], [V])
            P.op('pool', lambda g: g.tensor_tensor(out=Vh[:, :, :], in0=V[:, :, :],
                                                   in1=L[:, :, 127:128].broadcast_to([128, 8, 64]), op=ALU.mult),
                 [V, L], [Vh])
            for h in range(8):
                hs = slice(h * 64, (h + 1) * 64)
                P.op('pe', lambda g, h=h, hs=hs: g.matmul(pO[:, hs], lhsT=AT[:, h, :], rhs=V[:, h, :], start=True, stop=False),
                     [AT, V], [pO], inc=False)
                P.op('pe', lambda g, h=h, hs=hs: g.matmul(pO[:, hs], lhsT=Ct[:, h, :], rhs=Sall[:, h, :], start=False, stop=True),
                     [Ct, Sall], [pO], inc=(h == 7))
            for h in range(8):
                hs = slice(h * 64, (h + 1) * 64)
                P.op('pe', lambda g, h=h, hs=hs, b=b: g.matmul(pS[:, hs], lhsT=btm[b][:, (h // 4) * 128:(h // 4 + 1) * 128],
                                                              rhs=Vh[:, h, :], start=True, stop=True),
                     [btm[b], Vh], [pS], inc=(h == 7))
            P.op('dve', lambda g: g.tensor_tensor(out=Sall[:, :, :], in0=Sall[:, :, :],
                                                  in1=Dr[:, :, 127:128].broadcast_to([128, 8, 64]), op=ALU.mult),
                 [Sall, Dr], [Sall])
            P.op('dve', lambda g: g.tensor_tensor(out=Sall[:, :, :].rearrange("p h d -> p (h d)"),
                                                  in0=Sall[:, :, :].rearrange("p h d -> p (h d)"), in1=pS[:, :], op=ALU.add),
                 [Sall, pS], [Sall])
            Y = y[b]
            P.op('pool', lambda g, b=b, Y=Y: g.tensor_tensor(
                out=Y[:, :].rearrange("p (h d) -> p h d", h=8), in0=xtm[b][:, :].rearrange("p (h d) -> p h d", h=8),
                in1=dsk[:, :].unsqueeze(2).broadcast_to([128, 8, 64]), op=ALU.mult), [xtm[b], dsk], [Y])
            P.op('dve', lambda g, Y=Y: g.tensor_tensor(out=Y[:, :], in0=Y[:, :], in1=pO[:, :], op=ALU.add), [Y, pO], [Y])
            P.op('act', lambda g, b=b: g.activation(out=zt[b][:, :], in_=zt[b][:, :], func=AF.Silu), [zt[b]], [zt[b]])
            P.op('dve', lambda g, b=b, Y=Y: g.tensor_tensor(out=Y[:, :], in0=Y[:, :], in1=zt[b][:, :], op=ALU.mult),
                 [Y, zt[b]], [Y])
            group_rms(P, Y, sq, ms, 2, 256, 1e-6)
            P.op('pool', lambda g, Y=Y: g.tensor_tensor(out=Y[:, :], in0=Y[:, :], in1=ng[:, :], op=ALU.mult), [Y, ng], [Y])
            P.dma('act', mixed[sl, 1024:1536], Y[:, :], reads=[Y])
        P.barrier()


def group_rms(P, Y, sq, ms, ng, gw, eps):
    n = ng * gw
    P.op('pool', lambda g: g.tensor_tensor(out=sq[:, :n], in0=Y[:, :n], in1=Y[:, :n], op=ALU.mult), [Y], [sq])
    P.op('dve', lambda g: g.tensor_reduce(out=ms[:, :ng], in_=sq[:, :n].rearrange("p (g d) -> p g d", g=ng), axis=AX.X,
                                          op=ALU.add), [sq], [ms])
    P.op('act', lambda g: g.activation(out=ms[:, :ng], in_=ms[:, :ng], func=AF.Sqrt, bias=eps, scale=1.0 / gw), [ms], [ms])
    P.op('dve', lambda g: g.reciprocal(out=ms[:, :ng], in_=ms[:, :ng]), [ms], [ms])
    P.op('dve', lambda g: g.tensor_tensor(out=Y[:, :n].rearrange("p (g d) -> p g d", g=ng),
                                          in0=Y[:, :n].rearrange("p (g d) -> p g d", g=ng),
                                          in1=ms[:, :ng].unsqueeze(2).broadcast_to([128, ng, gw]), op=ALU.mult), [Y, ms], [Y])


def phase_hgrn(P, S, src, mixed, C, W, l, depth):
    H = 4
    with ExitStack() as es:
        lg = P.sb(es, [128, depth, H], F32, 'hlg')
        for m in range(depth):
            P.dma('sp', lg[:, m, :], W['hgrn_lb_logits'][m, :].rearrange("(h p) -> p h", p=128), writes=[lg],
                  allow_slow_non_contiguous=True)
        lb = P.sb(es, [128, H], F32, 'hlb')
        oml = P.sb(es, [128, H], F32, 'homl')
        ssum = P.sb(es, [128, H], F32, 'hss')
        P.op('act', lambda g: g.activation(out=lg[:, :, :], in_=lg[:, :, :], func=AF.Exp), [lg], [lg])
        P.op('pool', lambda g: g.memset(lb[:, :], 0.0), [], [lb])
        P.op('pool', lambda g: g.memset(ssum[:, :], 0.0), [], [ssum])
        for m in range(depth):
            P.op('dve', lambda g, m=m: g.tensor_tensor(out=ssum[:, :], in0=ssum[:, :], in1=lg[:, m, :], op=ALU.add),
                 [ssum, lg], [ssum])
            if 1 <= m <= l:
                P.op('dve', lambda g, m=m: g.tensor_tensor(out=lb[:, :], in0=lb[:, :], in1=lg[:, m, :], op=ALU.add),
                     [lb, lg], [lb])
        P.op('dve', lambda g: g.reciprocal(out=ssum[:, :], in_=ssum[:, :]), [ssum], [ssum])
        P.op('dve', lambda g: g.tensor_tensor(out=lb[:, :], in0=lb[:, :], in1=ssum[:, :], op=ALU.mult), [lb, ssum], [lb])
        P.op('dve', lambda g: g.tensor_scalar(out=oml[:, :], in0=lb[:, :], scalar1=-1.0, scalar2=1.0, op0=ALU.mult,
                                              op1=ALU.add), [lb], [oml])
        ng = bc_load(P, es, W['hgrn_norm_g'][l, :], 512, 'hng')
        qf = [P.sb(es, [128, H, 128], F32, 'hq') for _ in range(2)]
        zf = [P.sb(es, [128, H, 128], F32, 'hz') for _ in range(2)]
        vt = [P.sb(es, [128, 512], F32, 'hv') for _ in range(2)]
        gt = [P.sb(es, [128, 512], F32, 'hg') for _ in range(2)]
        sig = P.sb(es, [128, H, 128], F32, 'hsig')
        logf = P.sb(es, [128, H, 128], F32, 'hlf')
        kf = P.sb(es, [128, H, 128], F32, 'hkf')
        bT = P.sb(es, [128, H, 128], F32, 'hbT')
        EB = P.sb(es, [128, H, 128], F32, 'hEB')
        EBn = P.sb(es, [128, H, 128], F32, 'hEBn')
        qtil = P.sb(es, [128, H, 128], F32, 'hqt')
        ktil = P.sb(es, [128, H, 128], F32, 'hkt')
        lftm = P.sb(es, [128, 512], F32, 'hlftm')
        ed = P.sb(es, [128, 512], F32, 'hed')
        khat = P.sb(es, [128, 512], F32, 'hkh')
        khc = [P.sb(es, [128, 512], F32, 'hkhc') for _ in range(4)]
        qz = [P.sb(es, [128, H, 128], F32, 'hqz') for _ in range(4)]
        AT = P.sb(es, [128, H, 128], F32, 'hAT')
        Sr = [P.sb(es, [128, H, 128], F32, 'hS') for _ in range(5)]
        o1 = P.sb(es, [128, 512], F32, 'ho1')
        o = [P.sb(es, [128, 512], F32, 'ho') for _ in range(2)]
        sq = P.sb(es, [128, 512], F32, 'hsq')
        ms = P.sb(es, [128, 4], F32, 'hms')
        pTl = P.ps(es, [128, 512], F32, 'hpTl')
        pTk = P.ps(es, [128, 512], F32, 'hpTk')
        pD = P.ps(es, [128, 512], F32, 'hpD')
        pA = P.ps(es, [128, 512], F32, 'hpA')
        pO1 = P.ps(es, [128, 512], F32, 'hpO1')
        pO2 = P.ps(es, [128, 512], F32, 'hpO2')
        pKV = [P.ps(es, [128, 512], F32, 'hpKV') for _ in range(2)]
        for c in range(4):
            P.op('pool', lambda g, c=c: g.memset(qz[c][:, :, :], 0.0), [], [qz[c]])
        P.op('pool', lambda g: g.memset(Sr[0][:, :, :], 0.0), [], [Sr[0]])
        isr = 0
        ikv = 0
        for it in range(S // 128):
            b = it % 2
            t0 = it * 128
            sl = slice(t0, t0 + 128)
            P.dma('sp', qf[b][:, :, :], src['hq'].ap().rearrange("(h p) t -> p h t", p=128)[:, :, sl], writes=[qf[b]])
            P.dma('sp', zf[b][:, :, :], src['hfF'].ap().rearrange("(h p) t -> p h t", p=128)[:, :, sl], writes=[zf[b]])
            P.dma('sp', vt[b][:, :], src['hi'][sl, :], writes=[vt[b]])
            P.dma('sp', gt[b][:, :], src['hg'][sl, :], writes=[gt[b]])
            Z, Q = zf[b], qf[b]
            lbb = lb[:, :].unsqueeze(2).broadcast_to([128, H, 128])
            omb = oml[:, :].unsqueeze(2).broadcast_to([128, H, 128])
            P.op('act', lambda g, Z=Z: g.activation(out=sig[:, :, :], in_=Z[:, :, :], func=AF.Sigmoid), [Z], [sig])
            P.op('dve', lambda g: g.tensor_tensor(out=sig[:, :, :], in0=sig[:, :, :], in1=omb, op=ALU.mult), [sig, oml], [sig])
            P.op('dve', lambda g: g.tensor_tensor(out=sig[:, :, :], in0=sig[:, :, :], in1=lbb, op=ALU.add), [sig, lb], [sig])
            P.op('act', lambda g: g.activation(out=logf[:, :, :], in_=sig[:, :, :], func=AF.Ln), [sig], [logf])
            P.op('act', lambda g, Z=Z: g.activation(out=kf[:, :, :], in_=Z[:, :, :], func=AF.Sigmoid, scale=-1.0), [Z], [kf])
            P.op('pool', lambda g: g.tensor_tensor(out=kf[:, :, :], in0=kf[:, :, :], in1=omb, op=ALU.mult), [kf, oml], [kf])
            P.op('dve', lambda g: g.tensor_tensor_scan(out=bT[:, :, :].rearrange("p h t -> p (h t)"), data0=C['notstart'],
                                                       data1=logf[:, :, :].rearrange("p h t -> p (h t)"), initial=0.0,
                                                       op0=ALU.mult, op1=ALU.add), [logf, C.t], [bT])
            P.op('act', lambda g: g.activation(out=EB[:, :, :], in_=bT[:, :, :], func=AF.Exp), [bT], [EB])
            P.op('act', lambda g: g.activation(out=EBn[:, :, :], in_=bT[:, :, :], func=AF.Exp, scale=-1.0), [bT], [EBn])
            P.op('act', lambda g, Q=Q: g.activation(out=Q[:, :, :], in_=Q[:, :, :], func=AF.Silu), [Q], [Q])
            P.op('dve', lambda g, Q=Q: g.tensor_tensor(out=qtil[:, :, :], in0=Q[:, :, :], in1=EB[:, :, :], op=ALU.mult),
                 [Q, EB], [qtil])
            P.op('pool', lambda g: g.tensor_tensor(out=ktil[:, :, :], in0=kf[:, :, :], in1=EBn[:, :, :], op=ALU.mult),
                 [kf, EBn], [ktil])
            for c in range(4):
                cs = slice(32 * c, 32 * c + 32)
                P.op('pool', lambda g, c=c, cs=cs: g.tensor_copy(out=qz[c][:, :, cs], in_=qtil[:, :, cs]), [qtil], [qz[c]])
            for h in range(H):
                P.op('pe', lambda g, h=h: g.transpose(out=pTl[:, h * 128:(h + 1) * 128], in_=logf[:, h, :], identity=C['ident']),
                     [logf, C.t], [pTl], inc=(h == H - 1))
            for h in range(H):
                P.op('pe', lambda g, h=h: g.transpose(out=pTk[:, h * 128:(h + 1) * 128], in_=kf[:, h, :], identity=C['ident']),
                     [kf, C.t], [pTk], inc=(h == H - 1))
            P.op('act', lambda g: g.copy(out=lftm[:, :], in_=pTl[:, :]), [pTl], [lftm])
            P.op('pe', lambda g: g.matmul(pD[:, :], lhsT=C['V32'], rhs=lftm[:, :], start=True, stop=True), [C.t, lftm], [pD])
            P.op('act', lambda g: g.activation(out=ed[:, :], in_=pD[:, :], func=AF.Exp), [pD], [ed])
            P.op('dve', lambda g: g.tensor_tensor(out=khat[:, :], in0=ed[:, :], in1=pTk[:, :], op=ALU.mult), [ed, pTk], [khat])
            for c in range(4):
                P.op('pool', lambda g, c=c: g.tensor_scalar(out=khc[c][:, :], in0=khat[:, :], scalar1=C.sl('bm', c, c + 1),
                                                            scalar2=None, op0=ALU.mult), [khat, C.t], [khc[c]])
            for h in range(H):
                P.op('pe', lambda g, h=h: g.matmul(pA[:, h * 128:(h + 1) * 128], lhsT=ktil[:, h, :], rhs=qtil[:, h, :],
                                                   start=True, stop=True), [ktil, qtil], [pA], inc=(h == H - 1))
            P.op('dve', lambda g: g.tensor_tensor(out=AT[:, :, :], in0=C['M32'].unsqueeze(1).broadcast_to([128, H, 128]),
                                                  in1=pA[:, :].rearrange("p (h i) -> p h i", h=H), op=ALU.mult),
                 [C.t, pA], [AT])
            for h in range(H):
                hs = slice(h * 128, (h + 1) * 128)
                P.op('pe', lambda g, h=h, hs=hs, b=b: g.matmul(pO1[:, hs], lhsT=AT[:, h, :], rhs=vt[b][:, hs], start=True,
                                                              stop=True), [AT, vt[b]], [pO1], inc=(h == H - 1))
            P.op('act', lambda g: g.copy(out=o1[:, :], in_=pO1[:, :]), [pO1], [o1])
            Ss = [Sr[(isr + c) % 5] for c in range(5)]
            isr += 4
            for c in range(4):
                Sc, Sn = Ss[c], Ss[c + 1]
                kv = pKV[ikv % 2]
                ikv += 1
                for h in range(H):
                    hs = slice(h * 128, (h + 1) * 128)
                    P.op('pe', lambda g, h=h, hs=hs, c=c, kv=kv, b=b: g.matmul(kv[:, hs], lhsT=khc[c][:, hs], rhs=vt[b][:, hs],
                                                                             start=True, stop=True),
                         [khc[c], vt[b]], [kv], inc=(h == H - 1))
                dec = EB[:, :, 32 * c + 31:32 * c + 32].broadcast_to([128, H, 128])
                P.op('dve', lambda g, Sc=Sc, Sn=Sn, dec=dec: g.tensor_tensor(out=Sn[:, :, :], in0=Sc[:, :, :], in1=dec,
                                                                             op=ALU.mult), [Sc, EB], [Sn])
                P.op('dve', lambda g, Sn=Sn, kv=kv: g.tensor_tensor(out=Sn[:, :, :].rearrange("p h v -> p (h v)"),
                                                                    in0=Sn[:, :, :].rearrange("p h v -> p (h v)"),
                                                                    in1=kv[:, :], op=ALU.add), [Sn, kv], [Sn])
            for h in range(H):
                hs = slice(h * 128, (h + 1) * 128)
                for c in range(4):
                    P.op('pe', lambda g, h=h, hs=hs, c=c: g.matmul(pO2[:, hs], lhsT=qz[c][:, h, :], rhs=Ss[c][:, h, :],
                                                                 start=(c == 0), stop=(c == 3)),
                         [qz[c], Ss[c]], [pO2], inc=(h == H - 1 and c == 3))
            O = o[b]
            P.op('dve', lambda g, O=O: g.tensor_tensor(out=O[:, :], in0=o1[:, :], in1=pO2[:, :], op=ALU.add), [o1, pO2], [O])
            group_rms(P, O, sq, ms, 4, 128, 1e-6)
            P.op('pool', lambda g, O=O: g.tensor_tensor(out=O[:, :], in0=O[:, :], in1=ng[:, :], op=ALU.mult), [O, ng], [O])
            P.op('act', lambda g, b=b: g.activation(out=gt[b][:, :], in_=gt[b][:, :], func=AF.Silu), [gt[b]], [gt[b]])
            P.op('pool', lambda g, O=O, b=b: g.tensor_tensor(out=O[:, :], in0=O[:, :], in1=gt[b][:, :], op=ALU.mult),
                 [O, gt[b]], [O])
            P.dma('act', mixed[sl, 0:512], O[:, :], reads=[O])
        P.barrier()


NB = 8192
NB0 = 2560
NSA_TOP = 16


def _t5_bucket_np(n):
    n = np.maximum(n, 0)
    nf = np.maximum(n, 1).astype(np.float32)
    large = 16 + (np.log(nf / np.float32(16)) / np.float32(np.log(2048 / 16)) * np.float32(16)).astype(np.int32)
    return np.where(n < 16, n, np.minimum(large, 31))


def nsa_host_consts(S):
    dist = np.arange(NB) - NB0
    bk = _t5_bucket_np(dist)
    oh = np.zeros((33, NB), np.float32)
    oh[bk, np.arange(NB)] = 1.0
    oh[:, dist < 0] = 0.0
    oh[32, dist < 0] = 1.0
    ohw = oh.copy()
    ohw[:, dist >= 512] = 0.0
    ohw[32, dist >= 512] = 1.0
    nbp = S // 16
    nslc = S // 64
    wt = np.zeros((nbp, nslc), np.float32)
    for j in range(nslc):
        for m, w in ((4 * j - 1, 0.5), (4 * j, 1.0), (4 * j + 1, 1.0), (4 * j + 2, 1.0), (4 * j + 3, 0.5)):
            if 0 <= m < nbp - 1:
                wt[m, j] = w
    return {'c_oh': oh, 'c_ohw': ohw, 'c_wt': wt}


def phase_nsa_tables(P, rel_bias, cin, fh, fhw):
    with ExitStack() as es:
        tb = P.sb(es, [33, 8], F32, 'ntb')
        r31 = P.sb(es, [32, 8], F32, 'nr31')
        oh = P.sb(es, [33, NB], F32, 'noh')
        ob = [P.sb(es, [8, 512], F32, 'nob') for _ in range(2)]
        pp = [P.ps(es, [128, 512], F32, 'npp') for _ in range(2)]
        P.op('pool', lambda g: g.memset(tb[:, :], -30000.0), [], [tb])
        P.dma('sp', tb[0:32, :], rel_bias[:, :], writes=[tb])
        P.dma('sp', r31[:, :], rel_bias[31, :].partition_broadcast(32), writes=[r31])
        P.op('dve', lambda g: g.tensor_tensor(out=tb[0:32, :], in0=tb[0:32, :], in1=r31[:, :], op=ALU.subtract),
             [tb, r31], [tb])
        i = 0
        for (src, dst) in ((cin['oh'], fh), (cin['ohw'], fhw)):
            for c0 in range(0, NB, 2048):
                P.dma('sp', oh[:, c0:c0 + 2048], src[:, c0:c0 + 2048], writes=[oh])
            for n0 in range(0, NB, 512):
                p_, o_ = pp[i % 2], ob[i % 2]
                i += 1
                P.op('pe', lambda g, p_=p_, n0=n0: g.matmul(p_[0:8, :], lhsT=tb[:, :], rhs=oh[:, n0:n0 + 512], start=True,
                                                            stop=True), [tb, oh], [p_])
                P.op('act', lambda g, p_=p_, o_=o_: g.copy(out=o_[:, :], in_=p_[0:8, :]), [p_], [o_])
                P.dma('act', dst[:, n0:n0 + 512], o_[:, :], reads=[o_])
        P.barrier()


def phase_nsa_compress(P, S, src, W, l, kcT, vc):
    NBP = S // 16
    nblk = NBP - 1
    for kv in range(2):
        with ExitStack() as es:
            nm = 'k' if kv == 0 else 'v'
            srcF = src['nkc'] if kv == 0 else src['nvc']
            w1s = P.sb(es, [64, 32, 256], F32, 'cw1s')
            w1b = P.sb(es, [64, 32, 256], BF16, 'cw1b')
            w2s = P.sb(es, [128, 2, 64], F32, 'cw2s')
            w2b = P.sb(es, [128, 2, 64], BF16, 'cw2b')
            peT = P.sb(es, [64, 32], F32, 'cpe')
            P.dma('sp', w1s[:, :, :], W['nsa_w1_' + nm][l, :, :].rearrange("(t d) h -> d t h", d=64), writes=[w1s])
            P.dma('sp', w2s[:, :, :], W['nsa_w2_' + nm][l, :, :].rearrange("(c p) d -> p c d", p=128), writes=[w2s])
            P.dma('sp', peT[:, :], W['nsa_pe_' + nm][l, :, :].rearrange("t d -> d t"), writes=[peT],
                  allow_slow_non_contiguous=True)
            for t8 in range(4):
                cast_rr(P, t8, w1b[:, t8 * 8:(t8 + 1) * 8, :], w1s[:, t8 * 8:(t8 + 1) * 8, :], [w1s], [w1b])
            P.op('dve', lambda g: g.tensor_copy(out=w2b[:, :, :], in_=w2s[:, :, :]), [w2s], [w2b])
            stg = [P.sb(es, [64, 2048], F32, 'cstg') for _ in range(2)]
            kA = P.sb(es, [64, S], BF16, 'ckA')
            kB = P.sb(es, [64, S], BF16, 'ckB')
            hT = [P.sb(es, [128, 2, 512], BF16, 'chT') for _ in range(2)]
            ko = [P.sb(es, [64, 512], F32, 'cko') for _ in range(2)]
            vo = [P.sb(es, [128, 64], F32, 'cvo') for _ in range(2)]
            zz = P.sb(es, [128, 64], F32, 'czz')
            ph = [P.ps(es, [128, 512], F32, 'cph') for _ in range(4)]
            pk = [P.ps(es, [128, 512], F32, 'cpk') for _ in range(2)]
            P.op('pool', lambda g: g.memset(zz[:, :], 0.0), [], [zz])
            iv = 0
            for g_ in range(2):
                SEG = min(2048, S)
                for i0 in range(0, S, SEG):
                    st = stg[(i0 // SEG) % 2]
                    P.dma('sp', st[:, :SEG], srcF[64 * g_:64 * g_ + 64, i0:i0 + SEG], writes=[st])
                    for (dst, po, e) in ((kA, 0, 'dve'), (kB, 16, 'pool')):
                        P.op(e, lambda g, dst=dst, po=po, st=st, i0=i0: g.tensor_tensor(
                            out=dst[:, i0:i0 + SEG].rearrange("d (m r) -> d m r", r=16),
                            in0=st[:, :SEG].rearrange("d (m r) -> d m r", r=16),
                            in1=peT[:, po:po + 16].unsqueeze(1).broadcast_to([64, SEG // 16, 16]), op=ALU.add),
                            [st, peT], [dst])
                kA3 = kA[:, :].rearrange("d (m r) -> d m r", r=16)
                kB3 = kB[:, :].rearrange("d (m r) -> d m r", r=16)
                for ib, b0 in enumerate(range(0, nblk, 512)):
                    N = min(512, nblk - b0)
                    H_ = hT[ib % 2]
                    for hc in range(2):
                        p_ = ph[(2 * ib + hc) % 4]
                        for t in range(32):
                            rhs = kA3[:, b0:b0 + N, t] if t < 16 else kB3[:, b0 + 1:b0 + 1 + N, t - 16]
                            P.op('pe', lambda g, p_=p_, t=t, hc=hc, rhs=rhs, N=N: g.matmul(
                                p_[:, :N], lhsT=w1b[:, t, hc * 128:(hc + 1) * 128], rhs=rhs, start=(t == 0), stop=(t == 31)),
                                [w1b, kA, kB], [p_], inc=(t == 31))
                        P.op('act', lambda g, p_=p_, hc=hc, H_=H_, N=N: g.activation(out=H_[:, hc, :N], in_=p_[:, :N],
                                                                                    func=AF.Silu), [p_], [H_])
                    if kv == 0:
                        q_ = pk[ib % 2]
                        for hc in range(2):
                            P.op('pe', lambda g, q_=q_, hc=hc, H_=H_, N=N: g.matmul(q_[0:64, :N], lhsT=w2b[:, hc, :],
                                                                                 rhs=H_[:, hc, :N], start=(hc == 0),
                                                                                 stop=(hc == 1)), [w2b, H_], [q_], inc=(hc == 1))
                        o_ = ko[ib % 2]
                        P.op('dve', lambda g, q_=q_, o_=o_, N=N: g.tensor_copy(out=o_[:, :N], in_=q_[0:64, :N]), [q_], [o_])
                        P.dma('act', kcT[64 * g_:64 * g_ + 64, b0:b0 + N], o_[:, :N], reads=[o_])
                    else:
                        for s0 in range(0, N, 128):
                            n = min(128, N - s0)
                            q_ = pk[iv % 2]
                            o_ = vo[iv % 2]
                            iv += 1
                            for hc in range(2):
                                P.op('pe', lambda g, q_=q_, hc=hc, H_=H_, s0=s0, n=n: g.matmul(
                                    q_[:n, 0:64], lhsT=H_[:, hc, s0:s0 + n], rhs=w2b[:, hc, :], start=(hc == 0), stop=(hc == 1)),
                                    [w2b, H_], [q_], inc=(hc == 1))
                            P.op('dve', lambda g, q_=q_, o_=o_, n=n: g.tensor_copy(out=o_[:n, :], in_=q_[:n, 0:64]), [q_], [o_])
                            P.dma('act', vc[b0 + s0:b0 + s0 + n, 64 * g_:64 * g_ + 64], o_[:n, :], reads=[o_])
                if kv == 0:
                    P.dma('act', kcT[64 * g_:64 * g_ + 64, nblk:NBP], zz[0:64, 0:1], reads=[zz], allow_slow_non_contiguous=True)
                else:
                    P.dma('act', vc[nblk:NBP, 64 * g_:64 * g_ + 64], zz[0:1, :], reads=[zz])
            P.barrier()


def phase_nsa_attn(P, S, src, W, l, C, cin, kcT_d, vc_d, fh, fhw, mixed):
    NQ = S // 128
    NBP = S // 16
    NCC = max(1, NBP // 128)
    NSLC = S // 64
    ntop = min(NSA_TOP, NSLC)
    ND = 14
    with ExitStack() as es:
        ng = bc_load(P, es, W['nsa_norm_g'][l, :], 512, 'ang')
        Jb = P.sb(es, [128, 128], BF16, 'aJb')
        idb = P.sb(es, [128, 128], BF16, 'aidb')
        P.op('dve', lambda g: g.tensor_copy(out=Jb[:, :], in_=C['J']), [C.t], [Jb])
        P.op('dve', lambda g: g.tensor_copy(out=idb[:, :], in_=C['ident']), [C.t], [idb])
        I4 = P.sb(es, [128, 512], BF16, 'aI4')
        for q4 in range(4):
            P.op('dve', lambda g, q4=q4: g.tensor_copy(out=I4[:, q4 * 128:(q4 + 1) * 128], in_=C['ident']), [C.t], [I4])
        ksT = P.sb(es, [128, S], BF16, 'aks')
        vs1 = P.sb(es, [128, NQ, 2, 65], BF16, 'avs')
        kcT = P.sb(es, [128, NBP], F32, 'akc')
        vc1 = P.sb(es, [128, NCC, 2, 65], F32, 'avc')
        wt = P.sb(es, [128, NCC, NSLC], F32, 'awt')
        stg = [P.sb(es, [128, 2048], F32, 'astg') for _ in range(2)]
        P.op('pool', lambda g: g.memset(vs1[:, :, :, :], 1.0), [], [vs1])
        P.op('pool', lambda g: g.memset(vc1[:, :, :, :], 1.0), [], [vc1])
        SEG = min(2048, S)
        for i, i0 in enumerate(range(0, S, SEG)):
            st = stg[i % 2]
            P.dma('sp', st[:, :SEG], src['nks'][:, i0:i0 + SEG], writes=[st])
            cast_rr(P, i, ksT[:, i0:i0 + SEG], st[:, :SEG], [st], [ksT])
        for i, i0 in enumerate(range(0, S, 2048)):
            n = min(2048, S - i0)
            st = stg[i % 2]
            P.dma('sp', st[:, :n // 128 * 128].rearrange("p (c f) -> p c f", f=128),
                  src['nvs'].ap().rearrange("(c p) f -> p c f", p=128)[:, i0 // 128:(i0 + n) // 128, :], writes=[st])
            P.op('dve', lambda g, st=st, i0=i0, n=n: g.tensor_copy(
                out=vs1[:, i0 // 128:(i0 + n) // 128, :, 0:64],
                in_=st[:, :n].rearrange("p (c g d) -> p c g d", g=2, d=64)), [st], [vs1])
        P.dma('sp', kcT[:, :], kcT_d[:, :], writes=[kcT])
        if NBP >= 128:
            for g_ in range(2):
                P.dma('sp', vc1[:, :, g_, 0:64], vc_d.ap()[:, 64 * g_:64 * g_ + 64].rearrange("(c p) d -> p c d", p=128),
                      writes=[vc1])
            P.dma('sp', wt[:, :, :], cin['wt'].ap().rearrange("(c p) j -> p c j", p=128), writes=[wt])
        else:
            P.op('pool', lambda g: g.memset(wt[:, :, :], 0.0), [], [wt])
            P.op('pool', lambda g: g.memset(kcT[:, :], 0.0), [], [kcT])
            P.dma('sp', kcT[:, :NBP], kcT_d[:, :], writes=[kcT])
            P.dma('sp', vc1[:NBP, 0, :, 0:64], vc_d.ap().rearrange("p (g d) -> p g d", g=2), writes=[vc1])
            P.dma('sp', wt[:NBP, 0, :], cin['wt'][:, :], writes=[wt])
        hks = [[P.sb(es, [128, 4, 128], BF16, 'ahk') for _ in range(2)] for _ in range(ND)]
        hkw = [[P.sb(es, [128, 4, 128], BF16, 'ahkw') for _ in range(2)] for _ in range(5)]
        hst = [P.sb(es, [128, 4, 128], F32, 'ahst') for _ in range(2)]
        i = 0
        for (tiles, tab) in ((hks, fh), (hkw, fhw)):
            for dl, pair in enumerate(tiles):
                for g_ in range(2):
                    h_ = hst[i % 2]
                    for hh in range(4):
                        off = (4 * g_ + hh) * NB + NB0 + 128 * dl - 127
                        P.dma('sp', h_[:, hh, :], bass.AP(tensor=tab, offset=off, ap=[[1, 128], [1, 128]]), writes=[h_])
                    cast_rr(P, i, pair[g_][:, :, :], h_[:, :, :], [h_], [pair[g_]])
                    i += 1
        q32 = [P.sb(es, [128, 4, 128], F32, 'aq32') for _ in range(2)]
        qb = [P.sb(es, [128, 4, 128], BF16, 'aqb') for _ in range(2)]
        gsb = [P.sb(es, [128, 24], F32, 'ags') for _ in range(2)]
        kws = [P.sb(es, [128, 128], F32, 'akws') for _ in range(2)]
        vws = [P.sb(es, [128, 128], F32, 'avws') for _ in range(2)]
        kw = [P.sb(es, [128, 128], BF16, 'akw') for _ in range(6)]
        vw = [P.sb(es, [128, 2, 65], BF16, 'avw') for _ in range(6)]
        for t in vw:
            P.op('pool', lambda g, t=t: g.memset(t[:, :, :], 1.0), [], [t])
        PC = [P.sb(es, [128, 512], F32, 'aPC') for _ in range(min(NCC, 8))]
        hk32 = [P.sb(es, [128, 4, 128], F32, 'ahk32') for _ in range(2)]
        PT = [P.sb(es, [128, 512], BF16, 'aPT') for _ in range(4)]
        Mx = [P.sb(es, [128, 128], BF16, 'aMx') for _ in range(4)]
        Mb = [P.sb(es, [128, NSLC], BF16, 'aMb') for _ in range(2)]
        rd = P.sb(es, [128, 4], F32, 'ard')
        cf = P.sb(es, [128, 4], F32, 'acf')
        sc = P.sb(es, [128, NSLC], F32, 'asc')
        sc2 = P.sb(es, [128, NSLC], F32, 'asc2')
        m8 = P.sb(es, [128, 8], F32, 'am8')
        mk = [P.sb(es, [128, NSLC], BF16, 'amk') for _ in range(2)]
        oall = [P.sb(es, [128, 8, 64], F32, 'aoa') for _ in range(2)]
        sq = P.sb(es, [128, 512], F32, 'asq')
        ms = P.sb(es, [128, 2], F32, 'ams')
        pS = [P.ps(es, [128, 512], F32, 'apS') for _ in range(4)]
        pO = [P.ps(es, [128, 512], F32, 'apO') for _ in range(4)]
        isc = 0
        ihk = 0
        for T in range(NQ):
            if max(P.cnt.values()) > 30000 or max(d['n'] for d in P.dq.values()) > 15000:
                P.barrier()
            b = T % 2
            t0 = T * 128
            sl = slice(t0, t0 + 128)
            for g_ in range(2):
                P.dma('sp', q32[b][64 * g_:64 * g_ + 64, :, :],
                      src['nq'].ap()[256 * g_:256 * g_ + 256, sl].rearrange("(h d) t -> d h t", d=64), writes=[q32[b]])
            P.dma('sp', gsb[b][:, :], src['ngate'][sl, :], writes=[gsb[b]])
            P.dma('sp', kws[b][:, :], src['nkw'][:, sl], writes=[kws[b]])
            P.dma('sp', vws[b][:, :], src['nvw'][sl, :], writes=[vws[b]])
            P.op('act', lambda g, b=b: g.mul(out=q32[b][:, :, :], in_=q32[b][:, :, :], mul=0.125), [q32[b]], [q32[b]])
            P.op('dve', lambda g, b=b: g.tensor_copy(out=qb[b][:, :, :], in_=q32[b][:, :, :]), [q32[b]], [qb[b]])
            P.op('act', lambda g, b=b: g.activation(out=gsb[b][:, :], in_=gsb[b][:, :], func=AF.Sigmoid), [gsb[b]], [gsb[b]])
            P.op('pool', lambda g, b=b, T=T: g.tensor_copy(out=kw[T % 6][:, :], in_=kws[b][:, :]), [kws[b]], [kw[T % 6]])
            P.op('pool', lambda g, b=b, T=T: g.tensor_copy(out=vw[T % 6][:, :, 0:64],
                                                          in_=vws[b][:, :].rearrange("p (g d) -> p g d", g=2)),
                 [vws[b]], [vw[T % 6]])
            OA = oall[b]
            gs3 = gsb[b][:, :].rearrange("p (h k) -> p h k", k=3)
            for g_ in range(2):
                ks = slice(64 * g_, 64 * g_ + 64)
                Q32 = q32[b][ks, :, :].rearrange("d h t -> d (h t)")
                QB = qb[b][ks, :, :].rearrange("d h t -> d (h t)")
                cmax = min(NCC - 1, (8 * T + 6) // 128)
                for c in range(cmax + 1):
                    p_ = pS[isc % 4]
                    isc += 1
                    near = c >= cmax - 1
                    P.op('pe', lambda g, p_=p_, c=c, ks=ks, Q32=Q32, near=near: g.matmul(
                        p_[:, :], lhsT=kcT[ks, c * 128:(c + 1) * 128], rhs=Q32, start=True, stop=(not near)),
                        [kcT, q32[b]], [p_], inc=(not near))
                    if near:
                        h_ = hk32[ihk % 2]
                        ihk += 1
                        for hh in range(4):
                            off = (4 * g_ + hh) * NB + NB0 + t0 - 16 * (128 * c) - 2063
                            P.dma('sp', h_[:, hh, :], bass.AP(tensor=fh, offset=off, ap=[[16, 128], [1, 128]]), writes=[h_])
                        P.op('pe', lambda g, p_=p_, h_=h_: g.matmul(p_[:, :], lhsT=C['J'],
                                                                    rhs=h_[:, :, :].rearrange("p h i -> p (h i)"),
                                                                    start=False, stop=True), [C.t, h_], [p_])
                    P.op('act', lambda g, p_=p_, c=c: g.activation(out=PC[c][:, :], in_=p_[:, :], func=AF.Exp), [p_], [PC[c]])
                OCp = pO[0]
                for hh in range(4):
                    for c in range(cmax + 1):
                        P.op('pe', lambda g, hh=hh, c=c, g_=g_: g.matmul(
                            OCp[:, hh * 65:(hh + 1) * 65], lhsT=PC[c][:, hh * 128:(hh + 1) * 128], rhs=vc1[:, c, g_, :],
                            start=(c == 0), stop=(c == cmax)), [PC[c], vc1], [OCp], inc=(hh == 3 and c == cmax))
                for hh in range(4):
                    ip = pO[1 + hh // 2]
                    for c in range(cmax + 1):
                        P.op('pe', lambda g, hh=hh, c=c, ip=ip: g.matmul(
                            ip[:, (hh % 2) * NSLC:(hh % 2 + 1) * NSLC], lhsT=PC[c][:, hh * 128:(hh + 1) * 128], rhs=wt[:, c, :],
                            start=(c == 0), stop=(c == cmax)), [PC[c], wt], [ip], inc=(hh % 2 == 1 and c == cmax))
                oc3 = OCp[:, 0:260].rearrange("p (h e) -> p h e", e=65)
                P.op('dve', lambda g, oc3=oc3: g.tensor_scalar(out=rd[:, :], in0=oc3[:, :, 64], scalar1=1e-30, scalar2=None,
                                                              op0=ALU.max), [OCp], [rd])
                P.op('dve', lambda g: g.reciprocal(out=rd[:, :], in_=rd[:, :]), [rd], [rd])
                P.op('dve', lambda g, g_=g_, gs3=gs3: g.tensor_tensor(out=cf[:, :], in0=rd[:, :], in1=gs3[:, 4 * g_:4 * g_ + 4, 0],
                                                                    op=ALU.mult), [rd, gsb[b]], [cf])
                P.op('dve', lambda g, oc3=oc3, g_=g_: g.tensor_tensor(
                    out=OA[:, 4 * g_:4 * g_ + 4, :], in0=oc3[:, :, 0:64], in1=cf[:, :].unsqueeze(2).broadcast_to([128, 4, 64]),
                    op=ALU.mult), [OCp, cf], [OA])
                for hh in range(4):
                    ip = pO[1 + hh // 2]
                    src_ap = ip[:, (hh % 2) * NSLC:(hh % 2 + 1) * NSLC]
                    if hh == 0:
                        P.op('dve', lambda g, src_ap=src_ap: g.tensor_scalar(out=sc[:, :], in0=src_ap, scalar1=rd[:, 0:1],
                                                                            scalar2=None, op0=ALU.mult), [ip, rd], [sc])
                    else:
                        P.op('dve', lambda g, src_ap=src_ap, hh=hh: g.scalar_tensor_tensor(
                            out=sc[:, :], in0=src_ap, scalar=rd[:, hh:hh + 1], in1=sc[:, :], op0=ALU.mult, op1=ALU.add),
                            [ip, rd, sc], [sc])
                c0 = 2 * T
                if c0 + 2 < NSLC:
                    P.op('pool', lambda g, c0=c0: g.memset(sc[:, c0 + 2:NSLC], -1.0), [], [sc])
                P.op('pool', lambda g, c0=c0: g.memset(sc[0:64, c0 + 1:c0 + 2], -1.0), [], [sc])
                P.op('pool', lambda g, c0=c0: g.memset(sc[0:64, c0:c0 + 1], 5.0), [], [sc])
                if c0 - 1 >= 0:
                    P.op('pool', lambda g, c0=c0: g.memset(sc[0:64, c0 - 1:c0], 5.0), [], [sc])
                P.op('pool', lambda g, c0=c0: g.memset(sc[64:128, c0:c0 + 2], 5.0), [], [sc])
                P.op('pool', lambda g: g.memset(sc[:, 0:1], 5.0), [], [sc])
                M = mk[g_]
                if NSLC > ntop:
                    P.op('dve', lambda g: g.max(out=m8[:, :], in_=sc[:, :]), [sc], [m8])
                    P.op('dve', lambda g: g.match_replace(out=sc2[:, :], in_to_replace=m8[:, :], in_values=sc[:, :],
                                                          imm_value=-1e30), [sc, m8], [sc2])
                    P.op('dve', lambda g: g.max(out=m8[:, :], in_=sc2[:, :]), [sc2], [m8])
                    P.op('dve', lambda g, M=M: g.tensor_scalar(out=M[:, :], in0=sc[:, :], scalar1=m8[:, 7:8], scalar2=None,
                                                              op0=ALU.is_ge), [sc, m8], [M])
                else:
                    P.op('pool', lambda g, M=M: g.memset(M[:, :], 1.0), [], [M])
                MB = Mb[g_]
                P.op('dve', lambda g, M=M, MB=MB: g.tensor_scalar(out=MB[:, :], in0=M[:, :], scalar1=30000.0, scalar2=-30000.0,
                                                                 op0=ALU.mult, op1=ALU.add), [M], [MB])
                def slc_a(kc, i_):
                    p_ = pS[i_ % 4]
                    pt_ = PT[i_ % 4]
                    mx = Mx[i_ % 4]
                    dl = T - kc
                    near = dl < ND
                    P.op('dve', lambda g: g.tensor_copy(
                        out=mx[:, :].rearrange("p (j r) -> p j r", r=64),
                        in_=MB[:, 2 * kc:2 * kc + 2].unsqueeze(2).broadcast_to([128, 2, 64])), [MB], [mx])
                    P.op('pe', lambda g: g.matmul(
                        p_[:, :], lhsT=ksT[ks, kc * 128:(kc + 1) * 128], rhs=QB, start=True, stop=False),
                        [ksT, qb[b]], [p_], inc=False)
                    if near:
                        P.op('pe', lambda g: g.matmul(
                            p_[:, :], lhsT=Jb[:, :], rhs=hks[dl][g_][:, :, :].rearrange("p h i -> p (h i)"), start=False,
                            stop=False), [Jb, hks[dl][g_]], [p_], inc=False)
                    P.op('pe', lambda g: g.matmul(p_[:, :], lhsT=mx[:, :], rhs=I4[:, :], start=False, stop=True),
                         [mx, I4], [p_])
                    P.op('act', lambda g: g.activation(out=pt_[:, :], in_=p_[:, :], func=AF.Exp), [p_], [pt_])
                    return pt_

                def slc_b(kc, pt_):
                    for hh in range(4):
                        P.op('pe', lambda g, hh=hh: g.matmul(
                            pO[hh][:, 0:65], lhsT=pt_[:, hh * 128:(hh + 1) * 128], rhs=vs1[:, kc, g_, :], start=(kc == 0),
                            stop=(kc == T)), [pt_, vs1], [pO[hh]], inc=(kc == T))

                pend = []
                for kk in range(min(2, T + 1)):
                    pend.append(slc_a(kk, isc))
                    isc += 1
                for kc in range(T + 1):
                    if kc + 2 <= T:
                        pend.append(slc_a(kc + 2, isc))
                        isc += 1
                    slc_b(kc, pend.pop(0))
                for hh in range(4):
                    P.op('dve', lambda g, hh=hh: g.reciprocal(out=rd[:, hh:hh + 1], in_=pO[hh][:, 64:65]), [pO[hh]], [rd])
                P.op('dve', lambda g, g_=g_, gs3=gs3: g.tensor_tensor(out=cf[:, :], in0=rd[:, :], in1=gs3[:, 4 * g_:4 * g_ + 4, 1],
                                                                    op=ALU.mult), [rd, gsb[b]], [cf])
                for hh in range(4):
                    P.op('dve', lambda g, hh=hh, g_=g_: g.scalar_tensor_tensor(
                        out=OA[:, 4 * g_ + hh, :], in0=pO[hh][:, 0:64], scalar=cf[:, hh:hh + 1], in1=OA[:, 4 * g_ + hh, :],
                        op0=ALU.mult, op1=ALU.add), [pO[hh], cf, OA], [OA])
                k0 = max(0, T - 4)
                def win_a(kc, i_):
                    p_ = pS[i_ % 4]
                    pt_ = PT[i_ % 4]
                    dl = T - kc
                    P.op('pe', lambda g: g.matmul(
                        p_[:, :], lhsT=kw[kc % 6][ks, :], rhs=QB, start=True, stop=False), [kw[kc % 6], qb[b]], [p_], inc=False)
                    P.op('pe', lambda g: g.matmul(
                        p_[:, :], lhsT=Jb[:, :], rhs=hkw[dl][g_][:, :, :].rearrange("p h i -> p (h i)"), start=False, stop=True),
                        [Jb, hkw[dl][g_]], [p_])
                    P.op('act', lambda g: g.activation(out=pt_[:, :], in_=p_[:, :], func=AF.Exp), [p_], [pt_])
                    return pt_

                def win_b(kc, pt_):
                    for hh in range(4):
                        P.op('pe', lambda g, hh=hh: g.matmul(
                            pO[hh][:, 0:65], lhsT=pt_[:, hh * 128:(hh + 1) * 128], rhs=vw[kc % 6][:, g_, :], start=(kc == k0),
                            stop=(kc == T)), [pt_, vw[kc % 6]], [pO[hh]], inc=(kc == T))

                pend = win_a(k0, isc)
                isc += 1
                for kc in range(k0, T + 1):
                    nxt = None
                    if kc < T:
                        nxt = win_a(kc + 1, isc)
                        isc += 1
                    win_b(kc, pend)
                    pend = nxt
                for hh in range(4):
                    P.op('dve', lambda g, hh=hh: g.reciprocal(out=rd[:, hh:hh + 1], in_=pO[hh][:, 64:65]), [pO[hh]], [rd])
                P.op('dve', lambda g, g_=g_, gs3=gs3: g.tensor_tensor(out=cf[:, :], in0=rd[:, :], in1=gs3[:, 4 * g_:4 * g_ + 4, 2],
                                                                    op=ALU.mult), [rd, gsb[b]], [cf])
                for hh in range(4):
                    P.op('dve', lambda g, hh=hh, g_=g_: g.scalar_tensor_tensor(
                        out=OA[:, 4 * g_ + hh, :], in0=pO[hh][:, 0:64], scalar=cf[:, hh:hh + 1], in1=OA[:, 4 * g_ + hh, :],
                        op0=ALU.mult, op1=ALU.add), [pO[hh], cf, OA], [OA])
            O2 = OA[:, :, :].rearrange("p h d -> p (h d)")
            OAt = OA
            group_rms_ap(P, OAt, O2, sq, ms, 1, 512, 1e-6)
            P.op('pool', lambda g, O2=O2: g.tensor_tensor(out=O2, in0=O2, in1=ng[:, :], op=ALU.mult), [OAt, ng], [OAt])
            P.dma('act', mixed[sl, 512:1024], O2, reads=[OAt])
        P.barrier()


def group_rms_ap(P, Yt, Y, sq, ms, ng, gw, eps):
    n = ng * gw
    P.op('pool', lambda g: g.tensor_tensor(out=sq[:, :n], in0=Y, in1=Y, op=ALU.mult), [Yt], [sq])
    P.op('dve', lambda g: g.tensor_reduce(out=ms[:, :ng], in_=sq[:, :n].rearrange("p (g d) -> p g d", g=ng), axis=AX.X,
                                          op=ALU.add), [sq], [ms])
    P.op('act', lambda g: g.activation(out=ms[:, :ng], in_=ms[:, :ng], func=AF.Sqrt, bias=eps, scale=1.0 / gw), [ms], [ms])
    P.op('dve', lambda g: g.reciprocal(out=ms[:, :ng], in_=ms[:, :ng]), [ms], [ms])
    P.op('dve', lambda g: g.tensor_scalar(out=Y, in0=Y, scalar1=ms[:, 0:1], scalar2=None, op0=ALU.mult), [Yt, ms], [Yt])


def phase_wout(P, S, h_in, mixed, woutt, ln_g, ln_b, x_out, C):
    KC = D // 128
    with ExitStack() as es:
        gt = load_bcast_rows(P, es, ln_g, D, 'wlg')
        bt = load_bcast_rows(P, es, ln_b, D, 'wlb')
        wo = P.sb(es, [128, KC, D], BF16, 'wwo')
        for kc in range(KC):
            P.dma('sp', wo[:, kc, :], woutt[:, kc, :], writes=[wo])
        M = [P.sb(es, [128, D], F32, 'wm') for _ in range(2)]
        Hh = [P.sb(es, [128, D], F32, 'wh') for _ in range(2)]
        MT = [P.sb(es, [128, KC, 128], BF16, 'wmT') for _ in range(2)]
        y = [P.sb(es, [128, D], F32, 'wy') for _ in range(2)]
        st = [P.sb(es, [128, 4, 6], F32, 'wst') for _ in range(2)]
        mv = [P.sb(es, [128, 2], F32, 'wmv') for _ in range(2)]
        rs = [P.sb(es, [128, 1], F32, 'wrs') for _ in range(2)]
        pt = [P.ps(es, [128, 512], F32, 'wpt') for _ in range(2)]
        po = [P.ps(es, [128, 512], F32, 'wpo') for _ in range(4)]
        ic = [0]
        ip = 0
        for it in range(S // 128):
            b = it % 2
            sl = slice(it * 128, (it + 1) * 128)
            P.dma('sp', M[b][:, :], mixed[sl, :], writes=[M[b]])
            P.dma('sp', Hh[b][:, :], h_in[sl, :], writes=[Hh[b]])
            build_xT(P, M[b], MT[b], 0, C['ident'], C.t, pt, ic)
            for cg in range(4):
                p_ = po[ip % 4]
                ip += 1
                for kc in range(KC):
                    P.op('pe', lambda g, p_=p_, kc=kc, cg=cg, b=b: g.matmul(
                        p_[:, :], lhsT=MT[b][:, kc, :], rhs=wo[:, kc, cg * 512:(cg + 1) * 512], start=(kc == 0),
                        stop=(kc == KC - 1)), [MT[b], wo], [p_], inc=(kc == KC - 1))
                P.op('dve', lambda g, p_=p_, cg=cg, b=b: g.scalar_tensor_tensor(
                    out=Hh[b][:, cg * 512:(cg + 1) * 512], in0=Hh[b][:, cg * 512:(cg + 1) * 512], scalar=ALPHA, in1=p_[:, :],
                    op0=ALU.mult, op1=ALU.add), [p_, Hh[b]], [Hh[b]])
            layer_norm_rows(P, Hh[b], y[b], gt, bt, st[b], mv[b], rs[b], 1.0, 1e-5)
            P.dma('act', x_out[sl, :], y[b][:, :], reads=[y[b]])
        P.barrier()


WSHAPES = {'ln1_g': [D], 'ln1_b': [D], 'ffn1_w1': [D, DFF], 'ffn1_w3': [D, DFF], 'ffn1_w2': [DFF, D],
           'ln2_g': [D], 'ln2_b': [D], 'w_in': [D, DIN], 'w_out': [D, D],
           'hgrn_lb_logits': [512], 'hgrn_norm_g': [512],
           'nsa_pe_k': [32, 64], 'nsa_w1_k': [2048, 256], 'nsa_w2_k': [256, 64], 'nsa_pe_v': [32, 64],
           'nsa_w1_v': [2048, 256], 'nsa_w2_v': [256, 64], 'nsa_norm_g': [512],
           'ssm_conv_w': [4, 1024], 'ssm_conv_b': [1024], 'ssm_dt_bias': [8], 'ssm_a_log': [8], 'ssm_d': [8],
           'ssm_norm_g': [512], 'ln3_g': [D], 'ln3_b': [D], 'ffn2_w1': [D, DFF], 'ffn2_w3': [D, DFF], 'ffn2_w2': [DFF, D]}


def build(S, dff=DFF, depth=DEPTH, only='all'):
    nc = bass.Bass("TRN2", target_bir_lowering=False)
    dt = nc.dram_tensor
    x = dt("x", [S, D], F32, kind="ExternalInput")
    out = dt("out", [S, D], F32, kind="ExternalOutput")
    cin = {'pack': dt("c_pack", [128, NCONST], F32, kind="ExternalInput"),
           'cos': dt("c_cos", [S, 64], F32, kind="ExternalInput"),
           'sin': dt("c_sin", [S, 64], F32, kind="ExternalInput"),
           'oh': dt("c_oh", [33, NB], F32, kind="ExternalInput"),
           'ohw': dt("c_ohw", [33, NB], F32, kind="ExternalInput"),
           'wt': dt("c_wt", [S // 16, S // 64], F32, kind="ExternalInput")}
    W = {}
    for nm, shp in WSHAPES.items():
        shp = [dff if v == DFF else v for v in shp]
        W[nm] = dt(nm, [depth] + shp, F32, kind="ExternalInput")
    W['rel_bias'] = dt('rel_bias', [32, 8], F32, kind="ExternalInput")
    HC, KC = dff // 128, D // 128
    w1t = dt("w1t", [HC, 128, KC, 128], BF16)
    w3t = dt("w3t", [HC, 128, KC, 128], BF16)
    w2t = dt("w2t", [KC, 128, HC, 128], BF16)
    wint = dt("wint", [128, KC, DIN], BF16)
    woutt = dt("woutt", [128, KC, D], BF16)
    xa = dt("xa", [S, D], F32)
    xb = dt("xb", [S, D], F32)
    xc = dt("xc", [S, D], F32)
    mixed = dt("mixed", [S, D], F32)
    fh = dt("fh", [8, NB], F32)
    fhw = dt("fhw", [8, NB], F32)
    kcT_d = dt("kcT_d", [128, S // 16], F32)
    vc_d = dt("vc_d", [S // 16, 128], F32)
    proj = {}
    for (nm, c0, n, mode) in GROUPS:
        proj[nm] = dt("pj_" + nm, [n, S] if mode == 'FM' else [S, n], F32)
    P = Prog(nc)
    with ExitStack() as es:
        C = Consts(P, es, cin)
        if only == 'ffn':
            phase_convert_ffn(P, W['ffn1_w1'][0, :, :], W['ffn1_w3'][0, :, :], W['ffn1_w2'][0, :, :], w1t, w3t, w2t, dff)
            phase_ffn(P, S, x, out, w1t, w3t, w2t, W['ln1_g'][0, :], W['ln1_b'][0, :], C, dff)
        elif only.startswith('mix'):
            which = only.split(':')[1].split(',') if ':' in only else ['ret']
            with ExitStack() as es2:
                t = P.sb(es2, [128, D], F32, 'zz')
                P.op('pool', lambda g: g.memset(t[:, :], 0.0), [], [t])
                for i in range(S // 128):
                    P.dma('sp', mixed[i * 128:(i + 1) * 128, :], t[:, :], reads=[t])
                P.barrier()
            phase_convert_rows(P, W['w_in'][0, :, :], wint, DIN)
            phase_proj(P, S, x, wint, proj, C)
            if 'ret' in which:
                phase_retention(P, S, proj, mixed, C, cin)
            if 'ssd' in which:
                phase_ssd(P, S, proj, mixed, C, W, 0)
            if 'hgrn' in which:
                phase_hgrn(P, S, proj, mixed, C, W, LBL, depth)
            if 'nsa' in which:
                phase_nsa_tables(P, W['rel_bias'], cin, fh, fhw)
                phase_nsa_compress(P, S, proj, W, 0, kcT_d, vc_d)
                phase_nsa_attn(P, S, proj, W, 0, C, cin, kcT_d, vc_d, fh, fhw, mixed)
            with ExitStack() as es2:
                t = P.sb(es2, [128, D], F32, 'cp')
                for i in range(S // 128):
                    P.dma('sp', t[:, :], mixed[i * 128:(i + 1) * 128, :], writes=[t])
                    P.dma('sp', out[i * 128:(i + 1) * 128, :], t[:, :], reads=[t])
                P.barrier()
        else:
            phase_nsa_tables(P, W['rel_bias'], cin, fh, fhw)
            cur = x
            for l in range(depth):
                last = (l == depth - 1)
                phase_convert_ffn(P, W['ffn1_w1'][l, :, :], W['ffn1_w3'][l, :, :], W['ffn1_w2'][l, :, :], w1t, w3t, w2t, dff)
                phase_ffn(P, S, cur, xa, w1t, w3t, w2t, W['ln1_g'][l, :], W['ln1_b'][l, :], C, dff)
                phase_convert_rows(P, W['w_in'][l, :, :], wint, DIN)
                phase_proj(P, S, xa, wint, proj, C)
                phase_hgrn(P, S, proj, mixed, C, W, l, depth)
                phase_ssd(P, S, proj, mixed, C, W, l)
                phase_retention(P, S, proj, mixed, C, cin)
                phase_nsa_compress(P, S, proj, W, l, kcT_d, vc_d)
                phase_nsa_attn(P, S, proj, W, l, C, cin, kcT_d, vc_d, fh, fhw, mixed)
                phase_convert_rows(P, W['w_out'][l, :, :], woutt, D)
                phase_wout(P, S, xa, mixed, woutt, W['ln2_g'][l, :], W['ln2_b'][l, :], xb, C)
                phase_convert_ffn(P, W['ffn2_w1'][l, :, :], W['ffn2_w3'][l, :, :], W['ffn2_w2'][l, :, :], w1t, w3t, w2t, dff)
                phase_ffn(P, S, xb, out if last else xc, w1t, w3t, w2t, W['ln3_g'][l, :], W['ln3_b'][l, :], C, dff)
                cur = xc
    return nc


_NC_CACHE = {}


def kernel(**inputs):
    x = np.asarray(inputs['x'])
    B, S, _ = x.shape
    dff = int(np.asarray(inputs['ffn1_w1']).shape[-1])
    depth = int(np.asarray(inputs['ffn1_w1']).shape[0])
    key = (S, dff, depth)
    if key not in _NC_CACHE:
        _NC_CACHE[key] = build(S, dff=dff, depth=depth)
    nc = _NC_CACHE[key]
    consts = host_consts(S)
    base = {k: np.ascontiguousarray(np.asarray(inputs[k], dtype=np.float32)) for k in list(WSHAPES) + ['rel_bias']}
    base.update(consts)
    in_maps = []
    for b in range(B):
        m = dict(base)
        m['x'] = np.ascontiguousarray(x[b], dtype=np.float32)
        in_maps.append(m)
    res = run_bass_kernel_spmd(nc, in_maps, core_ids=list(range(B)))
    return np.stack([np.asarray(r['out']) for r in res.results], axis=0).astype(np.float32)
```
